# Optimizing a Trainium2 kernel written in Bass

```python
import math
import jax, jax.numpy as jnp
from jax import lax
import numpy as np

D_MODEL = 1024
BATCH = 32
SEQ = 256
DEPTH = 2
DEC_BATCH = 4
DEC_SEQ = 1024
PAST_LEN = 512

GRID_W = 64
HEAD_DIM = 64
MIX_WIDTH = D_MODEL
DIFF_WIDTH = MIX_WIDTH // 2
GQA_WIDTH = MIX_WIDTH - DIFF_WIDTH
DIFF_HEADS = DIFF_WIDTH // (2 * HEAD_DIM)
GQA_Q_HEADS = GQA_WIDTH // HEAD_DIM
GQA_KV_HEADS = 2
GQA_REP = GQA_Q_HEADS // GQA_KV_HEADS
DIFF_QK_COLS = 2 * DIFF_HEADS * HEAD_DIM
DIFF_V_COLS = DIFF_HEADS * 2 * HEAD_DIM
GQA_Q_COLS = GQA_Q_HEADS * HEAD_DIM
GQA_KV_COLS = GQA_KV_HEADS * HEAD_DIM
IN_COLS = 2 * DIFF_QK_COLS + DIFF_V_COLS + GQA_Q_COLS + 2 * GQA_KV_COLS
IN_SPLITS = (DIFF_QK_COLS,
             2 * DIFF_QK_COLS,
             2 * DIFF_QK_COLS + DIFF_V_COLS,
             2 * DIFF_QK_COLS + DIFF_V_COLS + GQA_Q_COLS,
             2 * DIFF_QK_COLS + DIFF_V_COLS + GQA_Q_COLS + GQA_KV_COLS)
FFN_HIDDEN = ((8 * D_MODEL + 3 * 256 - 1) // (3 * 256)) * 256
Q_BLOCK = 128
ROPE_THETA = 10000.0
EPS = 1e-6

kernel_name = "hybrid_diffattn_gqa_dit_step"


def rms_norm(x, gain):
    xf = x.astype(jnp.float32)
    y = xf * lax.rsqrt(jnp.mean(xf * xf, axis=-1, keepdims=True) + EPS)
    return (y * gain.astype(jnp.float32)).astype(x.dtype)


def axial_rope_angles(n_tokens):
    t = jnp.arange(n_tokens, dtype=jnp.int32)
    row = (t // GRID_W).astype(jnp.float32)
    col = (t % GRID_W).astype(jnp.float32)
    axis_dim = HEAD_DIM // 2
    freqs = ROPE_THETA ** (-jnp.arange(0, axis_dim, 2, dtype=jnp.float32) / axis_dim)
    ang = jnp.concatenate([row[:, None] * freqs, col[:, None] * freqs], axis=-1)
    return jnp.cos(ang), jnp.sin(ang)


def apply_rope(x, cos, sin):
    half = HEAD_DIM // 2
    xf = x.astype(jnp.float32)
    x1, x2 = xf[..., :half], xf[..., half:]
    shape = (1, cos.shape[0]) + (1,) * (x.ndim - 3) + (half,)
    c = cos.reshape(shape)
    s = sin.reshape(shape)
    return jnp.concatenate([x1 * c - x2 * s, x2 * c + x1 * s], axis=-1).astype(x.dtype)


def sweep_query_blocks(attend, q):
    b, t = q.shape[0], q.shape[1]
    nb = t // Q_BLOCK
    qb = jnp.moveaxis(q.reshape((b, nb, Q_BLOCK) + q.shape[2:]), 1, 0)
    out = lax.map(attend, qb)
    out = jnp.moveaxis(out, 0, 1)
    return out.reshape((b, t) + out.shape[3:])


def diff_attention(q, k, v, lam, lam_init, subln_w):
    scale = HEAD_DIM ** -0.5

    def attend(qb):
        s = jnp.einsum('bqhcd,bshcd->bhcqs', qb, k).astype(jnp.float32) * scale
        p = jax.nn.softmax(s, axis=-1)
        w = p[:, :, 0] - lam * p[:, :, 1]
        return jnp.einsum('bhqs,bshe->bqhe', w.astype(v.dtype), v)

    o = sweep_query_blocks(attend, q)
    return rms_norm(o, subln_w) * (1.0 - lam_init)


def gqa_attention(q, k, v):
    scale = HEAD_DIM ** -0.5

    def attend(qb):
        s = jnp.einsum('bqgrd,bsgd->bgrqs', qb, k).astype(jnp.float32) * scale
        p = jax.nn.softmax(s, axis=-1)
        return jnp.einsum('bgrqs,bsge->bqgre', p.astype(v.dtype), v)

    return sweep_query_blocks(attend, q)


def modulation(cond, w_mod, b_mod):
    m = jax.nn.silu(cond) @ w_mod + b_mod
    return jnp.split(m[:, None, :], 6, axis=-1)


def trunk_layer(x, cond, rope, ctx, layer_idx,
                w_mod, b_mod, norm_attn, w_in, q_norm_a, k_norm_a,
                lambda_q1, lambda_k1, lambda_q2, lambda_k2, subln,
                q_norm_b, k_norm_b, w_out, norm_ffn, w_gate_up, w_down):
    b, t, _ = x.shape
    sh_a, sc_a, g_a, sh_f, sc_f, g_f = modulation(cond, w_mod, b_mod)

    h = rms_norm(x, norm_attn) * (1.0 + sc_a) + sh_a
    proj = h @ w_in
    qa, ka, va, qb, kb, vb = jnp.split(proj, IN_SPLITS, axis=-1)
    qa = rms_norm(qa.reshape(b, t, DIFF_HEADS, 2, HEAD_DIM), q_norm_a)
    ka = rms_norm(ka.reshape(b, t, DIFF_HEADS, 2, HEAD_DIM), k_norm_a)
    va = va.reshape(b, t, DIFF_HEADS, 2 * HEAD_DIM)
    qb = rms_norm(qb.reshape(b, t, GQA_KV_HEADS, GQA_REP, HEAD_DIM), q_norm_b)
    kb = rms_norm(kb.reshape(b, t, GQA_KV_HEADS, HEAD_DIM), k_norm_b)
    vb = vb.reshape(b, t, GQA_KV_HEADS, HEAD_DIM)
    own_ctx = (ka, va, kb, vb)

    if ctx is None:
        keys_a, vals_a, keys_b, vals_b = ka, va, kb, vb
    else:
        cos, sin = rope
        qa, ka = apply_rope(qa, cos, sin), apply_rope(ka, cos, sin)
        qb, kb = apply_rope(qb, cos, sin), apply_rope(kb, cos, sin)
        c_ka, c_va, c_kb, c_vb = ctx
        keys_a = jnp.concatenate([ka, c_ka.astype(ka.dtype)], axis=1)
        vals_a = jnp.concatenate([va, c_va.astype(va.dtype)], axis=1)
        keys_b = jnp.concatenate([kb, c_kb.astype(kb.dtype)], axis=1)
        vals_b = jnp.concatenate([vb, c_vb.astype(vb.dtype)], axis=1)

    lam_init = 0.8 - 0.6 * math.exp(-0.3 * layer_idx)
    lam = (jnp.exp(jnp.sum(lambda_q1.astype(jnp.float32) * lambda_k1.astype(jnp.float32)))
           - jnp.exp(jnp.sum(lambda_q2.astype(jnp.float32) * lambda_k2.astype(jnp.float32)))
           + lam_init)
    out_a = diff_attention(qa, keys_a, vals_a, lam, lam_init, subln).reshape(b, t, DIFF_WIDTH)
    out_b = gqa_attention(qb, keys_b, vals_b).reshape(b, t, GQA_WIDTH)
    mix = jnp.concatenate([out_a, out_b], axis=-1) @ w_out
    x = x + g_a * mix

    h = rms_norm(x, norm_ffn) * (1.0 + sc_f) + sh_f
    gate, up = jnp.split(h @ w_gate_up, 2, axis=-1)
    x = x + g_f * ((jax.nn.silu(gate) * up) @ w_down)
    return x, own_ctx


def setup_inputs(seed: int = 0) -> dict:
    key = jax.random.key(seed)
    ks = jax.random.split(key, 32)

    def nrm(k, shape, s):
        return jax.random.normal(k, shape, jnp.float32) * s

    d = D_MODEL
    return {
        "x_prompt": nrm(ks[0], (BATCH, SEQ, d), 1.0),
        "x_sample": nrm(ks[1], (DEC_BATCH, DEC_SEQ, d), 1.0),
        "cache_diff_k": nrm(ks[2], (DEC_BATCH, DEPTH, PAST_LEN, DIFF_HEADS, 2, HEAD_DIM), 1.0),
        "cache_diff_v": nrm(ks[3], (DEC_BATCH, DEPTH, PAST_LEN, DIFF_HEADS, 2 * HEAD_DIM), 1.0),
        "cache_gqa_k": nrm(ks[4], (DEC_BATCH, DEPTH, PAST_LEN, GQA_KV_HEADS, HEAD_DIM), 1.0),
        "cache_gqa_v": nrm(ks[5], (DEC_BATCH, DEPTH, PAST_LEN, GQA_KV_HEADS, HEAD_DIM), 1.0),
        "c": nrm(ks[6], (DEC_BATCH, d), 1.0),
        "c_ctx": nrm(ks[7], (d,), 1.0),
        "w_mod": nrm(ks[8], (DEPTH, d, 6 * d), 0.5 * d ** -0.5),
        "b_mod": nrm(ks[9], (DEPTH, 6 * d), 0.02),
        "norm_attn": 1.0 + nrm(ks[10], (DEPTH, d), 0.02),
        "w_in": nrm(ks[11], (DEPTH, d, IN_COLS), d ** -0.5),
        "q_norm_a": 1.0 + nrm(ks[12], (DEPTH, HEAD_DIM), 0.02),
        "k_norm_a": 1.0 + nrm(ks[13], (DEPTH, HEAD_DIM), 0.02),
        "lambda_q1": nrm(ks[14], (DEPTH, HEAD_DIM), 0.1),
        "lambda_k1": nrm(ks[15], (DEPTH, HEAD_DIM), 0.1),
        "lambda_q2": nrm(ks[16], (DEPTH, HEAD_DIM), 0.1),
        "lambda_k2": nrm(ks[17], (DEPTH, HEAD_DIM), 0.1),
        "subln": 1.0 + nrm(ks[18], (DEPTH, 2 * HEAD_DIM), 0.02),
        "q_norm_b": 1.0 + nrm(ks[19], (DEPTH, HEAD_DIM), 0.02),
        "k_norm_b": 1.0 + nrm(ks[20], (DEPTH, HEAD_DIM), 0.02),
        "w_out": nrm(ks[21], (DEPTH, MIX_WIDTH, d), MIX_WIDTH ** -0.5),
        "norm_ffn": 1.0 + nrm(ks[22], (DEPTH, d), 0.02),
        "w_gate_up": nrm(ks[23], (DEPTH, d, 2 * FFN_HIDDEN), d ** -0.5),
        "w_down": nrm(ks[24], (DEPTH, FFN_HIDDEN, d), FFN_HIDDEN ** -0.5),
    }


def reference(x_prompt, x_sample, cache_diff_k, cache_diff_v, cache_gqa_k, cache_gqa_v,
              c, c_ctx, w_mod, b_mod, norm_attn, w_in, q_norm_a, k_norm_a,
              lambda_q1, lambda_k1, lambda_q2, lambda_k2, subln, q_norm_b, k_norm_b,
              w_out, norm_ffn, w_gate_up, w_down):
    rope = axial_rope_angles(x_sample.shape[1])
    cond_ctx = c_ctx[None, :]

    yp = x_prompt
    ys = x_sample
    dk, dv, gk, gv = [], [], [], []
    for l in range(DEPTH):
        lw = (w_mod[l], b_mod[l], norm_attn[l], w_in[l], q_norm_a[l], k_norm_a[l],
              lambda_q1[l], lambda_k1[l], lambda_q2[l], lambda_k2[l], subln[l],
              q_norm_b[l], k_norm_b[l], w_out[l], norm_ffn[l], w_gate_up[l], w_down[l])
        yp, (ka, va, kb, vb) = trunk_layer(yp, cond_ctx, None, None, l, *lw)
        dk.append(ka)
        dv.append(va)
        gk.append(kb)
        gv.append(vb)
        ctx = (cache_diff_k[:, l], cache_diff_v[:, l], cache_gqa_k[:, l], cache_gqa_v[:, l])
        ys, _ = trunk_layer(ys, c, rope, ctx, l, *lw)

    new_diff_k = jnp.stack(dk, axis=1)
    new_diff_v = jnp.stack(dv, axis=1)
    new_gqa_k = jnp.stack(gk, axis=1)
    new_gqa_v = jnp.stack(gv, axis=1)
    return (yp, ys, new_diff_k, new_diff_v, new_gqa_k, new_gqa_v)
```

```python
import math
from contextlib import ExitStack

import numpy as np
import concourse.bass as bass
import concourse.mybir as mybir
from concourse.bass_utils import run_bass_kernel_spmd

F32 = mybir.dt.float32
BF16 = mybir.dt.bfloat16
AF = mybir.ActivationFunctionType
ALU = mybir.AluOpType
AX = mybir.AxisListType

D = 1024
KC = 8
DEPTH = 2
HID = 2816
NJ = 22
INC = 2304
PAST = 512
EPS = 1e-6
NCORES = 8
T = 1024
MIX_DEFER = 5
SCHUNK = 3
OSUB = 12
PE_WARM = 0
NT = 8


class Buf:
    __slots__ = ("name", "last_w", "readers", "dsem", "dcount")

    def __init__(self, name):
        self.name = name
        self.last_w = None
        self.readers = []
        self.dsem = None
        self.dcount = 0


class Eng:
    def __init__(self, name, sem):
        self.name = name
        self.sem = sem
        self.count = 0
        self.known = {}
        self.ops = []


class Prog:
    def __init__(self, nc, stack):
        self.nc = nc
        self.stack = stack
        self.engs = {}
        for n in ("pe", "act", "dve", "pool", "sp"):
            self.engs[n] = Eng(n, self.new_sem("e_" + n))

    def new_sem(self, name):
        return self.stack.enter_context(self.nc.semaphore(name))

    def _deps(self, eng, reads, writes):
        need = {}

        def add(tok):
            if tok is None:
                return
            sem, val, en = tok
            if en == "pe" and eng.name == "pe":
                return
            k = id(sem)
            if k not in need or need[k][1] < val:
                need[k] = (sem, val)

        for b in reads:
            add(b.last_w)
        for b in writes:
            add(b.last_w)
            for r in b.readers:
                add(r)
        out = []
        for k, (sem, val) in need.items():
            if eng.known.get(k, 0) < val:
                eng.known[k] = val
                out.append((sem, val))
        return out

    @staticmethod
    def _mark(tok, reads, writes):
        for b in reads:
            b.readers.append(tok)
        for b in writes:
            b.last_w = tok
            b.readers = []

    def op(self, engname, fn, reads=(), writes=()):
        eng = self.engs[engname]
        waits = self._deps(eng, reads, writes)
        eng.count += 1
        sem = eng.sem

        def run(h, waits=waits, fn=fn, sem=sem):
            for s, v in waits:
                h.wait_ge(s, v)
            fn(h).then_inc(sem, 1)

        eng.ops.append(run)
        tok = (sem, eng.count, engname)
        self._mark(tok, reads, writes)
        return tok

    def dma(self, qname, fn, reads=(), writes=(), sembuf=None, nowait=False):
        eng = self.engs[qname]
        waits = [] if nowait else self._deps(eng, reads, writes)
        sb = sembuf if sembuf is not None else (writes[0] if writes else reads[0])
        if sb.dsem is None:
            sb.dsem = self.new_sem("d_" + sb.name)
        sb.dcount += 16
        sem, val = sb.dsem, sb.dcount

        def run(h, waits=waits, fn=fn, sem=sem):
            for s, v in waits:
                h.wait_ge(s, v)
            fn(h).then_inc(sem, 16)

        eng.ops.append(run)
        tok = (sem, val, "dma")
        self._mark(tok, reads, writes)
        return tok

    def final_wait(self, engname, bufs):
        eng = self.engs[engname]
        waits = self._deps(eng, [], bufs)

        def run(h, waits=waits):
            for s, v in waits:
                h.wait_ge(s, v)

        eng.ops.append(run)

    def emit(self):
        nc = self.nc
        E = self.engs
        with nc.Block() as block:
            @block.tensor
            def _(h):
                for f in E["pe"].ops:
                    f(h)

            @block.scalar
            def _(h):
                for f in E["act"].ops:
                    f(h)

            @block.vector
            def _(h):
                for f in E["dve"].ops:
                    f(h)

            @block.gpsimd
            def _(h):
                for f in E["pool"].ops:
                    f(h)

            @block.sync
            def _(h):
                for f in E["sp"].ops:
                    f(h)


class Ring:
    def __init__(self, items):
        self.items = items
        self.i = 0

    def next(self):
        it = self.items[self.i % len(self.items)]
        self.i += 1
        return it


def build_program(dbg=None):
    nc = bass.Bass("TRN2", target_bir_lowering=False)

    def din(name, shape, dt=F32):
        return nc.dram_tensor(name, list(shape), dt, kind="ExternalInput").ap()

    def dout(name, shape, dt=F32):
        return nc.dram_tensor(name, list(shape), dt, kind="ExternalOutput").ap()

    xin = {"P": din("xp", [T, D]), "S": din("xs", [T, D])}
    cdk = din("cdk", [DEPTH, PAST, 512])
    cdv = din("cdv", [DEPTH, PAST, 512])
    cgk = din("cgk", [DEPTH, PAST, 128])
    cgv = din("cgv", [DEPTH, PAST, 128])
    condT_d = din("condT", [128, KC, 2])
    w_mod = din("w_mod", [DEPTH, D, 6 * D])
    bmodT_d = din("bmodT", [128, DEPTH, 48])
    nrmT_d = din("nrmT", [128, 2, DEPTH, KC])
    w_in = din("w_in", [DEPTH, D, INC])
    gains_d = din("gains", [DEPTH, 4, 64])
    gainsT_d = din("gainsT", [128, DEPTH, 4])
    lamv_d = din("lamv", [DEPTH, 4, 64])
    subln_d = din("subln", [DEPTH, 128])
    w_out = din("w_out", [DEPTH, D, D])
    w_gu = din("w_gu", [DEPTH, D, 2 * HID])
    w_down = din("w_down", [DEPTH, HID, D])
    ropeC_d = din("ropeC", [128, NT, 64])
    ropeS_d = din("ropeS", [128, NT, 64])
    ident_d = din("identf", [128, 128])

    yout = {"P": dout("yp", [T, D]), "S": dout("ys", [T // 2, D])}
    ndk = dout("ndk", [4, DEPTH, 256, 512])
    ndv = dout("ndv", [4, DEPTH, 256, 512])
    ngk = dout("ngk", [4, DEPTH, 256, 128])
    ngv = dout("ngv", [4, DEPTH, 256, 128])
    dbg_out = {}
    if dbg:
        for name, (shape, dt_) in dbg.items():
            dbg_out[name] = dout("dbg_" + name, shape, dt_)

    st = ExitStack()
    with st:
        P = Prog(nc, st)
        nc._marks = []

        def mark(label):
            nc._marks.append((label, {n: len(e.ops) for n, e in P.engs.items()}))

        def sb(name, shape, dt=F32):
            return st.enter_context(nc.sbuf_tensor(name, list(shape), dt))

        xT = sb("xT", [128, KC, T], F32)
        hT = sb("hT", [128, KC, T], BF16)
        ARENA_N = 37504
        arena = sb("arena", [128, ARENA_N], BF16)
        QT = arena[:, 0:8192].rearrange("p (a b) -> p a b", a=8)
        KT = arena[:, 8192:17408].rearrange("p (a b) -> p a b", a=6)
        VA = arena[:, 17408:25160].rearrange("p (a b) -> p a b", a=12)
        PT = arena[:, 25160:37448].rearrange("p (s a b) -> p s a b", s=2, a=12)
        actT = arena[:, 0:22528].rearrange("p (a b) -> p a b", a=NJ)
        arena_f = arena[:, 0:16384].bitcast(F32)
        wm_slots = [arena_f[:, i * 4096:(i + 1) * 4096].rearrange("p (a b) -> p a b", a=8)
                    for i in range(2)]
        NW = 3
        wslots = [sb(f"wslot{i}", [128, 4096], BF16) for i in range(NW)]
        xq = sb("xq", [128, 2 * D], F32)
        xstage = [xq[:, i * D:(i + 1) * D] for i in range(2)]
        Qbd = xq[:, :].bitcast(BF16).rearrange("p (a b) -> p a b", a=8)
        identf = sb("identf_s", [128, 128], F32)
        identb = sb("identb", [128, 128], BF16)
        onesb = sb("onesb", [128, 128], BF16)
        ropeC = sb("ropeC_s", [128, NT, 64], F32)
        ropeS = sb("ropeS_s", [128, NT, 64], F32)
        gains = sb("gains_s", [128, DEPTH, 4, 64], F32)
        gainsT = sb("gainsT_s", [128, DEPTH, 4], F32)
        TCg = sb("TCg", [128, NT, 64], F32)
        TSg = sb("TSg", [128, NT, 64], F32)
        sublnb = sb("subln_s", [128, DEPTH, 128], F32)
        condT = sb("condT_s", [128, KC, 2], F32)
        scT = sb("scT", [128, KC, 2], F32)
        scTb = sb("scTb", [128, KC, 2], BF16)
        bmodT = sb("bmodT_s", [128, DEPTH, 48], F32)
        nrmT = sb("nrmT_s", [128, 2, DEPTH, KC], F32)
        MS = sb("MS", [128, DEPTH, 6, KC, 2], F32)
        lamt = sb("lamt", [128, DEPTH, 8], F32)
        epst = sb("epst", [128, 1], F32)
        sqmix = sb("sqmix", [128, 4096], BF16)
        sqb = sqmix[:, :].rearrange("p (a b) -> p a b", a=KC)
        NQ = 4
        qA = [sb(f"qA{i}", [128, 512], F32) for i in range(NQ)]
        qB = [sb(f"qB{i}", [128, 512], F32) for i in range(NQ)]
        qG = [sb(f"qG{i}", [128, 512], F32) for i in range(NQ)]
        qst = [sb(f"qst{i}", [128, 512], BF16) for i in range(NQ)]
        lamv = qB[1][:, 0:512].rearrange("p (l g d) -> p l g d", l=DEPTH, g=4)
        ntmp = qA[:3]
        rstdn = qB[:2]
        lnv = qG[0]
        sgb = qG[1:3]
        qss = [sb(f"qss{i}", [128, 24], F32) for i in range(NQ)]
        vF = [sb(f"vF{i}", [128, 512], F32) for i in range(2)]
        arec = [sb(f"arec{i}", [128, 16], F32) for i in range(6)]
        _atv = [qA[j][:, c * 128:(c + 1) * 128] for j in range(3) for c in range(4)]
        at_t = _atv[0:6]
        at_u = _atv[6:12]
        mixtok = [sqmix[:, i * D:(i + 1) * D] for i in range(4)]
        cst = [arena[:, 25160 + i * 768:25160 + (i + 1) * 768] for i in range(4)]

        psum = [st.enter_context(nc.psum_tensor(f"ps{i}", [128, 512], F32)) for i in range(8)]
        psb = [Buf(f"ps{i}") for i in range(8)]

        def bank_ring(ids):
            return Ring([(psum[i], psb[i]) for i in ids])

        xTb = [[Buf(f"xT{k}_{c}") for c in range(2)] for k in range(KC)]
        hTb = [[Buf(f"hT{k}_{c}") for c in range(2)] for k in range(KC)]
        QTb = [[Buf(f"QT{g}_{t}") for t in range(NT)] for g in range(2)]
        KTb = [[Buf(f"KT{g}_{t}") for t in range(12)] for g in range(2)]
        VAa = [Buf(f"VAa{t}") for t in range(12)]
        VAb = [Buf(f"VAb{t}") for t in range(12)]
        PTb = [Buf(f"PT{i}") for i in range(2)]
        actb = [[Buf(f"act{j}_{c}") for c in range(2)] for j in range(NJ)]
        wmb = [Buf(f"wm{i}") for i in range(2)]
        attn_bufs = [b for row in QTb for b in row] + [b for row in KTb for b in row] + VAa + VAb + PTb
        ffn_bufs = [b for row in actb for b in row] + wmb
        wsb = [Buf(f"ws{i}") for i in range(NW)]
        wsb2 = [Buf(f"wsu{i}") for i in range(NW)]
        xsb = [Buf(f"xs{i}") for i in range(2)]
        cbuf = Buf("consts")
        tabb = Buf("ropetab")
        msb = Buf("MS")
        sqbb = Buf("sqb")
        sqbb2 = Buf("sqb2")
        qAb = [Buf(f"qA{i}") for i in range(NQ)]
        qBb = [Buf(f"qB{i}") for i in range(NQ)]
        qGb = [Buf(f"qG{i}") for i in range(NQ)]
        ntmpb = qAb[:3]
        rstdnb = qBb[:2]
        lnvb = qGb[0]
        sgbb = qGb[1:3]
        qstb = [Buf(f"qst{i}") for i in range(NQ)]
        qssb = [Buf(f"qss{i}") for i in range(NQ)]
        vFb = [Buf(f"vF{i}") for i in range(2)]
        cstb = [(Buf(f"cstk{i}"), Buf(f"cstg{i}")) for i in range(4)]
        arecb = [Buf(f"arec{i}") for i in range(6)]
        attb = [Buf(f"att{i}") for i in range(6)]
        atub = [Buf(f"atu{i}") for i in range(6)]
        mixb = [Buf(f"mix{i}") for i in range(4)]
        outb = Buf("dram_out")
        out_bufs = [outb]

        def w_pieces(g, l):
            ps_ = []
            for nb in ("ka", "va", "kbvb", "qa", "qb"):
                ps_.append(("in", l, nb))
            for i in range(2):
                ps_.append(("out", l, i))
            for i in range(11):
                ps_.append(("gu", l, i))
            for m in range(KC):
                ps_.append(("down", l, m))
            return ps_

        IN_COLS = {"qa": (0, 512), "ka": (512, 512), "va": (1024, 512), "qb": (1536, 512),
                   "kbvb": (2048, 256)}
        group_order = ("P", "S")
        pieces = []
        for g in group_order:
            for l in range(DEPTH):
                wp = w_pieces(g, l)
                if g == group_order[0] and l == 0:
                    wp2 = [("mod", 0, pc) for pc in range(4)] + wp[:5] + \
                          [("mod", 0, pc) for pc in range(4, 12)] + wp[5:7]
                    for i in range(11):
                        wp2.append(wp[7 + i])
                        wp2.append(("mod", 1, i))
                    wp2.append(("mod", 1, 11))
                    wp = wp2 + wp[18:]
                pieces += wp
        wstate = {"issued": 0, "cur": 0}

        def issue_piece(i):
            kind, l, idx = pieces[i]
            slot = wslots[i % NW]
            b = wsb[i % NW]
            b2 = wsb2[i % NW]
            if kind == "mod":
                dst = slot[:, 0:4096].rearrange("p (a b) -> p a b", a=8)
                src = w_mod[l].rearrange("(k p) n -> p k n", p=128)[:, :, idx * 512:(idx + 1) * 512]
                P.dma("pool", lambda h, d=dst, s=src: h.dma_start(out=d, in_=s), writes=[b, b2])
            elif kind == "in":
                c0, n = IN_COLS[idx]
                dst = slot[:, 0:8 * n].rearrange("p (a b) -> p a b", a=8)
                src = w_in[l].rearrange("(k p) n -> p k n", p=128)[:, :, c0:c0 + n]
                P.dma("pool", lambda h, d=dst, s=src: h.dma_start(out=d, in_=s), writes=[b, b2])
            elif kind == "out":
                dst = slot[:, 0:4096].rearrange("p (a b) -> p a b", a=8)
                src = w_out[l].rearrange("(k p) n -> p k n", p=128)[:, :, idx * 512:(idx + 1) * 512]
                P.dma("pool", lambda h, d=dst, s=src: h.dma_start(out=d, in_=s), writes=[b, b2])
            elif kind == "gu":
                wv = w_gu[l].rearrange("(k p) n -> p k n", p=128)
                dg = slot[:, 0:2048].rearrange("p (a b) -> p a b", a=8)
                du = slot[:, 2048:4096].rearrange("p (a b) -> p a b", a=8)
                sg_ = wv[:, :, idx * 256:(idx + 1) * 256]
                su_ = wv[:, :, HID + idx * 256:HID + (idx + 1) * 256]
                P.dma("pool", lambda h, d=dg, s=sg_: h.dma_start(out=d, in_=s), writes=[b])
                P.dma("pool", lambda h, d=du, s=su_: h.dma_start(out=d, in_=s), writes=[b2])
            else:
                dst = slot[:, 0:NJ * 128].rearrange("p (a b) -> p a b", a=NJ)
                src = w_down[l].rearrange("(j p) n -> p j n", p=128)[:, :, idx * 128:(idx + 1) * 128]
                P.dma("pool", lambda h, d=dst, s=src: h.dma_start(out=d, in_=s), writes=[b, b2])

        def w_acquire(expect_kind, ahead=0):
            i = wstate["cur"] + ahead
            assert pieces[i][0] == expect_kind, (pieces[i], expect_kind)
            while wstate["issued"] <= i:
                issue_piece(wstate["issued"])
                wstate["issued"] += 1
            wstate["b2"] = wsb2[i % NW]
            return wslots[i % NW], wsb[i % NW]

        def w_release():
            i = wstate["cur"]
            wstate["cur"] += 1
            nxt = i + NW
            if nxt < len(pieces) and wstate["issued"] <= nxt:
                while wstate["issued"] <= nxt:
                    issue_piece(wstate["issued"])
                    wstate["issued"] += 1

        def ld(dst, src):
            P.dma("sp", lambda h, d=dst, s=src: h.dma_start(out=d, in_=s), writes=[cbuf], nowait=True)

        ld(identf[:], ident_d)
        ld(condT[:], condT_d)
        ld(bmodT[:], bmodT_d)
        ld(nrmT[:], nrmT_d)
        ld(ropeC[:], ropeC_d)
        ld(ropeS[:], ropeS_d)
        ld(gains[:], gains_d.partition_broadcast(128))
        ld(gainsT[:], gainsT_d)
        P.dma("sp", lambda h: h.dma_start(out=lamv, in_=lamv_d.partition_broadcast(128)), writes=[qBb[1]])
        ld(sublnb[:], subln_d.partition_broadcast(128))
        cdone = Buf("cdone")
        P.op("dve", lambda h: h.tensor_copy(out=identb[:], in_=identf[:]), reads=[cbuf], writes=[cdone])
        P.op("dve", lambda h: h.memset(onesb[:], 1.0), writes=[cdone])
        P.op("dve", lambda h: h.memset(epst[:], EPS), writes=[cdone])
        P.op("act", lambda h: h.activation(out=scTb[:], in_=condT[:], func=AF.Silu),
             reads=[cbuf], writes=[msb])
        for l in range(DEPTH):
            lam_init = 0.8 - 0.6 * math.exp(-0.3 * l)
            lt = lamt[:, l, :]
            P.op("dve", lambda h, l=l: h.tensor_tensor(out=qA[0][:, 0:64], in0=lamv[:, l, 0, :],
                                                       in1=lamv[:, l, 1, :], op=ALU.mult),
                 reads=[cbuf, qBb[1]], writes=[qAb[0]])
            P.op("dve", lambda h, l=l: h.tensor_tensor(out=qA[0][:, 64:128], in0=lamv[:, l, 2, :],
                                                       in1=lamv[:, l, 3, :], op=ALU.mult),
                 reads=[cbuf, qBb[1]], writes=[qAb[0]])
            P.op("dve", lambda h, lt=lt: h.tensor_reduce(
                out=lt[:, 0:2], in_=qA[0][:, 0:128].rearrange("p (a b) -> p a b", a=2),
                axis=AX.X, op=ALU.add), reads=[qAb[0]], writes=[cdone])
            P.op("act", lambda h, lt=lt: h.activation(out=lt[:, 2:4], in_=lt[:, 0:2], func=AF.Exp),
                 reads=[cdone], writes=[cdone])
            P.op("dve", lambda h, lt=lt: h.tensor_tensor(out=lt[:, 4:5], in0=lt[:, 3:4], in1=lt[:, 2:3],
                                                         op=ALU.subtract), reads=[cdone], writes=[cdone])
            P.op("dve", lambda h, lt=lt, li=lam_init: h.tensor_scalar(
                out=lt[:, 5:6], in0=lt[:, 4:5], scalar1=-li, scalar2=None, op0=ALU.add),
                reads=[cdone], writes=[cdone])
            P.op("dve", lambda h, l=l, li=lam_init: h.tensor_scalar(
                out=sublnb[:, l, :], in0=sublnb[:, l, :], scalar1=1.0 - li, scalar2=None, op0=ALU.mult),
                reads=[cbuf, cdone], writes=[cdone])


        def mod_piece(l, pc, ring):
            mps, mpb = ring.next()
            wslot, wb = w_acquire("mod")
            wv = wslot[:, 0:4096].rearrange("p (a b) -> p a b", a=8)
            for m4 in range(4):
                for k in range(KC):
                    P.op("pe", lambda h, mps=mps, wv=wv, m4=m4, k=k: h.matmul(
                        mps[:, m4 * 2:m4 * 2 + 2], lhsT=wv[:, k, m4 * 128:(m4 + 1) * 128],
                        rhs=scTb[:, k, :], start=(k == 0), stop=(k == KC - 1)),
                        reads=[wb, msb], writes=[mpb])
            w_release()
            msl = MS[:, l, :, :, :].rearrange("p s k n -> p (s k) n")[:, pc * 4:(pc + 1) * 4, :]
            P.op("dve", lambda h, mps=mps, msl=msl, l=l, pc=pc: h.tensor_tensor(
                out=msl, in0=mps[:, 0:8].rearrange("p (a n) -> p a n", n=2),
                in1=bmodT[:, l, pc * 4:(pc + 1) * 4].unsqueeze(2).broadcast_to([128, 4, 2]), op=ALU.add),
                reads=[mpb, cbuf, msb], writes=[msb])
            fold = {3: (1, 0), 9: (4, 1)}.get(pc)
            if fold is not None:
                s_idx, kind = fold
                P.op("dve", lambda h, l=l, s_idx=s_idx, kind=kind: h.scalar_tensor_tensor(
                    out=MS[:, l, s_idx, :, :], in0=MS[:, l, s_idx, :, :], scalar=1.0,
                    in1=nrmT[:, kind, l, :].unsqueeze(2).broadcast_to([128, KC, 2]),
                    op0=ALU.add, op1=ALU.mult), reads=[msb, cbuf], writes=[msb])

        tr_ring = bank_ring([6, 7])

        ldstage = [sqmix[:, :].bitcast(F32)[:, i * D:(i + 1) * D] for i in range(2)]
        ldb = [Buf(f"ldst{i}") for i in range(2)]
        xring = bank_ring([2, 3, 4, 5])

        def load_x(g, tiles=range(NT), alt=False):
            ring = xring
            for t in tiles:
                if alt:
                    xs_, xb_ = ldstage[t % 2], ldb[t % 2]
                else:
                    xs_, xb_ = xstage[t % 2], xsb[t % 2]
                src = xin[g][t * 128:(t + 1) * 128, :]
                P.dma("pool" if alt else "sp", lambda h, d=xs_, s=src: h.dma_start(out=d, in_=s), writes=[xb_])
                for half in range(2):
                    ps_, pb_ = ring.next()
                    for kk in range(4):
                        k = half * 4 + kk
                        P.op("pe", lambda h, ps_=ps_, xs_=xs_, k=k, kk=kk: h.transpose(
                            out=ps_[:, kk * 128:(kk + 1) * 128], in_=xs_[:, k * 128:(k + 1) * 128],
                            identity=identf[:]), reads=[xb_, cbuf], writes=[pb_])
                    dst = xT[:, half * 4:half * 4 + 4, t * 128:(t + 1) * 128]
                    srcp = ps_[:, :].rearrange("p (a b) -> p a b", a=4)
                    eng = "act" if half == 0 else "dve"
                    wr = [xTb[half * 4 + kk][t // 4] for kk in range(4)]
                    if eng == "act":
                        P.op("act", lambda h, d=dst, s=srcp: h.activation(out=d, in_=s, func=AF.Copy),
                             reads=[pb_], writes=wr)
                    else:
                        P.op("dve", lambda h, d=dst, s=srcp: h.tensor_copy(out=d, in_=s),
                             reads=[pb_], writes=wr)

        def store_y(g, tiles):
            ring = xring
            for t in tiles:
                xs_, xb_ = xstage[t % 2], xsb[t % 2]
                for half in range(2):
                    ps_, pb_ = ring.next()
                    for kk in range(4):
                        k = half * 4 + kk
                        P.op("pe", lambda h, ps_=ps_, k=k, kk=kk, t=t: h.transpose(
                            out=ps_[:, kk * 128:(kk + 1) * 128], in_=xT[:, k, t * 128:(t + 1) * 128],
                            identity=identf[:]), reads=[xTb[k][t // 4], cbuf], writes=[pb_])
                    dst = xs_[:, half * 512:(half + 1) * 512]
                    if half == 0:
                        P.op("act", lambda h, d=dst, s=ps_: h.activation(out=d, in_=s[:, :], func=AF.Copy),
                             reads=[pb_], writes=[xb_])
                    else:
                        P.op("dve", lambda h, d=dst, s=ps_: h.tensor_copy(out=d, in_=s[:, :]),
                             reads=[pb_], writes=[xb_])
                dsty = yout[g][t * 128:(t + 1) * 128, :]
                P.dma("sp", lambda h, d=dsty, s=xs_: h.dma_start(out=d, in_=s),
                      reads=[xb_], sembuf=xb_)

        def norm_mod(l, n, s_scale, s_shift, ncb):
            ring = bank_ring([0, 1])
            for cb in range(ncb):
                cs = slice(cb * 512, (cb + 1) * 512)
                P.op("act", lambda h, cs=cs: h.activation(out=sqb[:, 0:5, :], in_=xT[:, 0:5, cs], func=AF.Square),
                     reads=[xTb[k][cb] for k in range(5)], writes=[sqbb] + mixb)
                P.op("dve", lambda h, cs=cs: h.tensor_tensor(out=sqb[:, 5:8, :], in0=xT[:, 5:8, cs],
                                                             in1=xT[:, 5:8, cs], op=ALU.mult),
                     reads=[xTb[k][cb] for k in range(5, 8)], writes=[sqbb2] + mixb)
                ps_, pb_ = ring.next()
                for k in range(KC):
                    P.op("pe", lambda h, ps_=ps_, k=k: h.matmul(ps_[:, :], lhsT=onesb[:], rhs=sqb[:, k, :],
                                                                start=(k == 0), stop=(k == KC - 1)),
                         reads=[sqbb, sqbb2, cdone] + mixb, writes=[pb_])
                rs, rsb = rstdn[cb % 2], rstdnb[cb % 2]
                P.op("act", lambda h, ps_=ps_: h.activation(out=lnv[:], in_=ps_[:, :], func=AF.Ln,
                                                            bias=epst[:, 0:1], scale=1.0 / D),
                     reads=[pb_, cdone], writes=[lnvb])
                P.op("act", lambda h, rs=rs: h.activation(out=rs[:], in_=lnv[:], func=AF.Exp, scale=-0.5),
                     reads=[lnvb], writes=[rsb])
                for k in range(KC):
                    tm, tmb = ntmp[k % 3], ntmpb[k % 3]
                    P.op("dve", lambda h, tm=tm, k=k, cs=cs, rs=rs: h.tensor_tensor(
                        out=tm[:], in0=xT[:, k, cs], in1=rs[:], op=ALU.mult),
                        reads=[xTb[k][cb], rsb], writes=[tmb])
                    if k % 8 in (0, 2, 4, 6, 7):
                        P.op("act", lambda h, tm=tm, k=k, cs=cs: h.activation(
                            out=hT[:, k, cs], in_=tm[:], func=AF.Identity,
                            bias=MS[:, l, s_shift, k, n:n + 1], scale=MS[:, l, s_scale, k, n:n + 1]),
                            reads=[tmb, msb], writes=[hTb[k][cb]])
                    else:
                        P.op("pool", lambda h, tm=tm, k=k, cs=cs: h.tensor_scalar(
                            out=hT[:, k, cs], in0=tm[:], scalar1=MS[:, l, s_scale, k, n:n + 1],
                            scalar2=MS[:, l, s_shift, k, n:n + 1], op0=ALU.mult, op1=ALU.add),
                            reads=[tmb, msb], writes=[hTb[k][cb]])

        def dbg_dump(name, src_ap, reads):
            if dbg and name in dbg_out:
                db = Buf("dbg_" + name)
                out_bufs.append(db)
                P.dma("sp", lambda h: h.dma_start(out=dbg_out[name], in_=src_ap), reads=reads, writes=[db])

        cst_all = [b for pr in cstb for b in pr]

        def prefetch_cache(l):
            for ct in range(4):
                kt = 8 + ct
                cs_, (cbk_, cb_) = cst[ct], cstb[ct]
                rs = slice(ct * 128, (ct + 1) * 128)
                P.dma("pool", lambda h, cs_=cs_, rs=rs: h.dma_start(out=cs_[:, 0:512], in_=cdk[l, rs, :]),
                      writes=[cbk_] + PTb)
                P.dma("pool", lambda h, cs_=cs_, rs=rs: h.dma_start(out=cs_[:, 512:640], in_=cgk[l, rs, :]),
                      writes=[cb_] + PTb)
                vdst = VA[:, kt, 0:516].rearrange("p (h e) -> p h e", h=4)[:, :, 0:128]
                vbdst = VA[:, kt, 516:646].rearrange("p (h e) -> p h e", h=2)[:, :, 0:64]
                P.dma("pool", lambda h, vdst=vdst, rs=rs: h.dma_start(
                    out=vdst, in_=cdv[l, rs, :].rearrange("p (h e) -> p h e", h=4)),
                    writes=[VAa[kt]], sembuf=VAa[kt])
                P.dma("pool", lambda h, vbdst=vbdst, rs=rs: h.dma_start(
                    out=vbdst, in_=cgv[l, rs, :].rearrange("p (h e) -> p h e", h=2)),
                    writes=[VAb[kt]], sembuf=VAb[kt])

        def proj_step(g, l, tq_tiles):
            rope = (g == "S")
            proj_ring = bank_ring([2, 3, 4, 5])
            nkt = 12 if g == "S" else 8
            va4 = VA[:, 0:nkt, 0:516].rearrange("p t (h e) -> p t h e", h=4)[:, :, :, 128:129]
            vb2 = VA[:, 0:nkt, 516:646].rearrange("p t (h e) -> p t h e", h=2)[:, :, :, 64:65]
            P.op("pool", lambda h: h.memset(va4, 1.0), writes=VAa[:nkt] + ffn_bufs)
            P.op("pool", lambda h: h.memset(vb2, 1.0), writes=VAb[:nkt] + ffn_bufs)

            pending = []
            ucount = [0]

            def flush(keep):
                while len(pending) > keep:
                    pending.pop(0)()

            def build_rope_tables(gi):
                g1 = gains[:, l, gi, 0:32].unsqueeze(1).broadcast_to([128, NT, 32])
                g2 = gains[:, l, gi, 32:64].unsqueeze(1).broadcast_to([128, NT, 32])
                gf = gains[:, l, gi, :].unsqueeze(1).broadcast_to([128, NT, 64])
                P.op("dve", lambda h: h.tensor_tensor(out=TCg[:], in0=ropeC[:], in1=gf, op=ALU.mult),
                     reads=[cbuf], writes=[tabb])
                P.op("dve", lambda h: h.tensor_tensor(out=TSg[:, :, 0:32], in0=ropeS[:, :, 0:32], in1=g2,
                                                      op=ALU.mult), reads=[cbuf], writes=[tabb])
                P.op("dve", lambda h: h.tensor_tensor(out=TSg[:, :, 32:64], in0=ropeS[:, :, 32:64], in1=g1,
                                                      op=ALU.mult), reads=[cbuf], writes=[tabb])

            def qk_chain(ps_, pb_, nh, gi, t, slot, is_k, kout):
                w = nh * 64
                A, B, G, S_, SS = qA[slot], qB[slot], qG[slot], qst[slot], qss[slot]
                Ab, Bb, Gb, Sb, SSb = qAb[slot], qBb[slot], qGb[slot], qstb[slot], qssb[slot]
                v3 = lambda ap: ap[:, 0:w].rearrange("p (a b) -> p a b", a=nh)
                if rope:
                    P.op("act", lambda h: h.activation(out=B[:, 0:w], in_=ps_[:, 0:w], func=AF.Copy),
                         reads=[pb_], writes=[Bb])
                P.op("act", lambda h: h.activation(out=A[:, 0:w], in_=ps_[:, 0:w], func=AF.Square),
                     reads=[pb_], writes=[Ab])
                P.op("dve", lambda h: h.tensor_reduce(out=SS[:, 0:nh], in_=v3(A), axis=AX.X, op=ALU.add),
                     reads=[Ab], writes=[SSb])
                P.op("act", lambda h: h.activation(out=SS[:, 8:8 + nh], in_=SS[:, 0:nh], func=AF.Ln,
                                                   bias=epst[:, 0:1], scale=1.0 / 64),
                     reads=[SSb, cdone], writes=[SSb])
                P.op("act", lambda h: h.activation(out=SS[:, 16:16 + nh], in_=SS[:, 8:8 + nh], func=AF.Exp,
                                                   scale=-0.5), reads=[SSb], writes=[SSb])
                rstd_bc = SS[:, 16:16 + nh].unsqueeze(2).broadcast_to([128, nh, 64])
                if not rope:
                    if is_k:
                        def stage2():
                            P.op("dve", lambda h: h.tensor_tensor(out=v3(B), in0=v3(ps_), in1=rstd_bc, op=ALU.mult),
                                 reads=[pb_, SSb], writes=[Bb])
                            P.op("act", lambda h: h.activation(out=S_[:, 0:w], in_=B[:, 0:w], func=AF.Copy),
                                 reads=[Bb], writes=[Sb])
                            gain_bc = gains[:, l, gi, :].unsqueeze(1).broadcast_to([128, nh, 64])
                            P.op("pool", lambda h: h.tensor_tensor(out=v3(G), in0=v3(B), in1=gain_bc, op=ALU.mult),
                                 reads=[Bb, cbuf], writes=[Gb])
                            P.dma("sp", lambda h: h.dma_start(out=kout, in_=G[:, 0:w]), reads=[Gb], sembuf=Gb)
                    else:
                        def stage2():
                            P.op("dve", lambda h: h.tensor_tensor(out=v3(S_), in0=v3(ps_), in1=rstd_bc, op=ALU.mult),
                                 reads=[pb_, SSb], writes=[Sb])
                else:
                    A3, B3, G3 = v3(A), v3(B), v3(G)
                    cC = TCg[:, t, :].unsqueeze(1).broadcast_to([128, nh, 64])
                    s1 = TSg[:, t, 0:32].unsqueeze(1).broadcast_to([128, nh, 32])
                    s2 = TSg[:, t, 32:64].unsqueeze(1).broadcast_to([128, nh, 32])
                    P.op("pool" if slot % 2 else "dve",
                         lambda h: h.tensor_tensor(out=G3, in0=B3, in1=cC, op=ALU.mult),
                         reads=[Bb, tabb], writes=[Gb])
                    P.op("pool", lambda h: h.tensor_tensor(out=A3[:, :, 0:32], in0=B3[:, :, 32:64], in1=s1,
                                                           op=ALU.mult), reads=[Bb, tabb], writes=[Ab])
                    P.op("pool", lambda h: h.tensor_tensor(out=A3[:, :, 32:64], in0=B3[:, :, 0:32], in1=s2,
                                                           op=ALU.mult), reads=[Bb, tabb], writes=[Ab])

                    def stage2():
                        P.op("dve", lambda h: h.tensor_tensor(out=G3, in0=G3, in1=A3, op=ALU.add),
                             reads=[Gb, Ab], writes=[Gb])
                        P.op("dve", lambda h: h.tensor_tensor(out=v3(S_), in0=G3, in1=rstd_bc, op=ALU.mult),
                             reads=[Gb, SSb], writes=[Sb])
                return stage2

            q2 = []

            def defer(stage2_fn, tr_fn):
                q2.append((stage2_fn, tr_fn))
                while len(q2) > 1:
                    s2_, tr_ = q2.pop(0)
                    s2_()
                    pending.append(tr_)

            def drain_q2():
                while q2:
                    s2_, tr_ = q2.pop(0)
                    s2_()
                    pending.append(tr_)

            def transposes(src, srcb, nchunk, dst_fn, dst_bufs, gi=None):
                def run():
                    ps_, pb_ = tr_ring.next()
                    pv = ps_[:, :].bitcast(BF16)
                    for c in range(nchunk):
                        P.op("pe", lambda h, c=c: h.transpose(out=pv[:, c * 128:(c + 1) * 128],
                                                              in_=src[:, c * 128:(c + 1) * 128],
                                                              identity=identb[:]),
                             reads=srcb + [cdone], writes=[pb_])
                    dst = dst_fn()
                    srcv = pv[:, 0:nchunk * 128].rearrange("p (a b) -> p a b", a=nchunk)
                    ucount[0] += 1
                    if rope:
                        ucount[0] = 0
                    if gi is not None:
                        gsc = gainsT[:, l, gi:gi + 1]
                        if ucount[0] % 2 == 0:
                            P.op("act", lambda h: h.activation(out=dst, in_=srcv, func=AF.Copy, scale=gsc),
                                 reads=[pb_, cbuf], writes=dst_bufs + ffn_bufs)
                        else:
                            P.op("dve", lambda h: h.tensor_scalar(out=dst, in0=srcv, scalar1=gsc, scalar2=None,
                                                                  op0=ALU.mult),
                                 reads=[pb_, cbuf], writes=dst_bufs + ffn_bufs)
                    elif ucount[0] % 2 == 0:
                        P.op("act", lambda h: h.activation(out=dst, in_=srcv, func=AF.Copy),
                             reads=[pb_], writes=dst_bufs + ffn_bufs)
                    else:
                        P.op("dve", lambda h: h.tensor_copy(out=dst, in_=srcv),
                             reads=[pb_], writes=dst_bufs + ffn_bufs)
                return run

            slot_ctr = [0]
            vslot_ctr = [0]
            r8, rq = list(range(NT)), list(range(tq_tiles))
            grp_specs = [((("ka", r8), ("va", r8[0:4])), 1),
                         ((("va", r8[4:8]), ("kbvb", r8)), 2),
                         ((("qa", rq),), 1), ((("qb", rq),), 1)]
            for (grp, nrel) in grp_specs:
                views = {}
                for ai, (nb_, _tl) in enumerate(grp):
                    if rope and nb_ != "va":
                        build_rope_tables({"qa": 0, "ka": 1, "qb": 2, "kbvb": 3}[nb_])
                    assert pieces[wstate["cur"] + ai][2] == nb_
                    wslot_, wb_ = w_acquire("in", ahead=ai)
                    nc_ = IN_COLS[nb_][1]
                    views[nb_] = (wslot_[:, 0:8 * nc_].rearrange("p (a b) -> p a b", a=8), wb_, nc_)
                chain_u = [(nb_, t_) for (nb_, tl) in grp if nb_ != "va" for t_ in tl]
                free_u = [(nb_, t_) for (nb_, tl) in grp if nb_ == "va" for t_ in tl]
                step = max(1, len(chain_u) // max(1, len(free_u)))
                unit_list = []
                for ci, u_ in enumerate(chain_u):
                    unit_list.append(u_)
                    if free_u and (ci + 1) % step == 0:
                        unit_list.append(free_u.pop(0))
                unit_list += free_u
                for (nb, t) in unit_list:
                    wv, wb, ncols = views[nb]
                    ps_, pb_ = proj_ring.next()
                    for k in range(KC):
                        P.op("pe", lambda h, ps_=ps_, k=k, t=t, wv=wv, ncols=ncols: h.matmul(
                            ps_[:, 0:ncols], lhsT=hT[:, k, t * 128:(t + 1) * 128], rhs=wv[:, k, :],
                            start=(k == 0), stop=(k == KC - 1)),
                            reads=[hTb[k][t // 4], wb], writes=[pb_])
                    seq, r0 = t // 2, (t % 2) * 128
                    if nb in ("qa", "qb", "ka"):
                        slot = slot_ctr[0] % NQ
                        slot_ctr[0] += 1
                        gi = {"qa": 0, "ka": 1, "qb": 2}[nb]
                        kout = ndk[seq, l, r0:r0 + 128, :] if nb == "ka" else None
                        st2 = qk_chain(ps_, pb_, 8, gi, t, slot, nb == "ka" and g == "P", kout)
                        if nb == "ka":
                            dst_fn = (lambda t=t: KT[:, 0:4, t * 128:(t + 1) * 128])
                            dbs = [KTb[0][t]]
                        else:
                            c0 = 0 if nb == "qa" else 4
                            dst_fn = (lambda t=t, c0=c0: QT[:, c0:c0 + 4, t * 128:(t + 1) * 128])
                            dbs = [QTb[0 if nb == "qa" else 1][t]]
                        defer(st2, transposes(qst[slot], [qstb[slot]], 4, dst_fn, dbs, gi=(None if rope else gi)))
                    elif nb == "va":
                        vdst = VA[:, t, 0:516].rearrange("p (h e) -> p h e", h=4)[:, :, 0:128]
                        if g == "P":
                            vs = vslot_ctr[0] % 2
                            vslot_ctr[0] += 1
                            P.op("act", lambda h, vs=vs, ps_=ps_: h.activation(out=vF[vs][:], in_=ps_[:, :],
                                                                              func=AF.Copy),
                                 reads=[pb_], writes=[vFb[vs]])
                            vo = ndv[seq, l, r0:r0 + 128, :]
                            P.dma("sp", lambda h, vs=vs, vo=vo: h.dma_start(out=vo, in_=vF[vs][:]),
                                  reads=[vFb[vs]], sembuf=vFb[vs])
                            P.op("pool", lambda h, vs=vs, vdst=vdst: h.tensor_copy(
                                out=vdst, in_=vF[vs][:].rearrange("p (h e) -> p h e", h=4)),
                                reads=[vFb[vs]], writes=[VAa[t]] + ffn_bufs)
                        else:
                            P.op("act", lambda h, ps_=ps_, vdst=vdst: h.activation(
                                out=vdst, in_=ps_[:, :].rearrange("p (h e) -> p h e", h=4), func=AF.Copy),
                                reads=[pb_], writes=[VAa[t]] + ffn_bufs)
                    else:
                        slot = slot_ctr[0] % NQ
                        slot_ctr[0] += 1
                        kout = ngk[seq, l, r0:r0 + 128, :]
                        vbdst = VA[:, t, 516:646].rearrange("p (h e) -> p h e", h=2)[:, :, 0:64]
                        if g == "P":
                            vs = vslot_ctr[0] % 2
                            vslot_ctr[0] += 1
                            P.op("act", lambda h, vs=vs, ps_=ps_: h.activation(
                                out=vF[vs][:, 0:128], in_=ps_[:, 128:256], func=AF.Copy),
                                reads=[pb_], writes=[vFb[vs]])
                            vo = ngv[seq, l, r0:r0 + 128, :]
                            P.dma("sp", lambda h, vs=vs, vo=vo: h.dma_start(out=vo, in_=vF[vs][:, 0:128]),
                                  reads=[vFb[vs]], sembuf=vFb[vs])
                            P.op("pool", lambda h, vs=vs, vbdst=vbdst: h.tensor_copy(
                                out=vbdst, in_=vF[vs][:, 0:128].rearrange("p (h e) -> p h e", h=2)),
                                reads=[vFb[vs]], writes=[VAb[t]] + ffn_bufs)
                        else:
                            P.op("act", lambda h, ps_=ps_, vbdst=vbdst: h.activation(
                                out=vbdst, in_=ps_[:, 128:256].rearrange("p (h e) -> p h e", h=2),
                                func=AF.Copy), reads=[pb_], writes=[VAb[t]] + ffn_bufs)
                        st2k = qk_chain(ps_, pb_, 2, 3, t, slot, g == "P", kout)

                        def st2(st2k=st2k, slot=slot):
                            st2k()
                            S_ = qst[slot]
                            P.op("pool", lambda h, S_=S_: h.tensor_copy(
                                out=S_[:, 128:256].rearrange("p (a b) -> p a b", a=2),
                                in_=S_[:, 64:128].unsqueeze(1).broadcast_to([128, 2, 64])),
                                reads=[qstb[slot]], writes=[qstb[slot]])
                            P.op("pool", lambda h, S_=S_: h.tensor_copy(out=S_[:, 64:128], in_=S_[:, 0:64]),
                                 reads=[qstb[slot]], writes=[qstb[slot]])
                        dst_fn = (lambda t=t: KT[:, 4:6, t * 128:(t + 1) * 128])
                        defer(st2, transposes(qst[slot], [qstb[slot]], 2, dst_fn, [KTb[1][t]], gi=(None if rope else 3)))
                    flush(NQ - 2)
                drain_q2()
                flush(NQ - 2)
                for _ in range(nrel):
                    w_release()
            if g == "S":
                for ct in range(4):
                    kt = 8 + ct
                    cs_, (cbk_, cb_) = cst[ct], cstb[ct]
                    P.op("pool", lambda h, cs_=cs_: h.tensor_copy(
                        out=cs_[:, 640:768].rearrange("p (a b) -> p a b", a=2),
                        in_=cs_[:, 576:640].unsqueeze(1).broadcast_to([128, 2, 64])),
                        reads=[cb_], writes=[cb_])
                    P.op("pool", lambda h, cs_=cs_: h.tensor_copy(out=cs_[:, 576:640], in_=cs_[:, 512:576]),
                         reads=[cb_], writes=[cb_])
                    dst_fn = (lambda kt=kt: KT[:, 0:6, kt * 128:(kt + 1) * 128])
                    pending.append(transposes(cs_, [cbk_, cb_], 6, dst_fn, [KTb[0][kt], KTb[1][kt]]))
                    flush(1)
            flush(0)

        def attention(g, l, tq_tiles):
            if g == "P":
                s_ring = bank_ring([0, 1])
                o_ring = bank_ring([2, 3, 4, 5])
                mt_ring = bank_ring([6, 7])
            else:
                s_ring = bank_ring([0, 1, 2])
                o_ring = bank_ring([3, 4, 5, 6])
                mt_ring = bank_ring([7])
            if g == "P":
                blocks = [(s_ * 2, [s_ * 2, s_ * 2 + 1]) for s_ in range(4)]
            else:
                blocks = [(qb * 2, list(range(12))) for qb in range(tq_tiles // 2)]
            pctr = [0]
            att_alias = qAb[0:3] + attb + atub
            P.op("pool", lambda h: h.memset(qA[0][:, 0:8], 0.0), writes=att_alias)

            def mix_transposes(qt0, mslots):
                def run():
                    for i, ms in enumerate(mslots):
                        qt = qt0 + i
                        ps_, pb_ = mt_ring.next()
                        pv = ps_[:, :].bitcast(BF16)
                        for c in range(KC):
                            P.op("pe", lambda h, c=c, ms=ms, pv=pv: h.transpose(
                                out=pv[:, c * 128:(c + 1) * 128], in_=mixtok[ms][:, c * 128:(c + 1) * 128],
                                identity=identb[:]), reads=[mixb[ms], cdone], writes=[pb_])
                        dst = hT[:, :, qt * 128:(qt + 1) * 128]
                        srcv = pv[:, :].rearrange("p (a b) -> p a b", a=KC)
                        if g == "P" and i == 0:
                            P.op("act", lambda h, dst=dst, srcv=srcv: h.activation(out=dst, in_=srcv, func=AF.Copy),
                                 reads=[pb_], writes=[hTb[k][qt // 4] for k in range(KC)])
                        else:
                            P.op("dve", lambda h, dst=dst, srcv=srcv: h.tensor_copy(out=dst, in_=srcv),
                                 reads=[pb_], writes=[hTb[k][qt // 4] for k in range(KC)])
                return run

            units = []
            for bi, (qt0, ktiles) in enumerate(blocks):
                hc = [("d", h_, 0) for h_ in range(4)] + [("g", g_, rp_) for g_ in range(2) for rp_ in range(2)]
                for ui, (kind, a_, b_) in enumerate(hc):
                    units.append(dict(bi=bi, qt0=qt0, ktiles=ktiles, kind=kind, a=a_, b=b_, first=(ui == 0),
                                      last=(ui == len(hc) - 1), mslots=[(bi % 2) * 2 + i for i in range(2)]))
            for ui, u in enumerate(units):
                u["pslot"] = ui % 2
            obanks = {}
            qbd_b = xsb
            post_fin = []
            mod_tail = list(range(4, 12)) if (g == group_order[0] and l == 0) else []

            def build_qbd(u):
                qt0 = u["qt0"]
                qcs = slice(qt0 * 128, qt0 * 128 + 256)
                rd = [QTb[0][qt0], QTb[0][qt0 + 1], QTb[1][qt0], QTb[1][qt0 + 1]]
                P.op("dve", lambda h, qcs=qcs: h.tensor_copy(out=Qbd[0:64, :, 0:256], in_=QT[0:64, :, qcs]),
                     reads=rd, writes=qbd_b)
                P.op("dve", lambda h, qcs=qcs: h.tensor_copy(out=Qbd[64:128, :, 256:512], in_=QT[64:128, :, qcs]),
                     reads=rd, writes=qbd_b)

            def S_step(u, ki):
                kind, a_, b_ = u["kind"], u["a"], u["b"]
                kt = u["ktiles"][ki]
                if kind == "d":
                    qch, kch, kgrp = a_, a_, 0
                else:
                    qch, kch, kgrp = 4 + 2 * a_ + b_, 4 + a_, 1
                pslot = u["pslot"]
                ps_, pb_ = s_ring.next()
                P.op("pe", lambda h, ps_=ps_, kch=kch, kt=kt, qch=qch: h.matmul(
                    ps_[:, :], lhsT=KT[:, kch, kt * 128:(kt + 1) * 128], rhs=Qbd[:, qch, :],
                    start=True, stop=True),
                    reads=[KTb[kgrp][kt]] + qbd_b, writes=[pb_])
                P.op("act", lambda h, ps_=ps_, pslot=pslot, ki=ki: h.activation(
                    out=PT[:, pslot, ki, :], in_=ps_[:, :], func=AF.Exp, scale=0.125),
                    reads=[pb_], writes=[PTb[pslot]] + ffn_bufs + cst_all)
                if g == "S" and PE_WARM:
                    wps, wpb = mt_ring.next()
                    for _ in range(PE_WARM):
                        P.op("pe", lambda h, wps=wps, kch=kch, kt=kt, qch=qch: h.matmul(
                            wps[:, :], lhsT=KT[:, kch, kt * 128:(kt + 1) * 128], rhs=Qbd[:, qch, :],
                            start=True, stop=True),
                            reads=[KTb[kgrp][kt]] + qbd_b, writes=[wpb])

            def O_group(u, i, cc, k0=0, k1=None):
                kind, a_, b_ = u["kind"], u["a"], u["b"]
                nk = len(u["ktiles"])
                k1 = nk if k1 is None else k1
                pslot = u["pslot"]
                key = (u["bi"], kind, a_, i)
                if b_ == 0 and cc == 0 and k0 == 0:
                    obanks[key] = o_ring.next()
                ob, obb = obanks[key]
                for ki, kt in list(enumerate(u["ktiles"]))[k0:k1]:
                    if kind == "d":
                        oap = ob[:, cc * 129:(cc + 1) * 129]
                        rhs = VA[:, kt, a_ * 129:(a_ + 1) * 129]
                        vb_ = VAa
                    else:
                        r = 2 * b_ + cc
                        oap = ob[:, r * 65:(r + 1) * 65]
                        rhs = VA[:, kt, 516 + a_ * 65:516 + (a_ + 1) * 65]
                        vb_ = VAb
                    P.op("pe", lambda h, oap=oap, pslot=pslot, ki=ki, i=i, cc=cc, rhs=rhs, nk=nk: h.matmul(
                        oap, lhsT=PT[:, pslot, ki, cc * 256 + i * 128:cc * 256 + (i + 1) * 128], rhs=rhs,
                        start=(ki == 0), stop=(ki == nk - 1)),
                        reads=[PTb[pslot], vb_[kt]], writes=[obb])

            def post(u):
                kind, a_, b_ = u["kind"], u["a"], u["b"]
                mslots = u["mslots"]
                if kind == "d":
                    ctx = []
                    for i in range(2):
                        ob, obb = obanks[(u["bi"], "d", a_, i)]
                        ps2 = pctr[0] % 6
                        pctr[0] += 1
                        ctx.append(dict(ob=ob, obb=obb, ms=mslots[i], rc=arec[ps2], rcb=arecb[ps2],
                                        tt=at_t[ps2], ttb=attb[ps2], uu=at_u[ps2], uub=atub[ps2],
                                        o3=ob[:, 0:258].rearrange("p (c e) -> p c e", c=2)))
                    for c in ctx:
                        P.op("dve", lambda h, c=c: h.reciprocal(out=c["rc"][:, 0:2], in_=c["o3"][:, :, 128]),
                             reads=[c["obb"]], writes=[c["rcb"]])
                    for c in ctx:
                        P.op("dve", lambda h, c=c: h.tensor_tensor(out=c["rc"][:, 2:3], in0=c["rc"][:, 1:2],
                                                                   in1=lamt[:, l, 5:6], op=ALU.mult),
                             reads=[c["rcb"], cdone], writes=[c["rcb"]])
                    for c in ctx:
                        if g == "P":
                            P.op("act", lambda h, c=c: h.activation(
                                out=c["tt"][:], in_=c["ob"][:, 0:128], func=AF.Copy, scale=c["rc"][:, 0:1]),
                                reads=[c["obb"], c["rcb"]], writes=[c["ttb"]])
                        else:
                            P.op("dve", lambda h, c=c: h.tensor_scalar(
                                out=c["tt"][:], in0=c["ob"][:, 0:128], scalar1=c["rc"][:, 0:1], scalar2=None,
                                op0=ALU.mult), reads=[c["obb"], c["rcb"]], writes=[c["ttb"]])
                    for c in ctx:
                        P.op("dve", lambda h, c=c: h.scalar_tensor_tensor(
                            out=c["uu"][:], in0=c["ob"][:, 129:257], scalar=c["rc"][:, 2:3], in1=c["tt"][:],
                            op0=ALU.mult, op1=ALU.add), reads=[c["obb"], c["rcb"], c["ttb"]], writes=[c["uub"]])
                    for c in ctx:
                        if g == "P":
                            P.op("act", lambda h, c=c: h.activation(
                                out=c["tt"][:], in_=c["uu"][:], func=AF.Square, accum_out=c["rc"][:, 4:5]),
                                reads=[c["uub"]], writes=[c["ttb"], c["rcb"]])
                        else:
                            P.op("dve", lambda h, c=c: h.scalar_tensor_tensor(
                                out=c["tt"][:], in0=c["uu"][:], scalar=1.0, in1=c["uu"][:], op0=ALU.mult,
                                op1=ALU.mult, accum_out=c["rc"][:, 4:5]),
                                reads=[c["uub"]], writes=[c["ttb"], c["rcb"]])
                    for c in ctx:
                        P.op("act", lambda h, c=c: h.activation(out=c["rc"][:, 5:6], in_=c["rc"][:, 4:5],
                                                                func=AF.Ln, bias=epst[:, 0:1], scale=1.0 / 128),
                             reads=[c["rcb"], cdone], writes=[c["rcb"]])
                        P.op("act", lambda h, c=c: h.activation(out=c["rc"][:, 6:7], in_=c["rc"][:, 5:6],
                                                                func=AF.Exp, scale=-0.5),
                             reads=[c["rcb"]], writes=[c["rcb"]])
                    def fin(ctx=ctx, a_=a_):
                        for c in ctx:
                            P.op("dve", lambda h, c=c: h.scalar_tensor_tensor(
                                out=mixtok[c["ms"]][:, a_ * 128:(a_ + 1) * 128], in0=c["uu"][:],
                                scalar=c["rc"][:, 6:7], in1=sublnb[:, l, :], op0=ALU.mult, op1=ALU.mult),
                                reads=[c["uub"], c["rcb"], cdone], writes=[mixb[c["ms"]], sqbb, sqbb2])
                    post_fin.append(fin)
                if kind == "g" and b_ == 1:
                    for i in range(2):
                        ob, obb = obanks[(u["bi"], "g", a_, i)]
                        ms = mslots[i]
                        ps2 = pctr[0] % 6
                        pctr[0] += 1
                        rc, rcb = arec[ps2], arecb[ps2]
                        o3 = ob[:, 0:260].rearrange("p (r e) -> p r e", r=4)
                        P.op("dve", lambda h, rc=rc, o3=o3: h.reciprocal(out=rc[:, 8:12], in_=o3[:, :, 64]),
                             reads=[obb], writes=[rcb])
                        mdst = mixtok[ms][:, 512 + a_ * 256:512 + (a_ + 1) * 256].rearrange(
                            "p (r e) -> p r e", r=4)
                        P.op("dve", lambda h, rc=rc, o3=o3, mdst=mdst: h.tensor_tensor(
                            out=mdst, in0=o3[:, :, 0:64],
                            in1=rc[:, 8:12].unsqueeze(2).broadcast_to([128, 4, 64]), op=ALU.mult),
                            reads=[obb, rcb], writes=[mixb[ms], sqbb, sqbb2])

            pending = []
            nu = len(units)
            P.op("pool", lambda h: h.memset(Qbd[64:128, :, 0:256], 0.0), writes=qbd_b)
            P.op("pool", lambda h: h.memset(Qbd[0:64, :, 256:512], 0.0), writes=qbd_b)
            ogroups = [(i, cc) for i in range(2) for cc in range(2)]
            for idx in range(nu + 1):
                cur = units[idx] if idx < nu else None
                prv = units[idx - 1] if idx >= 1 else None
                nk = len((cur or prv)["ktiles"])
                if cur is not None and cur["first"] and idx == 0:
                    build_qbd(cur)
                chunks = [list(range(c0, min(c0 + SCHUNK, nk))) for c0 in range(0, nk, SCHUNK)]
                opieces = []
                if prv is not None:
                    nkp = len(prv["ktiles"])
                    for (i_, cc_) in ogroups:
                        for k0 in range(0, nkp, OSUB):
                            opieces.append((i_, cc_, k0, min(k0 + OSUB, nkp)))
                gi = 0
                for ci, ch in enumerate(chunks):
                    if cur is not None:
                        for ki in ch:
                            S_step(cur, ki)
                    if prv is not None:
                        ng = -(-len(opieces) * (ci + 1) // len(chunks))
                        while gi < ng:
                            O_group(prv, *opieces[gi])
                            gi += 1
                if cur is not None and cur["last"] and idx + 1 < nu:
                    build_qbd(units[idx + 1])
                if prv is not None:
                    while gi < len(opieces):
                        O_group(prv, *opieces[gi])
                        gi += 1
                    nfin = len(post_fin)
                    post(prv)
                    for _ in range(nfin):
                        post_fin.pop(0)()
                    if prv["last"]:
                        while post_fin:
                            post_fin.pop(0)()
                        if mod_tail:
                            for _ in range(2):
                                mod_piece(0, mod_tail.pop(0), s_ring)
                    for pnd in pending:
                        pnd[0] -= 1
                    if prv["last"]:
                        pending.append([MIX_DEFER, mix_transposes(prv["qt0"], prv["mslots"])])
                    while pending and pending[0][0] <= 0:
                        pending.pop(0)[1]()
            while pending:
                pending.pop(0)[1]()
            P.op("pool", lambda h: h.memset(qA[0][:, 0:8], 0.0), writes=att_alias)

        def out_proj(l, n, ntb):
            ring = bank_ring([0, 1, 2, 3])
            for pc in range(2):
                wslot, wb = w_acquire("out")
                wv = wslot[:, 0:4096].rearrange("p (a b) -> p a b", a=8)
                for tb in range(ntb):
                    cs = slice(tb * 512, (tb + 1) * 512)
                    for mm in range(4):
                        m = pc * 4 + mm
                        ps_, pb_ = ring.next()
                        for k in range(KC):
                            P.op("pe", lambda h, ps_=ps_, k=k, mm=mm, cs=cs, wv=wv: h.matmul(
                                ps_[:, :], lhsT=wv[:, k, mm * 128:(mm + 1) * 128], rhs=hT[:, k, cs],
                                start=(k == 0), stop=(k == KC - 1)),
                                reads=[wb, hTb[k][tb]], writes=[pb_])
                        P.op("dve", lambda h, ps_=ps_, m=m, cs=cs: h.scalar_tensor_tensor(
                            out=xT[:, m, cs], in0=ps_[:, :], scalar=MS[:, l, 2, m, n:n + 1], in1=xT[:, m, cs],
                            op0=ALU.mult, op1=ALU.add), reads=[pb_, msb, xTb[m][tb]], writes=[xTb[m][tb]])
                w_release()

        def ffn(l, n, ntb, with_mod=None):
            if with_mod is not None:
                mring = bank_ring([0, 1])
                ring = bank_ring([2, 3, 4, 5, 6, 7])
            else:
                ring = bank_ring([0, 1, 2, 3, 4, 5, 6, 7])
            sctr = [0]
            for pc in range(11):
                wslot, wb = w_acquire("gu")
                wb2 = wstate["b2"]
                wg = wslot[:, 0:2048].rearrange("p (a b) -> p a b", a=8)
                wu = wslot[:, 2048:4096].rearrange("p (a b) -> p a b", a=8)
                for tb in range(ntb):
                    cs = slice(tb * 512, (tb + 1) * 512)
                    for jj in range(2):
                        j = pc * 2 + jj
                        pg, pgb = ring.next()
                        pu, pub = ring.next()
                        for (pp, ppb, wv, wbx) in ((pg, pgb, wg, wb), (pu, pub, wu, wb2)):
                            for k in range(KC):
                                P.op("pe", lambda h, pp=pp, k=k, jj=jj, cs=cs, wv=wv: h.matmul(
                                    pp[:, :], lhsT=wv[:, k, jj * 128:(jj + 1) * 128], rhs=hT[:, k, cs],
                                    start=(k == 0), stop=(k == KC - 1)),
                                    reads=[wbx, hTb[k][tb]], writes=[ppb])
                        ss = sctr[0] % 2
                        sctr[0] += 1
                        P.op("act", lambda h, ss=ss, pg=pg: h.activation(out=sgb[ss][:], in_=pg[:, :], func=AF.Silu),
                             reads=[pgb], writes=[sgbb[ss]])
                        P.op("dve", lambda h, ss=ss, pu=pu, j=j, cs=cs: h.tensor_tensor(
                            out=actT[:, j, cs], in0=sgb[ss][:], in1=pu[:, :], op=ALU.mult),
                            reads=[sgbb[ss], pub], writes=[actb[j][tb]] + attn_bufs)
                w_release()
                if with_mod is not None:
                    mod_piece(with_mod, pc, mring)
                    if pc == 10:
                        mod_piece(with_mod, 11, mring)
            ring2 = bank_ring([0, 1, 2, 3])
            for m in range(KC):
                wslot, wb = w_acquire("down")
                wv = wslot[:, 0:NJ * 128].rearrange("p (a b) -> p a b", a=NJ)
                for tb in range(ntb):
                    cs = slice(tb * 512, (tb + 1) * 512)
                    ps_, pb_ = ring2.next()
                    for j in range(NJ):
                        P.op("pe", lambda h, ps_=ps_, j=j, cs=cs, wv=wv: h.matmul(
                            ps_[:, :], lhsT=wv[:, j, :], rhs=actT[:, j, cs], start=(j == 0), stop=(j == NJ - 1)),
                            reads=[wb, actb[j][tb]], writes=[pb_])
                    P.op("dve", lambda h, ps_=ps_, m=m, cs=cs: h.scalar_tensor_tensor(
                        out=xT[:, m, cs], in0=ps_[:, :], scalar=MS[:, l, 5, m, n:n + 1], in1=xT[:, m, cs],
                        op0=ALU.mult, op1=ALU.add), reads=[pb_, msb, xTb[m][tb]], writes=[xTb[m][tb]])
                w_release()

        for g in group_order:
            n = 0 if g == "P" else 1
            mark(f"{g}_loadx")
            if g == group_order[0]:
                load_x(g)
            if g == group_order[0]:
                ring0 = bank_ring([0, 1])
                for pc in range(4):
                    mod_piece(0, pc, ring0)
            for l in range(DEPTH):
                tq_tiles = 4 if (g == "S" and l == DEPTH - 1) else NT
                ntb = tq_tiles // 4
                all_h = [b for row in hTb for b in row]
                all_x = [b for row in xTb for b in row]
                mark(f"{g}{l}_norm1")
                if g == "S" and l == 0:
                    prefetch_cache(0)
                norm_mod(l, n, 1, 0, 2)
                dbg_dump(f"{g}{l}_h", hT[:], all_h)
                mark(f"{g}{l}_proj")
                proj_step(g, l, tq_tiles)
                dbg_dump(f"{g}{l}_QT", QT, attn_bufs)
                dbg_dump(f"{g}{l}_KT", KT, attn_bufs)
                dbg_dump(f"{g}{l}_VA", VA, attn_bufs)
                mark(f"{g}{l}_attn")
                attention(g, l, tq_tiles)
                dbg_dump(f"{g}{l}_mix", hT[:], all_h)
                mark(f"{g}{l}_outproj")
                if g == "S" and l + 1 < DEPTH:
                    prefetch_cache(l + 1)
                out_proj(l, n, ntb)
                dbg_dump(f"{g}{l}_x1", xT[:], all_x)
                mark(f"{g}{l}_norm2")
                norm_mod(l, n, 4, 3, ntb)
                dbg_dump(f"{g}{l}_h2", hT[:], all_h)
                mark(f"{g}{l}_ffn")
                ffn(l, n, ntb, with_mod=(1 if (g == group_order[0] and l == 0) else None))
                dbg_dump(f"{g}{l}_x2", xT[:], all_x)
            mark(f"{g}_store")
            gi_ = group_order.index(g)
            if gi_ + 1 < len(group_order):
                gn = group_order[gi_ + 1]
                lal = [sqbb, sqbb2] + mixb + ldb
                P.op("pool", lambda h: h.memset(sqmix[:, 0:8], 0.0), writes=lal)
                store_y(g, range(0, 4))
                load_x(gn, range(0, 4), alt=True)
                store_y(g, range(4, 8))
                load_x(gn, range(4, 8), alt=True)
                P.op("pool", lambda h: h.memset(sqmix[:, 0:8], 0.0), writes=lal)
            else:
                store_y(g, range(NT if g == "P" else NT // 2))

        mark("end")
        P.final_wait("sp", out_bufs + xsb + qGb + vFb)
        P.emit()
    return nc


def _rope_tables(perm):
    t = np.asarray(perm, dtype=np.int64)
    row = (t // 64).astype(np.float32)
    col = (t % 64).astype(np.float32)
    freqs = (10000.0 ** (-np.arange(0, 32, 2, dtype=np.float32) / 32)).astype(np.float32)
    ang = np.concatenate([row[:, None] * freqs, col[:, None] * freqs], axis=-1).astype(np.float32)
    c, s = np.cos(ang).astype(np.float32), np.sin(ang).astype(np.float32)
    C = np.concatenate([c, c], axis=-1)
    S = np.concatenate([-s, s], axis=-1)
    C = np.ascontiguousarray(C.reshape(NT, 128, 64).transpose(1, 0, 2))
    S = np.ascontiguousarray(S.reshape(NT, 128, 64).transpose(1, 0, 2))
    return C, S


def make_in_maps(inp, cores):
    f = lambda a: np.ascontiguousarray(np.asarray(a, dtype=np.float32))
    x_prompt, x_sample = f(inp["x_prompt"]), f(inp["x_sample"])
    shared = {
        "w_mod": f(inp["w_mod"]), "w_in": f(inp["w_in"]), "w_out": f(inp["w_out"]),
        "w_gu": f(inp["w_gate_up"]), "w_down": f(inp["w_down"]),
        "bmodT": f(np.asarray(inp["b_mod"]).reshape(DEPTH, 48, 128).transpose(2, 0, 1)),
        "nrmT": f(np.stack([np.asarray(inp["norm_attn"]), np.asarray(inp["norm_ffn"])], 0)
                  .reshape(2, DEPTH, KC, 128).transpose(3, 0, 1, 2)),
        "gains": f(np.stack([np.asarray(inp["q_norm_a"]), np.asarray(inp["k_norm_a"]),
                             np.asarray(inp["q_norm_b"]), np.asarray(inp["k_norm_b"])], 1)),
        "gainsT": f(np.tile(np.stack([np.asarray(inp["q_norm_a"]), np.asarray(inp["k_norm_a"]),
                                      np.asarray(inp["q_norm_b"]), np.asarray(inp["k_norm_b"])], 1), (1, 1, 2))
                    .transpose(2, 0, 1)),
        "lamv": f(np.stack([np.asarray(inp["lambda_q1"]), np.asarray(inp["lambda_k1"]),
                            np.asarray(inp["lambda_q2"]), np.asarray(inp["lambda_k2"])], 1)),
        "subln": f(inp["subln"]),
        "identf": np.eye(128, dtype=np.float32),
    }
    maps = []
    for c in cores:
        b, hh = c // 2, c % 2
        perm = np.concatenate([np.arange(hh * 512, hh * 512 + 512), np.arange((1 - hh) * 512, (1 - hh) * 512 + 512)])
        C, S = _rope_tables(perm)
        cond = np.stack([np.asarray(inp["c_ctx"]), np.asarray(inp["c"])[b]], 0)
        m = dict(shared)
        m.update({
            "xp": f(x_prompt[4 * c:4 * c + 4].reshape(T, D)),
            "xs": f(x_sample[b][perm]),
            "cdk": f(np.asarray(inp["cache_diff_k"])[b].reshape(DEPTH, PAST, 512)),
            "cdv": f(np.asarray(inp["cache_diff_v"])[b].reshape(DEPTH, PAST, 512)),
            "cgk": f(np.asarray(inp["cache_gqa_k"])[b].reshape(DEPTH, PAST, 128)),
            "cgv": f(np.asarray(inp["cache_gqa_v"])[b].reshape(DEPTH, PAST, 128)),
            "condT": f(cond.reshape(2, KC, 128).transpose(2, 1, 0)),
            "ropeC": C, "ropeS": S,
        })
        maps.append(m)
    return maps


_NC_CACHE = {}


def kernel(**inputs):
    if "nc" not in _NC_CACHE:
        _NC_CACHE["nc"] = build_program()
    nc = _NC_CACHE["nc"]
    cores = list(range(NCORES))
    in_maps = make_in_maps(inputs, cores)
    res = run_bass_kernel_spmd(nc, in_maps, core_ids=cores)
    R = res.results
    yp = np.concatenate([np.asarray(R[c]["yp"]).reshape(4, 256, D) for c in cores], 0)
    ys = np.zeros((4, 1024, D), np.float32)
    for c in cores:
        b, hh = c // 2, c % 2
        ys[b, hh * 512:(hh + 1) * 512] = np.asarray(R[c]["ys"])
    ndk = np.concatenate([np.asarray(R[c]["ndk"]) for c in cores], 0).reshape(32, DEPTH, 256, 4, 2, 64)
    ndv = np.concatenate([np.asarray(R[c]["ndv"]) for c in cores], 0).reshape(32, DEPTH, 256, 4, 128)
    ngk = np.concatenate([np.asarray(R[c]["ngk"]) for c in cores], 0).reshape(32, DEPTH, 256, 2, 64)
    ngv = np.concatenate([np.asarray(R[c]["ngv"]) for c in cores], 0).reshape(32, DEPTH, 256, 2, 64)
    return (yp.astype(np.float32), ys, ndk.astype(np.float32), ndv.astype(np.float32),
            ngk.astype(np.float32), ngv.astype(np.float32))
```

```python
import math
from contextlib import ExitStack

import numpy as np
import concourse.bass as bass
import concourse.mybir as mybir
from concourse.bass_utils import run_bass_kernel_spmd

F32 = mybir.dt.float32
BF16 = mybir.dt.bfloat16
AF = mybir.ActivationFunctionType
ALU = mybir.AluOpType
AX = mybir.AxisListType

D = 1024
KC = 8
DEPTH = 2
HID = 2816
NJ = 22
INC = 2304
PAST = 512
EPS = 1e-6
NCORES = 8
T = 1024
MIX_DEFER = 5
SCHUNK = 3
OSUB = 12
PE_WARM = 0
NT = 8


class Buf:
    __slots__ = ("name", "last_w", "readers", "dsem", "dcount")

    def __init__(self, name):
        self.name = name
        self.last_w = None
        self.readers = []
        self.dsem = None
        self.dcount = 0


class Eng:
    def __init__(self, name, sem):
        self.name = name
        self.sem = sem
        self.count = 0
        self.known = {}
        self.ops = []


class Prog:
    def __init__(self, nc, stack):
        self.nc = nc
        self.stack = stack
        self.engs = {}
        for n in ("pe", "act", "dve", "pool", "sp"):
            self.engs[n] = Eng(n, self.new_sem("e_" + n))

    def new_sem(self, name):
        return self.stack.enter_context(self.nc.semaphore(name))

    def _deps(self, eng, reads, writes):
        need = {}

        def add(tok):
            if tok is None:
                return
            sem, val, en = tok
            if en == "pe" and eng.name == "pe":
                return
            k = id(sem)
            if k not in need or need[k][1] < val:
                need[k] = (sem, val)

        for b in reads:
            add(b.last_w)
        for b in writes:
            add(b.last_w)
            for r in b.readers:
                add(r)
        out = []
        for k, (sem, val) in need.items():
            if eng.known.get(k, 0) < val:
                eng.known[k] = val
                out.append((sem, val))
        return out

    @staticmethod
    def _mark(tok, reads, writes):
        for b in reads:
            b.readers.append(tok)
        for b in writes:
            b.last_w = tok
            b.readers = []

    def op(self, engname, fn, reads=(), writes=()):
        eng = self.engs[engname]
        waits = self._deps(eng, reads, writes)
        eng.count += 1
        sem = eng.sem

        def run(h, waits=waits, fn=fn, sem=sem):
            for s, v in waits:
                h.wait_ge(s, v)
            fn(h).then_inc(sem, 1)

        eng.ops.append(run)
        tok = (sem, eng.count, engname)
        self._mark(tok, reads, writes)
        return tok

    def dma(self, qname, fn, reads=(), writes=(), sembuf=None, nowait=False):
        eng = self.engs[qname]
        waits = [] if nowait else self._deps(eng, reads, writes)
        sb = sembuf if sembuf is not None else (writes[0] if writes else reads[0])
        if sb.dsem is None:
            sb.dsem = self.new_sem("d_" + sb.name)
        sb.dcount += 16
        sem, val = sb.dsem, sb.dcount

        def run(h, waits=waits, fn=fn, sem=sem):
            for s, v in waits:
                h.wait_ge(s, v)
            fn(h).then_inc(sem, 16)

        eng.ops.append(run)
        tok = (sem, val, "dma")
        self._mark(tok, reads, writes)
        return tok

    def final_wait(self, engname, bufs):
        eng = self.engs[engname]
        waits = self._deps(eng, [], bufs)

        def run(h, waits=waits):
            for s, v in waits:
                h.wait_ge(s, v)

        eng.ops.append(run)

    def emit(self):
        nc = self.nc
        E = self.engs
        with nc.Block() as block:
            @block.tensor
            def _(h):
                for f in E["pe"].ops:
                    f(h)

            @block.scalar
            def _(h):
                for f in E["act"].ops:
                    f(h)

            @block.vector
            def _(h):
                for f in E["dve"].ops:
                    f(h)

            @block.gpsimd
            def _(h):
                for f in E["pool"].ops:
                    f(h)

            @block.sync
            def _(h):
                for f in E["sp"].ops:
                    f(h)


class Ring:
    def __init__(self, items):
        self.items = items
        self.i = 0

    def next(self):
        it = self.items[self.i % len(self.items)]
        self.i += 1
        return it


def build_program(dbg=None):
    nc = bass.Bass("TRN2", target_bir_lowering=False)

    def din(name, shape, dt=F32):
        return nc.dram_tensor(name, list(shape), dt, kind="ExternalInput").ap()

    def dout(name, shape, dt=F32):
        return nc.dram_tensor(name, list(shape), dt, kind="ExternalOutput").ap()

    xin = {"P": din("xp", [T, D]), "S": din("xs", [T, D])}
    cdk = din("cdk", [DEPTH, PAST, 512])
    cdv = din("cdv", [DEPTH, PAST, 512])
    cgk = din("cgk", [DEPTH, PAST, 128])
    cgv = din("cgv", [DEPTH, PAST, 128])
    condT_d = din("condT", [128, KC, 2])
    w_mod = din("w_mod", [DEPTH, D, 6 * D])
    bmodT_d = din("bmodT", [128, DEPTH, 48])
    nrmT_d = din("nrmT", [128, 2, DEPTH, KC])
    w_in = din("w_in", [DEPTH, D, INC])
    gains_d = din("gains", [DEPTH, 4, 64])
    gainsT_d = din("gainsT", [128, DEPTH, 4])
    lamv_d = din("lamv", [DEPTH, 4, 64])
    subln_d = din("subln", [DEPTH, 128])
    w_out = din("w_out", [DEPTH, D, D])
    w_gu = din("w_gu", [DEPTH, D, 2 * HID])
    w_down = din("w_down", [DEPTH, HID, D])
    ropeC_d = din("ropeC", [128, NT, 64])
    ropeS_d = din("ropeS", [128, NT, 64])
    ident_d = din("identf", [128, 128])

    yout = {"P": dout("yp", [T, D]), "S": dout("ys", [T // 2, D])}
    ndk = dout("ndk", [4, DEPTH, 256, 512])
    ndv = dout("ndv", [4, DEPTH, 256, 512])
    ngk = dout("ngk", [4, DEPTH, 256, 128])
    ngv = dout("ngv", [4, DEPTH, 256, 128])
    dbg_out = {}
    if dbg:
        for name, (shape, dt_) in dbg.items():
            dbg_out[name] = dout("dbg_" + name, shape, dt_)

    st = ExitStack()
    with st:
        P = Prog(nc, st)
        nc._marks = []

        def mark(label):
            nc._marks.append((label, {n: len(e.ops) for n, e in P.engs.items()}))

        def sb(name, shape, dt=F32):
            return st.enter_context(nc.sbuf_tensor(name, list(shape), dt))

        xT = sb("xT", [128, KC, T], F32)
        hT = sb("hT", [128, KC, T], BF16)
        ARENA_N = 37504
        arena = sb("arena", [128, ARENA_N], BF16)
        QT = arena[:, 0:8192].rearrange("p (a b) -> p a b", a=8)
        KT = arena[:, 8192:17408].rearrange("p (a b) -> p a b", a=6)
        VA = arena[:, 17408:25160].rearrange("p (a b) -> p a b", a=12)
        PT = arena[:, 25160:37448].rearrange("p (s a b) -> p s a b", s=2, a=12)
        actT = arena[:, 0:22528].rearrange("p (a b) -> p a b", a=NJ)
        arena_f = arena[:, 0:16384].bitcast(F32)
        wm_slots = [arena_f[:, i * 4096:(i + 1) * 4096].rearrange("p (a b) -> p a b", a=8)
                    for i in range(2)]
        NW = 3
        wslots = [sb(f"wslot{i}", [128, 4096], BF16) for i in range(NW)]
        xq = sb("xq", [128, 2 * D], F32)
        xstage = [xq[:, i * D:(i + 1) * D] for i in range(2)]
        Qbd = xq[:, :].bitcast(BF16).rearrange("p (a b) -> p a b", a=8)
        identf = sb("identf_s", [128, 128], F32)
        identb = sb("identb", [128, 128], BF16)
        onesb = sb("onesb", [128, 128], BF16)
        ropeC = sb("ropeC_s", [128, NT, 64], F32)
        ropeS = sb("ropeS_s", [128, NT, 64], F32)
        gains = sb("gains_s", [128, DEPTH, 4, 64], F32)
        gainsT = sb("gainsT_s", [128, DEPTH, 4], F32)
        TCg = sb("TCg", [128, NT, 64], F32)
        TSg = sb("TSg", [128, NT, 64], F32)
        sublnb = sb("subln_s", [128, DEPTH, 128], F32)
        condT = sb("condT_s", [128, KC, 2], F32)
        scT = sb("scT", [128, KC, 2], F32)
        scTb = sb("scTb", [128, KC, 2], BF16)
        bmodT = sb("bmodT_s", [128, DEPTH, 48], F32)
        nrmT = sb("nrmT_s", [128, 2, DEPTH, KC], F32)
        MS = sb("MS", [128, DEPTH, 6, KC, 2], F32)
        lamt = sb("lamt", [128, DEPTH, 8], F32)
        epst = sb("epst", [128, 1], F32)
        sqmix = sb("sqmix", [128, 4096], BF16)
        sqb = sqmix[:, :].rearrange("p (a b) -> p a b", a=KC)
        NQ = 4
        qA = [sb(f"qA{i}", [128, 512], F32) for i in range(NQ)]
        qB = [sb(f"qB{i}", [128, 512], F32) for i in range(NQ)]
        qG = [sb(f"qG{i}", [128, 512], F32) for i in range(NQ)]
        qst = [sb(f"qst{i}", [128, 512], BF16) for i in range(NQ)]
        lamv = qB[1][:, 0:512].rearrange("p (l g d) -> p l g d", l=DEPTH, g=4)
        ntmp = qA[:3]
        rstdn = qB[:2]
        lnv = qG[0]
        sgb = qG[1:3]
        qss = [sb(f"qss{i}", [128, 24], F32) for i in range(NQ)]
        vF = [sb(f"vF{i}", [128, 512], F32) for i in range(2)]
        arec = [sb(f"arec{i}", [128, 16], F32) for i in range(6)]
        _atv = [qA[j][:, c * 128:(c + 1) * 128] for j in range(3) for c in range(4)]
        at_t = _atv[0:6]
        at_u = _atv[6:12]
        mixtok = [sqmix[:, i * D:(i + 1) * D] for i in range(4)]
        cst = [arena[:, 25160 + i * 768:25160 + (i + 1) * 768] for i in range(4)]

        psum = [st.enter_context(nc.psum_tensor(f"ps{i}", [128, 512], F32)) for i in range(8)]
        psb = [Buf(f"ps{i}") for i in range(8)]

        def bank_ring(ids):
            return Ring([(psum[i], psb[i]) for i in ids])

        xTb = [[Buf(f"xT{k}_{c}") for c in range(2)] for k in range(KC)]
        hTb = [[Buf(f"hT{k}_{c}") for c in range(2)] for k in range(KC)]
        QTb = [[Buf(f"QT{g}_{t}") for t in range(NT)] for g in range(2)]
        KTb = [[Buf(f"KT{g}_{t}") for t in range(12)] for g in range(2)]
        VAa = [Buf(f"VAa{t}") for t in range(12)]
        VAb = [Buf(f"VAb{t}") for t in range(12)]
        PTb = [Buf(f"PT{i}") for i in range(2)]
        actb = [[Buf(f"act{j}_{c}") for c in range(2)] for j in range(NJ)]
        wmb = [Buf(f"wm{i}") for i in range(2)]
        attn_bufs = [b for row in QTb for b in row] + [b for row in KTb for b in row] + VAa + VAb + PTb
        ffn_bufs = [b for row in actb for b in row] + wmb
        wsb = [Buf(f"ws{i}") for i in range(NW)]
        wsb2 = [Buf(f"wsu{i}") for i in range(NW)]
        xsb = [Buf(f"xs{i}") for i in range(2)]
        cbuf = Buf("consts")
        tabb = Buf("ropetab")
        msb = Buf("MS")
        sqbb = Buf("sqb")
        sqbb2 = Buf("sqb2")
        qAb = [Buf(f"qA{i}") for i in range(NQ)]
        qBb = [Buf(f"qB{i}") for i in range(NQ)]
        qGb = [Buf(f"qG{i}") for i in range(NQ)]
        ntmpb = qAb[:3]
        rstdnb = qBb[:2]
        lnvb = qGb[0]
        sgbb = qGb[1:3]
        qstb = [Buf(f"qst{i}") for i in range(NQ)]
        qssb = [Buf(f"qss{i}") for i in range(NQ)]
        vFb = [Buf(f"vF{i}") for i in range(2)]
        cstb = [(Buf(f"cstk{i}"), Buf(f"cstg{i}")) for i in range(4)]
        arecb = [Buf(f"arec{i}") for i in range(6)]
        attb = [Buf(f"att{i}") for i in range(6)]
        atub = [Buf(f"atu{i}") for i in range(6)]
        mixb = [Buf(f"mix{i}") for i in range(4)]
        outb = Buf("dram_out")
        out_bufs = [outb]

        def w_pieces(g, l):
            ps_ = []
            for nb in ("ka", "va", "kbvb", "qa", "qb"):
                ps_.append(("in", l, nb))
            for i in range(2):
                ps_.append(("out", l, i))
            for i in range(11):
                ps_.append(("gu", l, i))
            for m in range(KC):
                ps_.append(("down", l, m))
            return ps_

        IN_COLS = {"qa": (0, 512), "ka": (512, 512), "va": (1024, 512), "qb": (1536, 512),
                   "kbvb": (2048, 256)}
        group_order = ("P", "S")
        pieces = []
        for g in group_order:
            for l in range(DEPTH):
                wp = w_pieces(g, l)
                if g == group_order[0] and l == 0:
                    wp2 = [("mod", 0, pc) for pc in range(4)] + wp[:5] + \
                          [("mod", 0, pc) for pc in range(4, 12)] + wp[5:7]
                    for i in range(11):
                        wp2.append(wp[7 + i])
                        wp2.append(("mod", 1, i))
                    wp2.append(("mod", 1, 11))
                    wp = wp2 + wp[18:]
                pieces += wp
        wstate = {"issued": 0, "cur": 0}

        def issue_piece(i):
            kind, l, idx = pieces[i]
            slot = wslots[i % NW]
            b = wsb[i % NW]
            b2 = wsb2[i % NW]
            if kind == "mod":
                dst = slot[:, 0:4096].rearrange("p (a b) -> p a b", a=8)
                src = w_mod[l].rearrange("(k p) n -> p k n", p=128)[:, :, idx * 512:(idx + 1) * 512]
                P.dma("pool", lambda h, d=dst, s=src: h.dma_start(out=d, in_=s), writes=[b, b2])
            elif kind == "in":
                c0, n = IN_COLS[idx]
                dst = slot[:, 0:8 * n].rearrange("p (a b) -> p a b", a=8)
                src = w_in[l].rearrange("(k p) n -> p k n", p=128)[:, :, c0:c0 + n]
                P.dma("pool", lambda h, d=dst, s=src: h.dma_start(out=d, in_=s), writes=[b, b2])
            elif kind == "out":
                dst = slot[:, 0:4096].rearrange("p (a b) -> p a b", a=8)
                src = w_out[l].rearrange("(k p) n -> p k n", p=128)[:, :, idx * 512:(idx + 1) * 512]
                P.dma("pool", lambda h, d=dst, s=src: h.dma_start(out=d, in_=s), writes=[b, b2])
            elif kind == "gu":
                wv = w_gu[l].rearrange("(k p) n -> p k n", p=128)
                dg = slot[:, 0:2048].rearrange("p (a b) -> p a b", a=8)
                du = slot[:, 2048:4096].rearrange("p (a b) -> p a b", a=8)
                sg_ = wv[:, :, idx * 256:(idx + 1) * 256]
                su_ = wv[:, :, HID + idx * 256:HID + (idx + 1) * 256]
                P.dma("pool", lambda h, d=dg, s=sg_: h.dma_start(out=d, in_=s), writes=[b])
                P.dma("pool", lambda h, d=du, s=su_: h.dma_start(out=d, in_=s), writes=[b2])
            else:
                dst = slot[:, 0:NJ * 128].rearrange("p (a b) -> p a b", a=NJ)
                src = w_down[l].rearrange("(j p) n -> p j n", p=128)[:, :, idx * 128:(idx + 1) * 128]
                P.dma("pool", lambda h, d=dst, s=src: h.dma_start(out=d, in_=s), writes=[b, b2])

        def w_acquire(expect_kind, ahead=0):
            i = wstate["cur"] + ahead
            assert pieces[i][0] == expect_kind, (pieces[i], expect_kind)
            while wstate["issued"] <= i:
                issue_piece(wstate["issued"])
                wstate["issued"] += 1
            wstate["b2"] = wsb2[i % NW]
            return wslots[i % NW], wsb[i % NW]

        def w_release():
            i = wstate["cur"]
            wstate["cur"] += 1
            nxt = i + NW
            if nxt < len(pieces) and wstate["issued"] <= nxt:
                while wstate["issued"] <= nxt:
                    issue_piece(wstate["issued"])
                    wstate["issued"] += 1

        def ld(dst, src):
            P.dma("sp", lambda h, d=dst, s=src: h.dma_start(out=d, in_=s), writes=[cbuf], nowait=True)

        ld(identf[:], ident_d)
        ld(condT[:], condT_d)
        ld(bmodT[:], bmodT_d)
        ld(nrmT[:], nrmT_d)
        ld(ropeC[:], ropeC_d)
        ld(ropeS[:], ropeS_d)
        ld(gains[:], gains_d.partition_broadcast(128))
        ld(gainsT[:], gainsT_d)
        P.dma("sp", lambda h: h.dma_start(out=lamv, in_=lamv_d.partition_broadcast(128)), writes=[qBb[1]])
        ld(sublnb[:], subln_d.partition_broadcast(128))
        cdone = Buf("cdone")
        P.op("dve", lambda h: h.tensor_copy(out=identb[:], in_=identf[:]), reads=[cbuf], writes=[cdone])
        P.op("dve", lambda h: h.memset(onesb[:], 1.0), writes=[cdone])
        P.op("dve", lambda h: h.memset(epst[:], EPS), writes=[cdone])
        P.op("act", lambda h: h.activation(out=scTb[:], in_=condT[:], func=AF.Silu),
             reads=[cbuf], writes=[msb])
        for l in range(DEPTH):
            lam_init = 0.8 - 0.6 * math.exp(-0.3 * l)
            lt = lamt[:, l, :]
            P.op("dve", lambda h, l=l: h.tensor_tensor(out=qA[0][:, 0:64], in0=lamv[:, l, 0, :],
                                                       in1=lamv[:, l, 1, :], op=ALU.mult),
                 reads=[cbuf, qBb[1]], writes=[qAb[0]])
            P.op("dve", lambda h, l=l: h.tensor_tensor(out=qA[0][:, 64:128], in0=lamv[:, l, 2, :],
                                                       in1=lamv[:, l, 3, :], op=ALU.mult),
                 reads=[cbuf, qBb[1]], writes=[qAb[0]])
            P.op("dve", lambda h, lt=lt: h.tensor_reduce(
                out=lt[:, 0:2], in_=qA[0][:, 0:128].rearrange("p (a b) -> p a b", a=2),
                axis=AX.X, op=ALU.add), reads=[qAb[0]], writes=[cdone])
            P.op("act", lambda h, lt=lt: h.activation(out=lt[:, 2:4], in_=lt[:, 0:2], func=AF.Exp),
                 reads=[cdone], writes=[cdone])
            P.op("dve", lambda h, lt=lt: h.tensor_tensor(out=lt[:, 4:5], in0=lt[:, 3:4], in1=lt[:, 2:3],
                                                         op=ALU.subtract), reads=[cdone], writes=[cdone])
            P.op("dve", lambda h, lt=lt, li=lam_init: h.tensor_scalar(
                out=lt[:, 5:6], in0=lt[:, 4:5], scalar1=-li, scalar2=None, op0=ALU.add),
                reads=[cdone], writes=[cdone])
            P.op("dve", lambda h, l=l, li=lam_init: h.tensor_scalar(
                out=sublnb[:, l, :], in0=sublnb[:, l, :], scalar1=1.0 - li, scalar2=None, op0=ALU.mult),
                reads=[cbuf, cdone], writes=[cdone])


        def mod_piece(l, pc, ring):
            mps, mpb = ring.next()
            wslot, wb = w_acquire("mod")
            wv = wslot[:, 0:4096].rearrange("p (a b) -> p a b", a=8)
            for m4 in range(4):
                for k in range(KC):
                    P.op("pe", lambda h, mps=mps, wv=wv, m4=m4, k=k: h.matmul(
                        mps[:, m4 * 2:m4 * 2 + 2], lhsT=wv[:, k, m4 * 128:(m4 + 1) * 128],
                        rhs=scTb[:, k, :], start=(k == 0), stop=(k == KC - 1)),
                        reads=[wb, msb], writes=[mpb])
            w_release()
            msl = MS[:, l, :, :, :].rearrange("p s k n -> p (s k) n")[:, pc * 4:(pc + 1) * 4, :]
            P.op("dve", lambda h, mps=mps, msl=msl, l=l, pc=pc: h.tensor_tensor(
                out=msl, in0=mps[:, 0:8].rearrange("p (a n) -> p a n", n=2),
                in1=bmodT[:, l, pc * 4:(pc + 1) * 4].unsqueeze(2).broadcast_to([128, 4, 2]), op=ALU.add),
                reads=[mpb, cbuf, msb], writes=[msb])
            fold = {3: (1, 0), 9: (4, 1)}.get(pc)
            if fold is not None:
                s_idx, kind = fold
                P.op("dve", lambda h, l=l, s_idx=s_idx, kind=kind: h.scalar_tensor_tensor(
                    out=MS[:, l, s_idx, :, :], in0=MS[:, l, s_idx, :, :], scalar=1.0,
                    in1=nrmT[:, kind, l, :].unsqueeze(2).broadcast_to([128, KC, 2]),
                    op0=ALU.add, op1=ALU.mult), reads=[msb, cbuf], writes=[msb])

        tr_ring = bank_ring([6, 7])

        ldstage = [sqmix[:, :].bitcast(F32)[:, i * D:(i + 1) * D] for i in range(2)]
        ldb = [Buf(f"ldst{i}") for i in range(2)]
        xring = bank_ring([2, 3, 4, 5])

        def load_x(g, tiles=range(NT), alt=False):
            ring = xring
            for t in tiles:
                if alt:
                    xs_, xb_ = ldstage[t % 2], ldb[t % 2]
                else:
                    xs_, xb_ = xstage[t % 2], xsb[t % 2]
                src = xin[g][t * 128:(t + 1) * 128, :]
                P.dma("pool" if alt else "sp", lambda h, d=xs_, s=src: h.dma_start(out=d, in_=s), writes=[xb_])
                for half in range(2):
                    ps_, pb_ = ring.next()
                    for kk in range(4):
                        k = half * 4 + kk
                        P.op("pe", lambda h, ps_=ps_, xs_=xs_, k=k, kk=kk: h.transpose(
                            out=ps_[:, kk * 128:(kk + 1) * 128], in_=xs_[:, k * 128:(k + 1) * 128],
                            identity=identf[:]), reads=[xb_, cbuf], writes=[pb_])
                    dst = xT[:, half * 4:half * 4 + 4, t * 128:(t + 1) * 128]
                    srcp = ps_[:, :].rearrange("p (a b) -> p a b", a=4)
                    eng = "act" if half == 0 else "dve"
                    wr = [xTb[half * 4 + kk][t // 4] for kk in range(4)]
                    if eng == "act":
                        P.op("act", lambda h, d=dst, s=srcp: h.activation(out=d, in_=s, func=AF.Copy),
                             reads=[pb_], writes=wr)
                    else:
                        P.op("dve", lambda h, d=dst, s=srcp: h.tensor_copy(out=d, in_=s),
                             reads=[pb_], writes=wr)

        def store_y(g, tiles):
            ring = xring
            for t in tiles:
                xs_, xb_ = xstage[t % 2], xsb[t % 2]
                for half in range(2):
                    ps_, pb_ = ring.next()
                    for kk in range(4):
                        k = half * 4 + kk
                        P.op("pe", lambda h, ps_=ps_, k=k, kk=kk, t=t: h.transpose(
                            out=ps_[:, kk * 128:(kk + 1) * 128], in_=xT[:, k, t * 128:(t + 1) * 128],
                            identity=identf[:]), reads=[xTb[k][t // 4], cbuf], writes=[pb_])
                    dst = xs_[:, half * 512:(half + 1) * 512]
                    if half == 0:
                        P.op("act", lambda h, d=dst, s=ps_: h.activation(out=d, in_=s[:, :], func=AF.Copy),
                             reads=[pb_], writes=[xb_])
                    else:
                        P.op("dve", lambda h, d=dst, s=ps_: h.tensor_copy(out=d, in_=s[:, :]),
                             reads=[pb_], writes=[xb_])
                dsty = yout[g][t * 128:(t + 1) * 128, :]
                P.dma("sp", lambda h, d=dsty, s=xs_: h.dma_start(out=d, in_=s),
                      reads=[xb_], sembuf=xb_)

        def norm_mod(l, n, s_scale, s_shift, ncb):
            ring = bank_ring([0, 1])
            for cb in range(ncb):
                cs = slice(cb * 512, (cb + 1) * 512)
                P.op("act", lambda h, cs=cs: h.activation(out=sqb[:, 0:4, :], in_=xT[:, 0:4, cs], func=AF.Square),
                     reads=[xTb[k][cb] for k in range(4)], writes=[sqbb] + mixb)
                P.op("dve", lambda h, cs=cs: h.tensor_tensor(out=sqb[:, 4:8, :], in0=xT[:, 4:8, cs],
                                                             in1=xT[:, 4:8, cs], op=ALU.mult),
                     reads=[xTb[k][cb] for k in range(4, 8)], writes=[sqbb2] + mixb)
                ps_, pb_ = ring.next()
                for k in range(KC):
                    P.op("pe", lambda h, ps_=ps_, k=k: h.matmul(ps_[:, :], lhsT=onesb[:], rhs=sqb[:, k, :],
                                                                start=(k == 0), stop=(k == KC - 1)),
                         reads=[sqbb, sqbb2, cdone] + mixb, writes=[pb_])
                rs, rsb = rstdn[cb % 2], rstdnb[cb % 2]
                P.op("act", lambda h, ps_=ps_: h.activation(out=lnv[:], in_=ps_[:, :], func=AF.Ln,
                                                            bias=epst[:, 0:1], scale=1.0 / D),
                     reads=[pb_, cdone], writes=[lnvb])
                P.op("act", lambda h, rs=rs: h.activation(out=rs[:], in_=lnv[:], func=AF.Exp, scale=-0.5),
                     reads=[lnvb], writes=[rsb])
                for k in range(KC):
                    tm, tmb = ntmp[k % 3], ntmpb[k % 3]
                    P.op("dve", lambda h, tm=tm, k=k, cs=cs, rs=rs: h.tensor_tensor(
                        out=tm[:], in0=xT[:, k, cs], in1=rs[:], op=ALU.mult),
                        reads=[xTb[k][cb], rsb], writes=[tmb])
                    if k % 8 in (0, 2, 4, 6, 7):
                        P.op("act", lambda h, tm=tm, k=k, cs=cs: h.activation(
                            out=hT[:, k, cs], in_=tm[:], func=AF.Identity,
                            bias=MS[:, l, s_shift, k, n:n + 1], scale=MS[:, l, s_scale, k, n:n + 1]),
                            reads=[tmb, msb], writes=[hTb[k][cb]])
                    else:
                        P.op("pool", lambda h, tm=tm, k=k, cs=cs: h.tensor_scalar(
                            out=hT[:, k, cs], in0=tm[:], scalar1=MS[:, l, s_scale, k, n:n + 1],
                            scalar2=MS[:, l, s_shift, k, n:n + 1], op0=ALU.mult, op1=ALU.add),
                            reads=[tmb, msb], writes=[hTb[k][cb]])

        def dbg_dump(name, src_ap, reads):
            if dbg and name in dbg_out:
                db = Buf("dbg_" + name)
                out_bufs.append(db)
                P.dma("sp", lambda h: h.dma_start(out=dbg_out[name], in_=src_ap), reads=reads, writes=[db])

        cst_all = [b for pr in cstb for b in pr]

        def prefetch_cache(l):
            for ct in range(4):
                kt = 8 + ct
                cs_, (cbk_, cb_) = cst[ct], cstb[ct]
                rs = slice(ct * 128, (ct + 1) * 128)
                P.dma("pool", lambda h, cs_=cs_, rs=rs: h.dma_start(out=cs_[:, 0:512], in_=cdk[l, rs, :]),
                      writes=[cbk_] + PTb)
                P.dma("pool", lambda h, cs_=cs_, rs=rs: h.dma_start(out=cs_[:, 512:640], in_=cgk[l, rs, :]),
                      writes=[cb_] + PTb)
                vdst = VA[:, kt, 0:516].rearrange("p (h e) -> p h e", h=4)[:, :, 0:128]
                vbdst = VA[:, kt, 516:646].rearrange("p (h e) -> p h e", h=2)[:, :, 0:64]
                P.dma("pool", lambda h, vdst=vdst, rs=rs: h.dma_start(
                    out=vdst, in_=cdv[l, rs, :].rearrange("p (h e) -> p h e", h=4)),
                    writes=[VAa[kt]], sembuf=VAa[kt])
                P.dma("pool", lambda h, vbdst=vbdst, rs=rs: h.dma_start(
                    out=vbdst, in_=cgv[l, rs, :].rearrange("p (h e) -> p h e", h=2)),
                    writes=[VAb[kt]], sembuf=VAb[kt])

        def proj_step(g, l, tq_tiles):
            rope = (g == "S")
            proj_ring = bank_ring([2, 3, 4, 5])
            nkt = 12 if g == "S" else 8
            va4 = VA[:, 0:nkt, 0:516].rearrange("p t (h e) -> p t h e", h=4)[:, :, :, 128:129]
            vb2 = VA[:, 0:nkt, 516:646].rearrange("p t (h e) -> p t h e", h=2)[:, :, :, 64:65]
            P.op("pool", lambda h: h.memset(va4, 1.0), writes=VAa[:nkt] + ffn_bufs)
            P.op("pool", lambda h: h.memset(vb2, 1.0), writes=VAb[:nkt] + ffn_bufs)

            pending = []
            ucount = [0]

            def flush(keep):
                while len(pending) > keep:
                    pending.pop(0)()

            def build_rope_tables(gi):
                g1 = gains[:, l, gi, 0:32].unsqueeze(1).broadcast_to([128, NT, 32])
                g2 = gains[:, l, gi, 32:64].unsqueeze(1).broadcast_to([128, NT, 32])
                gf = gains[:, l, gi, :].unsqueeze(1).broadcast_to([128, NT, 64])
                P.op("dve", lambda h: h.tensor_tensor(out=TCg[:], in0=ropeC[:], in1=gf, op=ALU.mult),
                     reads=[cbuf], writes=[tabb])
                P.op("dve", lambda h: h.tensor_tensor(out=TSg[:, :, 0:32], in0=ropeS[:, :, 0:32], in1=g2,
                                                      op=ALU.mult), reads=[cbuf], writes=[tabb])
                P.op("dve", lambda h: h.tensor_tensor(out=TSg[:, :, 32:64], in0=ropeS[:, :, 32:64], in1=g1,
                                                      op=ALU.mult), reads=[cbuf], writes=[tabb])

            def qk_chain(ps_, pb_, nh, gi, t, slot, is_k, kout):
                w = nh * 64
                A, B, G, S_, SS = qA[slot], qB[slot], qG[slot], qst[slot], qss[slot]
                Ab, Bb, Gb, Sb, SSb = qAb[slot], qBb[slot], qGb[slot], qstb[slot], qssb[slot]
                v3 = lambda ap: ap[:, 0:w].rearrange("p (a b) -> p a b", a=nh)
                if rope:
                    P.op("act", lambda h: h.activation(out=B[:, 0:w], in_=ps_[:, 0:w], func=AF.Copy),
                         reads=[pb_], writes=[Bb])
                P.op("act", lambda h: h.activation(out=A[:, 0:w], in_=ps_[:, 0:w], func=AF.Square),
                     reads=[pb_], writes=[Ab])
                P.op("dve", lambda h: h.tensor_reduce(out=SS[:, 0:nh], in_=v3(A), axis=AX.X, op=ALU.add),
                     reads=[Ab], writes=[SSb])
                P.op("act", lambda h: h.activation(out=SS[:, 8:8 + nh], in_=SS[:, 0:nh], func=AF.Ln,
                                                   bias=epst[:, 0:1], scale=1.0 / 64),
                     reads=[SSb, cdone], writes=[SSb])
                P.op("act", lambda h: h.activation(out=SS[:, 16:16 + nh], in_=SS[:, 8:8 + nh], func=AF.Exp,
                                                   scale=-0.5), reads=[SSb], writes=[SSb])
                rstd_bc = SS[:, 16:16 + nh].unsqueeze(2).broadcast_to([128, nh, 64])
                if not rope:
                    if is_k:
                        def stage2():
                            P.op("dve", lambda h: h.tensor_tensor(out=v3(B), in0=v3(ps_), in1=rstd_bc, op=ALU.mult),
                                 reads=[pb_, SSb], writes=[Bb])
                            P.op("act", lambda h: h.activation(out=S_[:, 0:w], in_=B[:, 0:w], func=AF.Copy),
                                 reads=[Bb], writes=[Sb])
                            gain_bc = gains[:, l, gi, :].unsqueeze(1).broadcast_to([128, nh, 64])
                            P.op("pool", lambda h: h.tensor_tensor(out=v3(G), in0=v3(B), in1=gain_bc, op=ALU.mult),
                                 reads=[Bb, cbuf], writes=[Gb])
                            P.dma("sp", lambda h: h.dma_start(out=kout, in_=G[:, 0:w]), reads=[Gb], sembuf=Gb)
                    else:
                        def stage2():
                            P.op("dve", lambda h: h.tensor_tensor(out=v3(S_), in0=v3(ps_), in1=rstd_bc, op=ALU.mult),
                                 reads=[pb_, SSb], writes=[Sb])
                else:
                    A3, B3, G3 = v3(A), v3(B), v3(G)
                    cC = TCg[:, t, :].unsqueeze(1).broadcast_to([128, nh, 64])
                    s1 = TSg[:, t, 0:32].unsqueeze(1).broadcast_to([128, nh, 32])
                    s2 = TSg[:, t, 32:64].unsqueeze(1).broadcast_to([128, nh, 32])
                    P.op("pool" if slot % 2 else "dve",
                         lambda h: h.tensor_tensor(out=G3, in0=B3, in1=cC, op=ALU.mult),
                         reads=[Bb, tabb], writes=[Gb])
                    P.op("pool", lambda h: h.tensor_tensor(out=A3[:, :, 0:32], in0=B3[:, :, 32:64], in1=s1,
                                                           op=ALU.mult), reads=[Bb, tabb], writes=[Ab])
                    P.op("pool", lambda h: h.tensor_tensor(out=A3[:, :, 32:64], in0=B3[:, :, 0:32], in1=s2,
                                                           op=ALU.mult), reads=[Bb, tabb], writes=[Ab])

                    def stage2():
                        P.op("dve", lambda h: h.tensor_tensor(out=G3, in0=G3, in1=A3, op=ALU.add),
                             reads=[Gb, Ab], writes=[Gb])
                        P.op("dve", lambda h: h.tensor_tensor(out=v3(S_), in0=G3, in1=rstd_bc, op=ALU.mult),
                             reads=[Gb, SSb], writes=[Sb])
                return stage2

            q2 = []

            def defer(stage2_fn, tr_fn):
                q2.append((stage2_fn, tr_fn))
                while len(q2) > 1:
                    s2_, tr_ = q2.pop(0)
                    s2_()
                    pending.append(tr_)

            def drain_q2():
                while q2:
                    s2_, tr_ = q2.pop(0)
                    s2_()
                    pending.append(tr_)

            def transposes(src, srcb, nchunk, dst_fn, dst_bufs, gi=None):
                def run():
                    ps_, pb_ = tr_ring.next()
                    pv = ps_[:, :].bitcast(BF16)
                    for c in range(nchunk):
                        P.op("pe", lambda h, c=c: h.transpose(out=pv[:, c * 128:(c + 1) * 128],
                                                              in_=src[:, c * 128:(c + 1) * 128],
                                                              identity=identb[:]),
                             reads=srcb + [cdone], writes=[pb_])
                    dst = dst_fn()
                    srcv = pv[:, 0:nchunk * 128].rearrange("p (a b) -> p a b", a=nchunk)
                    ucount[0] += 1
                    if rope:
                        ucount[0] = 0
                    if gi is not None:
                        gsc = gainsT[:, l, gi:gi + 1]
                        if ucount[0] % 2 == 0:
                            P.op("act", lambda h: h.activation(out=dst, in_=srcv, func=AF.Copy, scale=gsc),
                                 reads=[pb_, cbuf], writes=dst_bufs + ffn_bufs)
                        else:
                            P.op("dve", lambda h: h.tensor_scalar(out=dst, in0=srcv, scalar1=gsc, scalar2=None,
                                                                  op0=ALU.mult),
                                 reads=[pb_, cbuf], writes=dst_bufs + ffn_bufs)
                    elif ucount[0] % 2 == 0:
                        P.op("act", lambda h: h.activation(out=dst, in_=srcv, func=AF.Copy),
                             reads=[pb_], writes=dst_bufs + ffn_bufs)
                    else:
                        P.op("dve", lambda h: h.tensor_copy(out=dst, in_=srcv),
                             reads=[pb_], writes=dst_bufs + ffn_bufs)
                return run

            slot_ctr = [0]
            vslot_ctr = [0]
            r8, rq = list(range(NT)), list(range(tq_tiles))
            grp_specs = [((("ka", r8), ("va", r8[0:4])), 1),
                         ((("va", r8[4:8]), ("kbvb", r8)), 2),
                         ((("qa", rq),), 1), ((("qb", rq),), 1)]
            for (grp, nrel) in grp_specs:
                views = {}
                for ai, (nb_, _tl) in enumerate(grp):
                    if rope and nb_ != "va":
                        build_rope_tables({"qa": 0, "ka": 1, "qb": 2, "kbvb": 3}[nb_])
                    assert pieces[wstate["cur"] + ai][2] == nb_
                    wslot_, wb_ = w_acquire("in", ahead=ai)
                    nc_ = IN_COLS[nb_][1]
                    views[nb_] = (wslot_[:, 0:8 * nc_].rearrange("p (a b) -> p a b", a=8), wb_, nc_)
                chain_u = [(nb_, t_) for (nb_, tl) in grp if nb_ != "va" for t_ in tl]
                free_u = [(nb_, t_) for (nb_, tl) in grp if nb_ == "va" for t_ in tl]
                step = max(1, len(chain_u) // max(1, len(free_u)))
                unit_list = []
                for ci, u_ in enumerate(chain_u):
                    unit_list.append(u_)
                    if free_u and (ci + 1) % step == 0:
                        unit_list.append(free_u.pop(0))
                unit_list += free_u
                for (nb, t) in unit_list:
                    wv, wb, ncols = views[nb]
                    ps_, pb_ = proj_ring.next()
                    for k in range(KC):
                        P.op("pe", lambda h, ps_=ps_, k=k, t=t, wv=wv, ncols=ncols: h.matmul(
                            ps_[:, 0:ncols], lhsT=hT[:, k, t * 128:(t + 1) * 128], rhs=wv[:, k, :],
                            start=(k == 0), stop=(k == KC - 1)),
                            reads=[hTb[k][t // 4], wb], writes=[pb_])
                    seq, r0 = t // 2, (t % 2) * 128
                    if nb in ("qa", "qb", "ka"):
                        slot = slot_ctr[0] % NQ
                        slot_ctr[0] += 1
                        gi = {"qa": 0, "ka": 1, "qb": 2}[nb]
                        kout = ndk[seq, l, r0:r0 + 128, :] if nb == "ka" else None
                        st2 = qk_chain(ps_, pb_, 8, gi, t, slot, nb == "ka" and g == "P", kout)
                        if nb == "ka":
                            dst_fn = (lambda t=t: KT[:, 0:4, t * 128:(t + 1) * 128])
                            dbs = [KTb[0][t]]
                        else:
                            c0 = 0 if nb == "qa" else 4
                            dst_fn = (lambda t=t, c0=c0: QT[:, c0:c0 + 4, t * 128:(t + 1) * 128])
                            dbs = [QTb[0 if nb == "qa" else 1][t]]
                        defer(st2, transposes(qst[slot], [qstb[slot]], 4, dst_fn, dbs, gi=(None if rope else gi)))
                    elif nb == "va":
                        vdst = VA[:, t, 0:516].rearrange("p (h e) -> p h e", h=4)[:, :, 0:128]
                        if g == "P":
                            vs = vslot_ctr[0] % 2
                            vslot_ctr[0] += 1
                            P.op("act", lambda h, vs=vs, ps_=ps_: h.activation(out=vF[vs][:], in_=ps_[:, :],
                                                                              func=AF.Copy),
                                 reads=[pb_], writes=[vFb[vs]])
                            vo = ndv[seq, l, r0:r0 + 128, :]
                            P.dma("sp", lambda h, vs=vs, vo=vo: h.dma_start(out=vo, in_=vF[vs][:]),
                                  reads=[vFb[vs]], sembuf=vFb[vs])
                            P.op("pool", lambda h, vs=vs, vdst=vdst: h.tensor_copy(
                                out=vdst, in_=vF[vs][:].rearrange("p (h e) -> p h e", h=4)),
                                reads=[vFb[vs]], writes=[VAa[t]] + ffn_bufs)
                        else:
                            P.op("act", lambda h, ps_=ps_, vdst=vdst: h.activation(
                                out=vdst, in_=ps_[:, :].rearrange("p (h e) -> p h e", h=4), func=AF.Copy),
                                reads=[pb_], writes=[VAa[t]] + ffn_bufs)
                    else:
                        slot = slot_ctr[0] % NQ
                        slot_ctr[0] += 1
                        kout = ngk[seq, l, r0:r0 + 128, :]
                        vbdst = VA[:, t, 516:646].rearrange("p (h e) -> p h e", h=2)[:, :, 0:64]
                        if g == "P":
                            vs = vslot_ctr[0] % 2
                            vslot_ctr[0] += 1
                            P.op("act", lambda h, vs=vs, ps_=ps_: h.activation(
                                out=vF[vs][:, 0:128], in_=ps_[:, 128:256], func=AF.Copy),
                                reads=[pb_], writes=[vFb[vs]])
                            vo = ngv[seq, l, r0:r0 + 128, :]
                            P.dma("sp", lambda h, vs=vs, vo=vo: h.dma_start(out=vo, in_=vF[vs][:, 0:128]),
                                  reads=[vFb[vs]], sembuf=vFb[vs])
                            P.op("pool", lambda h, vs=vs, vbdst=vbdst: h.tensor_copy(
                                out=vbdst, in_=vF[vs][:, 0:128].rearrange("p (h e) -> p h e", h=2)),
                                reads=[vFb[vs]], writes=[VAb[t]] + ffn_bufs)
                        else:
                            P.op("act", lambda h, ps_=ps_, vbdst=vbdst: h.activation(
                                out=vbdst, in_=ps_[:, 128:256].rearrange("p (h e) -> p h e", h=2),
                                func=AF.Copy), reads=[pb_], writes=[VAb[t]] + ffn_bufs)
                        st2k = qk_chain(ps_, pb_, 2, 3, t, slot, g == "P", kout)

                        def st2(st2k=st2k, slot=slot):
                            st2k()
                            S_ = qst[slot]
                            P.op("pool", lambda h, S_=S_: h.tensor_copy(
                                out=S_[:, 128:256].rearrange("p (a b) -> p a b", a=2),
                                in_=S_[:, 64:128].unsqueeze(1).broadcast_to([128, 2, 64])),
                                reads=[qstb[slot]], writes=[qstb[slot]])
                            P.op("pool", lambda h, S_=S_: h.tensor_copy(out=S_[:, 64:128], in_=S_[:, 0:64]),
                                 reads=[qstb[slot]], writes=[qstb[slot]])
                        dst_fn = (lambda t=t: KT[:, 4:6, t * 128:(t + 1) * 128])
                        defer(st2, transposes(qst[slot], [qstb[slot]], 2, dst_fn, [KTb[1][t]], gi=(None if rope else 3)))
                    flush(NQ - 2)
                drain_q2()
                flush(NQ - 2)
                for _ in range(nrel):
                    w_release()
            if g == "S":
                for ct in range(4):
                    kt = 8 + ct
                    cs_, (cbk_, cb_) = cst[ct], cstb[ct]
                    P.op("pool", lambda h, cs_=cs_: h.tensor_copy(
                        out=cs_[:, 640:768].rearrange("p (a b) -> p a b", a=2),
                        in_=cs_[:, 576:640].unsqueeze(1).broadcast_to([128, 2, 64])),
                        reads=[cb_], writes=[cb_])
                    P.op("pool", lambda h, cs_=cs_: h.tensor_copy(out=cs_[:, 576:640], in_=cs_[:, 512:576]),
                         reads=[cb_], writes=[cb_])
                    dst_fn = (lambda kt=kt: KT[:, 0:6, kt * 128:(kt + 1) * 128])
                    pending.append(transposes(cs_, [cbk_, cb_], 6, dst_fn, [KTb[0][kt], KTb[1][kt]]))
                    flush(1)
            flush(0)

        def attention(g, l, tq_tiles):
            if g == "P":
                s_ring = bank_ring([0, 1])
                o_ring = bank_ring([2, 3, 4, 5])
                mt_ring = bank_ring([6, 7])
            else:
                s_ring = bank_ring([0, 1, 2])
                o_ring = bank_ring([3, 4, 5, 6])
                mt_ring = bank_ring([7])
            if g == "P":
                blocks = [(s_ * 2, [s_ * 2, s_ * 2 + 1]) for s_ in range(4)]
            else:
                blocks = [(qb * 2, list(range(12))) for qb in range(tq_tiles // 2)]
            pctr = [0]
            att_alias = qAb[0:3] + attb + atub
            P.op("pool", lambda h: h.memset(qA[0][:, 0:8], 0.0), writes=att_alias)

            def mix_transposes(qt0, mslots):
                def run():
                    for i, ms in enumerate(mslots):
                        qt = qt0 + i
                        ps_, pb_ = mt_ring.next()
                        pv = ps_[:, :].bitcast(BF16)
                        for c in range(KC):
                            P.op("pe", lambda h, c=c, ms=ms, pv=pv: h.transpose(
                                out=pv[:, c * 128:(c + 1) * 128], in_=mixtok[ms][:, c * 128:(c + 1) * 128],
                                identity=identb[:]), reads=[mixb[ms], cdone], writes=[pb_])
                        dst = hT[:, :, qt * 128:(qt + 1) * 128]
                        srcv = pv[:, :].rearrange("p (a b) -> p a b", a=KC)
                        if g == "P" and i == 0:
                            P.op("act", lambda h, dst=dst, srcv=srcv: h.activation(out=dst, in_=srcv, func=AF.Copy),
                                 reads=[pb_], writes=[hTb[k][qt // 4] for k in range(KC)])
                        else:
                            P.op("dve", lambda h, dst=dst, srcv=srcv: h.tensor_copy(out=dst, in_=srcv),
                                 reads=[pb_], writes=[hTb[k][qt // 4] for k in range(KC)])
                return run

            units = []
            for bi, (qt0, ktiles) in enumerate(blocks):
                hc = [("d", h_, 0) for h_ in range(4)] + [("g", g_, rp_) for g_ in range(2) for rp_ in range(2)]
                for ui, (kind, a_, b_) in enumerate(hc):
                    units.append(dict(bi=bi, qt0=qt0, ktiles=ktiles, kind=kind, a=a_, b=b_, first=(ui == 0),
                                      last=(ui == len(hc) - 1), mslots=[(bi % 2) * 2 + i for i in range(2)]))
            for ui, u in enumerate(units):
                u["pslot"] = ui % 2
            obanks = {}
            qbd_b = xsb
            post_fin = []
            mod_tail = list(range(4, 12)) if (g == group_order[0] and l == 0) else []

            def build_qbd(u):
                qt0 = u["qt0"]
                qcs = slice(qt0 * 128, qt0 * 128 + 256)
                rd = [QTb[0][qt0], QTb[0][qt0 + 1], QTb[1][qt0], QTb[1][qt0 + 1]]
                P.op("dve", lambda h, qcs=qcs: h.tensor_copy(out=Qbd[0:64, :, 0:256], in_=QT[0:64, :, qcs]),
                     reads=rd, writes=qbd_b)
                P.op("dve", lambda h, qcs=qcs: h.tensor_copy(out=Qbd[64:128, :, 256:512], in_=QT[64:128, :, qcs]),
                     reads=rd, writes=qbd_b)

            def S_step(u, ki):
                kind, a_, b_ = u["kind"], u["a"], u["b"]
                kt = u["ktiles"][ki]
                if kind == "d":
                    qch, kch, kgrp = a_, a_, 0
                else:
                    qch, kch, kgrp = 4 + 2 * a_ + b_, 4 + a_, 1
                pslot = u["pslot"]
                ps_, pb_ = s_ring.next()
                P.op("pe", lambda h, ps_=ps_, kch=kch, kt=kt, qch=qch: h.matmul(
                    ps_[:, :], lhsT=KT[:, kch, kt * 128:(kt + 1) * 128], rhs=Qbd[:, qch, :],
                    start=True, stop=True),
                    reads=[KTb[kgrp][kt]] + qbd_b, writes=[pb_])
                P.op("act", lambda h, ps_=ps_, pslot=pslot, ki=ki: h.activation(
                    out=PT[:, pslot, ki, :], in_=ps_[:, :], func=AF.Exp, scale=0.125),
                    reads=[pb_], writes=[PTb[pslot]] + ffn_bufs + cst_all)
                if g == "S" and PE_WARM:
                    wps, wpb = mt_ring.next()
                    for _ in range(PE_WARM):
                        P.op("pe", lambda h, wps=wps, kch=kch, kt=kt, qch=qch: h.matmul(
                            wps[:, :], lhsT=KT[:, kch, kt * 128:(kt + 1) * 128], rhs=Qbd[:, qch, :],
                            start=True, stop=True),
                            reads=[KTb[kgrp][kt]] + qbd_b, writes=[wpb])

            def O_group(u, i, cc, k0=0, k1=None):
                kind, a_, b_ = u["kind"], u["a"], u["b"]
                nk = len(u["ktiles"])
                k1 = nk if k1 is None else k1
                pslot = u["pslot"]
                key = (u["bi"], kind, a_, i)
                if b_ == 0 and cc == 0 and k0 == 0:
                    obanks[key] = o_ring.next()
                ob, obb = obanks[key]
                for ki, kt in list(enumerate(u["ktiles"]))[k0:k1]:
                    if kind == "d":
                        oap = ob[:, cc * 129:(cc + 1) * 129]
                        rhs = VA[:, kt, a_ * 129:(a_ + 1) * 129]
                        vb_ = VAa
                    else:
                        r = 2 * b_ + cc
                        oap = ob[:, r * 65:(r + 1) * 65]
                        rhs = VA[:, kt, 516 + a_ * 65:516 + (a_ + 1) * 65]
                        vb_ = VAb
                    P.op("pe", lambda h, oap=oap, pslot=pslot, ki=ki, i=i, cc=cc, rhs=rhs, nk=nk: h.matmul(
                        oap, lhsT=PT[:, pslot, ki, cc * 256 + i * 128:cc * 256 + (i + 1) * 128], rhs=rhs,
                        start=(ki == 0), stop=(ki == nk - 1)),
                        reads=[PTb[pslot], vb_[kt]], writes=[obb])

            def post(u):
                kind, a_, b_ = u["kind"], u["a"], u["b"]
                mslots = u["mslots"]
                if kind == "d":
                    ctx = []
                    for i in range(2):
                        ob, obb = obanks[(u["bi"], "d", a_, i)]
                        ps2 = pctr[0] % 6
                        pctr[0] += 1
                        ctx.append(dict(ob=ob, obb=obb, ms=mslots[i], rc=arec[ps2], rcb=arecb[ps2],
                                        tt=at_t[ps2], ttb=attb[ps2], uu=at_u[ps2], uub=atub[ps2],
                                        o3=ob[:, 0:258].rearrange("p (c e) -> p c e", c=2)))
                    for c in ctx:
                        P.op("dve", lambda h, c=c: h.reciprocal(out=c["rc"][:, 0:2], in_=c["o3"][:, :, 128]),
                             reads=[c["obb"]], writes=[c["rcb"]])
                    for c in ctx:
                        P.op("dve", lambda h, c=c: h.tensor_tensor(out=c["rc"][:, 2:3], in0=c["rc"][:, 1:2],
                                                                   in1=lamt[:, l, 5:6], op=ALU.mult),
                             reads=[c["rcb"], cdone], writes=[c["rcb"]])
                    for c in ctx:
                        if g == "P":
                            P.op("act", lambda h, c=c: h.activation(
                                out=c["tt"][:], in_=c["ob"][:, 0:128], func=AF.Copy, scale=c["rc"][:, 0:1]),
                                reads=[c["obb"], c["rcb"]], writes=[c["ttb"]])
                        else:
                            P.op("dve", lambda h, c=c: h.tensor_scalar(
                                out=c["tt"][:], in0=c["ob"][:, 0:128], scalar1=c["rc"][:, 0:1], scalar2=None,
                                op0=ALU.mult), reads=[c["obb"], c["rcb"]], writes=[c["ttb"]])
                    for c in ctx:
                        P.op("dve", lambda h, c=c: h.scalar_tensor_tensor(
                            out=c["uu"][:], in0=c["ob"][:, 129:257], scalar=c["rc"][:, 2:3], in1=c["tt"][:],
                            op0=ALU.mult, op1=ALU.add), reads=[c["obb"], c["rcb"], c["ttb"]], writes=[c["uub"]])
                    for c in ctx:
                        if g == "P":
                            P.op("act", lambda h, c=c: h.activation(
                                out=c["tt"][:], in_=c["uu"][:], func=AF.Square, accum_out=c["rc"][:, 4:5]),
                                reads=[c["uub"]], writes=[c["ttb"], c["rcb"]])
                        else:
                            P.op("dve", lambda h, c=c: h.scalar_tensor_tensor(
                                out=c["tt"][:], in0=c["uu"][:], scalar=1.0, in1=c["uu"][:], op0=ALU.mult,
                                op1=ALU.mult, accum_out=c["rc"][:, 4:5]),
                                reads=[c["uub"]], writes=[c["ttb"], c["rcb"]])
                    for c in ctx:
                        P.op("act", lambda h, c=c: h.activation(out=c["rc"][:, 5:6], in_=c["rc"][:, 4:5],
                                                                func=AF.Ln, bias=epst[:, 0:1], scale=1.0 / 128),
                             reads=[c["rcb"], cdone], writes=[c["rcb"]])
                        P.op("act", lambda h, c=c: h.activation(out=c["rc"][:, 6:7], in_=c["rc"][:, 5:6],
                                                                func=AF.Exp, scale=-0.5),
                             reads=[c["rcb"]], writes=[c["rcb"]])
                    def fin(ctx=ctx, a_=a_):
                        for c in ctx:
                            P.op("dve", lambda h, c=c: h.scalar_tensor_tensor(
                                out=mixtok[c["ms"]][:, a_ * 128:(a_ + 1) * 128], in0=c["uu"][:],
                                scalar=c["rc"][:, 6:7], in1=sublnb[:, l, :], op0=ALU.mult, op1=ALU.mult),
                                reads=[c["uub"], c["rcb"], cdone], writes=[mixb[c["ms"]], sqbb, sqbb2])
                    post_fin.append(fin)
                if kind == "g" and b_ == 1:
                    for i in range(2):
                        ob, obb = obanks[(u["bi"], "g", a_, i)]
                        ms = mslots[i]
                        ps2 = pctr[0] % 6
                        pctr[0] += 1
                        rc, rcb = arec[ps2], arecb[ps2]
                        o3 = ob[:, 0:260].rearrange("p (r e) -> p r e", r=4)
                        P.op("dve", lambda h, rc=rc, o3=o3: h.reciprocal(out=rc[:, 8:12], in_=o3[:, :, 64]),
                             reads=[obb], writes=[rcb])
                        mdst = mixtok[ms][:, 512 + a_ * 256:512 + (a_ + 1) * 256].rearrange(
                            "p (r e) -> p r e", r=4)
                        P.op("dve", lambda h, rc=rc, o3=o3, mdst=mdst: h.tensor_tensor(
                            out=mdst, in0=o3[:, :, 0:64],
                            in1=rc[:, 8:12].unsqueeze(2).broadcast_to([128, 4, 64]), op=ALU.mult),
                            reads=[obb, rcb], writes=[mixb[ms], sqbb, sqbb2])

            pending = []
            nu = len(units)
            P.op("pool", lambda h: h.memset(Qbd[64:128, :, 0:256], 0.0), writes=qbd_b)
            P.op("pool", lambda h: h.memset(Qbd[0:64, :, 256:512], 0.0), writes=qbd_b)
            ogroups = [(i, cc) for i in range(2) for cc in range(2)]
            for idx in range(nu + 1):
                cur = units[idx] if idx < nu else None
                prv = units[idx - 1] if idx >= 1 else None
                nk = len((cur or prv)["ktiles"])
                if cur is not None and cur["first"] and idx == 0:
                    build_qbd(cur)
                chunks = [list(range(c0, min(c0 + SCHUNK, nk))) for c0 in range(0, nk, SCHUNK)]
                opieces = []
                if prv is not None:
                    nkp = len(prv["ktiles"])
                    for (i_, cc_) in ogroups:
                        for k0 in range(0, nkp, OSUB):
                            opieces.append((i_, cc_, k0, min(k0 + OSUB, nkp)))
                gi = 0
                for ci, ch in enumerate(chunks):
                    if cur is not None:
                        for ki in ch:
                            S_step(cur, ki)
                    if prv is not None:
                        ng = -(-len(opieces) * (ci + 1) // len(chunks))
                        while gi < ng:
                            O_group(prv, *opieces[gi])
                            gi += 1
                if cur is not None and cur["last"] and idx + 1 < nu:
                    build_qbd(units[idx + 1])
                if prv is not None:
                    while gi < len(opieces):
                        O_group(prv, *opieces[gi])
                        gi += 1
                    nfin = len(post_fin)
                    post(prv)
                    for _ in range(nfin):
                        post_fin.pop(0)()
                    if prv["last"]:
                        while post_fin:
                            post_fin.pop(0)()
                        if mod_tail:
                            for _ in range(2):
                                mod_piece(0, mod_tail.pop(0), s_ring)
                    for pnd in pending:
                        pnd[0] -= 1
                    if prv["last"]:
                        pending.append([MIX_DEFER, mix_transposes(prv["qt0"], prv["mslots"])])
                    while pending and pending[0][0] <= 0:
                        pending.pop(0)[1]()
            while pending:
                pending.pop(0)[1]()
            P.op("pool", lambda h: h.memset(qA[0][:, 0:8], 0.0), writes=att_alias)

        def out_proj(l, n, ntb):
            ring = bank_ring([0, 1, 2, 3])
            for pc in range(2):
                wslot, wb = w_acquire("out")
                wv = wslot[:, 0:4096].rearrange("p (a b) -> p a b", a=8)
                for tb in range(ntb):
                    cs = slice(tb * 512, (tb + 1) * 512)
                    for mm in range(4):
                        m = pc * 4 + mm
                        ps_, pb_ = ring.next()
                        for k in range(KC):
                            P.op("pe", lambda h, ps_=ps_, k=k, mm=mm, cs=cs, wv=wv: h.matmul(
                                ps_[:, :], lhsT=wv[:, k, mm * 128:(mm + 1) * 128], rhs=hT[:, k, cs],
                                start=(k == 0), stop=(k == KC - 1)),
                                reads=[wb, hTb[k][tb]], writes=[pb_])
                        P.op("dve", lambda h, ps_=ps_, m=m, cs=cs: h.scalar_tensor_tensor(
                            out=xT[:, m, cs], in0=ps_[:, :], scalar=MS[:, l, 2, m, n:n + 1], in1=xT[:, m, cs],
                            op0=ALU.mult, op1=ALU.add), reads=[pb_, msb, xTb[m][tb]], writes=[xTb[m][tb]])
                w_release()

        def ffn(l, n, ntb, with_mod=None):
            if with_mod is not None:
                mring = bank_ring([0, 1])
                ring = bank_ring([2, 3, 4, 5, 6, 7])
            else:
                ring = bank_ring([0, 1, 2, 3, 4, 5, 6, 7])
            sctr = [0]
            for pc in range(11):
                wslot, wb = w_acquire("gu")
                wb2 = wstate["b2"]
                wg = wslot[:, 0:2048].rearrange("p (a b) -> p a b", a=8)
                wu = wslot[:, 2048:4096].rearrange("p (a b) -> p a b", a=8)
                for tb in range(ntb):
                    cs = slice(tb * 512, (tb + 1) * 512)
                    for jj in range(2):
                        j = pc * 2 + jj
                        pg, pgb = ring.next()
                        pu, pub = ring.next()
                        for (pp, ppb, wv, wbx) in ((pg, pgb, wg, wb), (pu, pub, wu, wb2)):
                            for k in range(KC):
                                P.op("pe", lambda h, pp=pp, k=k, jj=jj, cs=cs, wv=wv: h.matmul(
                                    pp[:, :], lhsT=wv[:, k, jj * 128:(jj + 1) * 128], rhs=hT[:, k, cs],
                                    start=(k == 0), stop=(k == KC - 1)),
                                    reads=[wbx, hTb[k][tb]], writes=[ppb])
                        ss = sctr[0] % 2
                        sctr[0] += 1
                        P.op("act", lambda h, ss=ss, pg=pg: h.activation(out=sgb[ss][:], in_=pg[:, :], func=AF.Silu),
                             reads=[pgb], writes=[sgbb[ss]])
                        P.op("dve", lambda h, ss=ss, pu=pu, j=j, cs=cs: h.tensor_tensor(
                            out=actT[:, j, cs], in0=sgb[ss][:], in1=pu[:, :], op=ALU.mult),
                            reads=[sgbb[ss], pub], writes=[actb[j][tb]] + attn_bufs)
                w_release()
                if with_mod is not None:
                    mod_piece(with_mod, pc, mring)
                    if pc == 10:
                        mod_piece(with_mod, 11, mring)
            ring2 = bank_ring([0, 1, 2, 3])
            for m in range(KC):
                wslot, wb = w_acquire("down")
                wv = wslot[:, 0:NJ * 128].rearrange("p (a b) -> p a b", a=NJ)
                for tb in range(ntb):
                    cs = slice(tb * 512, (tb + 1) * 512)
                    ps_, pb_ = ring2.next()
                    for j in range(NJ):
                        P.op("pe", lambda h, ps_=ps_, j=j, cs=cs, wv=wv: h.matmul(
                            ps_[:, :], lhsT=wv[:, j, :], rhs=actT[:, j, cs], start=(j == 0), stop=(j == NJ - 1)),
                            reads=[wb, actb[j][tb]], writes=[pb_])
                    P.op("dve", lambda h, ps_=ps_, m=m, cs=cs: h.scalar_tensor_tensor(
                        out=xT[:, m, cs], in0=ps_[:, :], scalar=MS[:, l, 5, m, n:n + 1], in1=xT[:, m, cs],
                        op0=ALU.mult, op1=ALU.add), reads=[pb_, msb, xTb[m][tb]], writes=[xTb[m][tb]])
                w_release()

        for g in group_order:
            n = 0 if g == "P" else 1
            mark(f"{g}_loadx")
            if g == group_order[0]:
                load_x(g)
            if g == group_order[0]:
                ring0 = bank_ring([0, 1])
                for pc in range(4):
                    mod_piece(0, pc, ring0)
            for l in range(DEPTH):
                tq_tiles = 4 if (g == "S" and l == DEPTH - 1) else NT
                ntb = tq_tiles // 4
                all_h = [b for row in hTb for b in row]
                all_x = [b for row in xTb for b in row]
                mark(f"{g}{l}_norm1")
                if g == "S" and l == 0:
                    prefetch_cache(0)
                norm_mod(l, n, 1, 0, 2)
                dbg_dump(f"{g}{l}_h", hT[:], all_h)
                mark(f"{g}{l}_proj")
                proj_step(g, l, tq_tiles)
                dbg_dump(f"{g}{l}_QT", QT, attn_bufs)
                dbg_dump(f"{g}{l}_KT", KT, attn_bufs)
                dbg_dump(f"{g}{l}_VA", VA, attn_bufs)
                mark(f"{g}{l}_attn")
                attention(g, l, tq_tiles)
                dbg_dump(f"{g}{l}_mix", hT[:], all_h)
                mark(f"{g}{l}_outproj")
                if g == "S" and l + 1 < DEPTH:
                    prefetch_cache(l + 1)
                out_proj(l, n, ntb)
                dbg_dump(f"{g}{l}_x1", xT[:], all_x)
                mark(f"{g}{l}_norm2")
                norm_mod(l, n, 4, 3, ntb)
                dbg_dump(f"{g}{l}_h2", hT[:], all_h)
                mark(f"{g}{l}_ffn")
                ffn(l, n, ntb, with_mod=(1 if (g == group_order[0] and l == 0) else None))
                dbg_dump(f"{g}{l}_x2", xT[:], all_x)
            mark(f"{g}_store")
            gi_ = group_order.index(g)
            if gi_ + 1 < len(group_order):
                gn = group_order[gi_ + 1]
                lal = [sqbb, sqbb2] + mixb + ldb
                P.op("pool", lambda h: h.memset(sqmix[:, 0:8], 0.0), writes=lal)
                store_y(g, range(0, 4))
                load_x(gn, range(0, 4), alt=True)
                store_y(g, range(4, 8))
                load_x(gn, range(4, 8), alt=True)
                P.op("pool", lambda h: h.memset(sqmix[:, 0:8], 0.0), writes=lal)
            else:
                store_y(g, range(NT if g == "P" else NT // 2))

        mark("end")
        P.final_wait("sp", out_bufs + xsb + qGb + vFb)
        P.emit()
    return nc


def _rope_tables(perm):
    t = np.asarray(perm, dtype=np.int64)
    row = (t // 64).astype(np.float32)
    col = (t % 64).astype(np.float32)
    freqs = (10000.0 ** (-np.arange(0, 32, 2, dtype=np.float32) / 32)).astype(np.float32)
    ang = np.concatenate([row[:, None] * freqs, col[:, None] * freqs], axis=-1).astype(np.float32)
    c, s = np.cos(ang).astype(np.float32), np.sin(ang).astype(np.float32)
    C = np.concatenate([c, c], axis=-1)
    S = np.concatenate([-s, s], axis=-1)
    C = np.ascontiguousarray(C.reshape(NT, 128, 64).transpose(1, 0, 2))
    S = np.ascontiguousarray(S.reshape(NT, 128, 64).transpose(1, 0, 2))
    return C, S


def make_in_maps(inp, cores):
    f = lambda a: np.ascontiguousarray(np.asarray(a, dtype=np.float32))
    x_prompt, x_sample = f(inp["x_prompt"]), f(inp["x_sample"])
    shared = {
        "w_mod": f(inp["w_mod"]), "w_in": f(inp["w_in"]), "w_out": f(inp["w_out"]),
        "w_gu": f(inp["w_gate_up"]), "w_down": f(inp["w_down"]),
        "bmodT": f(np.asarray(inp["b_mod"]).reshape(DEPTH, 48, 128).transpose(2, 0, 1)),
        "nrmT": f(np.stack([np.asarray(inp["norm_attn"]), np.asarray(inp["norm_ffn"])], 0)
                  .reshape(2, DEPTH, KC, 128).transpose(3, 0, 1, 2)),
        "gains": f(np.stack([np.asarray(inp["q_norm_a"]), np.asarray(inp["k_norm_a"]),
                             np.asarray(inp["q_norm_b"]), np.asarray(inp["k_norm_b"])], 1)),
        "gainsT": f(np.tile(np.stack([np.asarray(inp["q_norm_a"]), np.asarray(inp["k_norm_a"]),
                                      np.asarray(inp["q_norm_b"]), np.asarray(inp["k_norm_b"])], 1), (1, 1, 2))
                    .transpose(2, 0, 1)),
        "lamv": f(np.stack([np.asarray(inp["lambda_q1"]), np.asarray(inp["lambda_k1"]),
                            np.asarray(inp["lambda_q2"]), np.asarray(inp["lambda_k2"])], 1)),
        "subln": f(inp["subln"]),
        "identf": np.eye(128, dtype=np.float32),
    }
    maps = []
    for c in cores:
        b, hh = c // 2, c % 2
        perm = np.concatenate([np.arange(hh * 512, hh * 512 + 512), np.arange((1 - hh) * 512, (1 - hh) * 512 + 512)])
        C, S = _rope_tables(perm)
        cond = np.stack([np.asarray(inp["c_ctx"]), np.asarray(inp["c"])[b]], 0)
        m = dict(shared)
        m.update({
            "xp": f(x_prompt[4 * c:4 * c + 4].reshape(T, D)),
            "xs": f(x_sample[b][perm]),
            "cdk": f(np.asarray(inp["cache_diff_k"])[b].reshape(DEPTH, PAST, 512)),
            "cdv": f(np.asarray(inp["cache_diff_v"])[b].reshape(DEPTH, PAST, 512)),
            "cgk": f(np.asarray(inp["cache_gqa_k"])[b].reshape(DEPTH, PAST, 128)),
            "cgv": f(np.asarray(inp["cache_gqa_v"])[b].reshape(DEPTH, PAST, 128)),
            "condT": f(cond.reshape(2, KC, 128).transpose(2, 1, 0)),
            "ropeC": C, "ropeS": S,
        })
        maps.append(m)
    return maps


_NC_CACHE = {}


def kernel(**inputs):
    if "nc" not in _NC_CACHE:
        _NC_CACHE["nc"] = build_program()
    nc = _NC_CACHE["nc"]
    cores = list(range(NCORES))
    in_maps = make_in_maps(inputs, cores)
    res = run_bass_kernel_spmd(nc, in_maps, core_ids=cores)
    R = res.results
    yp = np.concatenate([np.asarray(R[c]["yp"]).reshape(4, 256, D) for c in cores], 0)
    ys = np.zeros((4, 1024, D), np.float32)
    for c in cores:
        b, hh = c // 2, c % 2
        ys[b, hh * 512:(hh + 1) * 512] = np.asarray(R[c]["ys"])
    ndk = np.concatenate([np.asarray(R[c]["ndk"]) for c in cores], 0).reshape(32, DEPTH, 256, 4, 2, 64)
    ndv = np.concatenate([np.asarray(R[c]["ndv"]) for c in cores], 0).reshape(32, DEPTH, 256, 4, 128)
    ngk = np.concatenate([np.asarray(R[c]["ngk"]) for c in cores], 0).reshape(32, DEPTH, 256, 2, 64)
    ngv = np.concatenate([np.asarray(R[c]["ngv"]) for c in cores], 0).reshape(32, DEPTH, 256, 2, 64)
    return (yp.astype(np.float32), ys, ndk.astype(np.float32), ndv.astype(np.float32),
            ngk.astype(np.float32), ngv.astype(np.float32))
```

```python
import math
from contextlib import ExitStack

import numpy as np
import concourse.bass as bass
import concourse.mybir as mybir
from concourse.bass_utils import run_bass_kernel_spmd

F32 = mybir.dt.float32
BF16 = mybir.dt.bfloat16
AF = mybir.ActivationFunctionType
ALU = mybir.AluOpType
AX = mybir.AxisListType

D = 1024
KC = 8
DEPTH = 2
HID = 2816
NJ = 22
INC = 2304
PAST = 512
EPS = 1e-6
NCORES = 8
T = 1024
MIX_DEFER = 5
SCHUNK = 3
OSUB = 12
PE_WARM = 0
NT = 8


class Buf:
    __slots__ = ("name", "last_w", "readers", "dsem", "dcount")

    def __init__(self, name):
        self.name = name
        self.last_w = None
        self.readers = []
        self.dsem = None
        self.dcount = 0


class Eng:
    def __init__(self, name, sem):
        self.name = name
        self.sem = sem
        self.count = 0
        self.known = {}
        self.ops = []


class Prog:
    def __init__(self, nc, stack):
        self.nc = nc
        self.stack = stack
        self.engs = {}
        for n in ("pe", "act", "dve", "pool", "sp"):
            self.engs[n] = Eng(n, self.new_sem("e_" + n))

    def new_sem(self, name):
        return self.stack.enter_context(self.nc.semaphore(name))

    def _deps(self, eng, reads, writes):
        need = {}

        def add(tok):
            if tok is None:
                return
            sem, val, en = tok
            if en == "pe" and eng.name == "pe":
                return
            k = id(sem)
            if k not in need or need[k][1] < val:
                need[k] = (sem, val)

        for b in reads:
            add(b.last_w)
        for b in writes:
            add(b.last_w)
            for r in b.readers:
                add(r)
        out = []
        for k, (sem, val) in need.items():
            if eng.known.get(k, 0) < val:
                eng.known[k] = val
                out.append((sem, val))
        return out

    @staticmethod
    def _mark(tok, reads, writes):
        for b in reads:
            b.readers.append(tok)
        for b in writes:
            b.last_w = tok
            b.readers = []

    def op(self, engname, fn, reads=(), writes=()):
        eng = self.engs[engname]
        waits = self._deps(eng, reads, writes)
        eng.count += 1
        sem = eng.sem

        def run(h, waits=waits, fn=fn, sem=sem):
            for s, v in waits:
                h.wait_ge(s, v)
            fn(h).then_inc(sem, 1)

        eng.ops.append(run)
        tok = (sem, eng.count, engname)
        self._mark(tok, reads, writes)
        return tok

    def dma(self, qname, fn, reads=(), writes=(), sembuf=None, nowait=False):
        eng = self.engs[qname]
        waits = [] if nowait else self._deps(eng, reads, writes)
        sb = sembuf if sembuf is not None else (writes[0] if writes else reads[0])
        if sb.dsem is None:
            sb.dsem = self.new_sem("d_" + sb.name)
        sb.dcount += 16
        sem, val = sb.dsem, sb.dcount

        def run(h, waits=waits, fn=fn, sem=sem):
            for s, v in waits:
                h.wait_ge(s, v)
            fn(h).then_inc(sem, 16)

        eng.ops.append(run)
        tok = (sem, val, "dma")
        self._mark(tok, reads, writes)
        return tok

    def final_wait(self, engname, bufs):
        eng = self.engs[engname]
        waits = self._deps(eng, [], bufs)

        def run(h, waits=waits):
            for s, v in waits:
                h.wait_ge(s, v)

        eng.ops.append(run)

    def emit(self):
        nc = self.nc
        E = self.engs
        with nc.Block() as block:
            @block.tensor
            def _(h):
                for f in E["pe"].ops:
                    f(h)

            @block.scalar
            def _(h):
                for f in E["act"].ops:
                    f(h)

            @block.vector
            def _(h):
                for f in E["dve"].ops:
                    f(h)

            @block.gpsimd
            def _(h):
                for f in E["pool"].ops:
                    f(h)

            @block.sync
            def _(h):
                for f in E["sp"].ops:
                    f(h)


class Ring:
    def __init__(self, items):
        self.items = items
        self.i = 0

    def next(self):
        it = self.items[self.i % len(self.items)]
        self.i += 1
        return it


def build_program(dbg=None):
    nc = bass.Bass("TRN2", target_bir_lowering=False)

    def din(name, shape, dt=F32):
        return nc.dram_tensor(name, list(shape), dt, kind="ExternalInput").ap()

    def dout(name, shape, dt=F32):
        return nc.dram_tensor(name, list(shape), dt, kind="ExternalOutput").ap()

    xin = {"P": din("xp", [T, D]), "S": din("xs", [T, D])}
    cdk = din("cdk", [DEPTH, PAST, 512])
    cdv = din("cdv", [DEPTH, PAST, 512])
    cgk = din("cgk", [DEPTH, PAST, 128])
    cgv = din("cgv", [DEPTH, PAST, 128])
    condT_d = din("condT", [128, KC, 2])
    w_mod = din("w_mod", [DEPTH, D, 6 * D])
    bmodT_d = din("bmodT", [128, DEPTH, 48])
    nrmT_d = din("nrmT", [128, 2, DEPTH, KC])
    w_in = din("w_in", [DEPTH, D, INC])
    gains_d = din("gains", [DEPTH, 4, 64])
    gainsT_d = din("gainsT", [128, DEPTH, 4])
    lamv_d = din("lamv", [DEPTH, 4, 64])
    subln_d = din("subln", [DEPTH, 128])
    w_out = din("w_out", [DEPTH, D, D])
    w_gu = din("w_gu", [DEPTH, D, 2 * HID])
    w_down = din("w_down", [DEPTH, HID, D])
    ropeC_d = din("ropeC", [128, NT, 64])
    ropeS_d = din("ropeS", [128, NT, 64])
    ident_d = din("identf", [128, 128])

    yout = {"P": dout("yp", [T, D]), "S": dout("ys", [T // 2, D])}
    ndk = dout("ndk", [4, DEPTH, 256, 512])
    ndv = dout("ndv", [4, DEPTH, 256, 512])
    ngk = dout("ngk", [4, DEPTH, 256, 128])
    ngv = dout("ngv", [4, DEPTH, 256, 128])
    dbg_out = {}
    if dbg:
        for name, (shape, dt_) in dbg.items():
            dbg_out[name] = dout("dbg_" + name, shape, dt_)

    st = ExitStack()
    with st:
        P = Prog(nc, st)
        nc._marks = []

        def mark(label):
            nc._marks.append((label, {n: len(e.ops) for n, e in P.engs.items()}))

        def sb(name, shape, dt=F32):
            return st.enter_context(nc.sbuf_tensor(name, list(shape), dt))

        xT = sb("xT", [128, KC, T], F32)
        hT = sb("hT", [128, KC, T], BF16)
        ARENA_N = 37504
        arena = sb("arena", [128, ARENA_N], BF16)
        QT = arena[:, 0:8192].rearrange("p (a b) -> p a b", a=8)
        KT = arena[:, 8192:17408].rearrange("p (a b) -> p a b", a=6)
        VA = arena[:, 17408:25160].rearrange("p (a b) -> p a b", a=12)
        PT = arena[:, 25160:37448].rearrange("p (s a b) -> p s a b", s=2, a=12)
        actT = arena[:, 0:22528].rearrange("p (a b) -> p a b", a=NJ)
        arena_f = arena[:, 0:16384].bitcast(F32)
        wm_slots = [arena_f[:, i * 4096:(i + 1) * 4096].rearrange("p (a b) -> p a b", a=8)
                    for i in range(2)]
        NW = 3
        wslots = [sb(f"wslot{i}", [128, 4096], BF16) for i in range(NW)]
        xq = sb("xq", [128, 2 * D], F32)
        xstage = [xq[:, i * D:(i + 1) * D] for i in range(2)]
        Qbd = xq[:, :].bitcast(BF16).rearrange("p (a b) -> p a b", a=8)
        identf = sb("identf_s", [128, 128], F32)
        identb = sb("identb", [128, 128], BF16)
        onesb = sb("onesb", [128, 128], BF16)
        ropeC = sb("ropeC_s", [128, NT, 64], F32)
        ropeS = sb("ropeS_s", [128, NT, 64], F32)
        gains = sb("gains_s", [128, DEPTH, 4, 64], F32)
        gainsT = sb("gainsT_s", [128, DEPTH, 4], F32)
        TCg = sb("TCg", [128, NT, 64], F32)
        TSg = sb("TSg", [128, NT, 64], F32)
        sublnb = sb("subln_s", [128, DEPTH, 128], F32)
        condT = sb("condT_s", [128, KC, 2], F32)
        scT = sb("scT", [128, KC, 2], F32)
        scTb = sb("scTb", [128, KC, 2], BF16)
        bmodT = sb("bmodT_s", [128, DEPTH, 48], F32)
        nrmT = sb("nrmT_s", [128, 2, DEPTH, KC], F32)
        MS = sb("MS", [128, DEPTH, 6, KC, 2], F32)
        lamt = sb("lamt", [128, DEPTH, 8], F32)
        epst = sb("epst", [128, 1], F32)
        sqmix = sb("sqmix", [128, 4096], BF16)
        sqb = sqmix[:, :].rearrange("p (a b) -> p a b", a=KC)
        NQ = 4
        qA = [sb(f"qA{i}", [128, 512], F32) for i in range(NQ)]
        qB = [sb(f"qB{i}", [128, 512], F32) for i in range(NQ)]
        qG = [sb(f"qG{i}", [128, 512], F32) for i in range(NQ)]
        qst = [sb(f"qst{i}", [128, 512], BF16) for i in range(NQ)]
        lamv = qB[1][:, 0:512].rearrange("p (l g d) -> p l g d", l=DEPTH, g=4)
        ntmp = qA[:3]
        rstdn = qB[:2]
        lnv = qG[0]
        sgb = qG[1:3]
        qss = [sb(f"qss{i}", [128, 24], F32) for i in range(NQ)]
        vF = [sb(f"vF{i}", [128, 512], F32) for i in range(2)]
        arec = [sb(f"arec{i}", [128, 16], F32) for i in range(6)]
        _atv = [qA[j][:, c * 128:(c + 1) * 128] for j in range(3) for c in range(4)]
        at_t = _atv[0:6]
        at_u = _atv[6:12]
        mixtok = [sqmix[:, i * D:(i + 1) * D] for i in range(4)]
        cst = [arena[:, 25160 + i * 768:25160 + (i + 1) * 768] for i in range(4)]

        psum = [st.enter_context(nc.psum_tensor(f"ps{i}", [128, 512], F32)) for i in range(8)]
        psb = [Buf(f"ps{i}") for i in range(8)]

        def bank_ring(ids):
            return Ring([(psum[i], psb[i]) for i in ids])

        xTb = [[Buf(f"xT{k}_{c}") for c in range(2)] for k in range(KC)]
        hTb = [[Buf(f"hT{k}_{c}") for c in range(2)] for k in range(KC)]
        QTb = [[Buf(f"QT{g}_{t}") for t in range(NT)] for g in range(2)]
        KTb = [[Buf(f"KT{g}_{t}") for t in range(12)] for g in range(2)]
        VAa = [Buf(f"VAa{t}") for t in range(12)]
        VAb = [Buf(f"VAb{t}") for t in range(12)]
        PTb = [Buf(f"PT{i}") for i in range(2)]
        actb = [[Buf(f"act{j}_{c}") for c in range(2)] for j in range(NJ)]
        wmb = [Buf(f"wm{i}") for i in range(2)]
        attn_bufs = [b for row in QTb for b in row] + [b for row in KTb for b in row] + VAa + VAb + PTb
        ffn_bufs = [b for row in actb for b in row] + wmb
        wsb = [Buf(f"ws{i}") for i in range(NW)]
        wsb2 = [Buf(f"wsu{i}") for i in range(NW)]
        xsb = [Buf(f"xs{i}") for i in range(2)]
        cbuf = Buf("consts")
        tabb = Buf("ropetab")
        msb = Buf("MS")
        sqbb = Buf("sqb")
        sqbb2 = Buf("sqb2")
        qAb = [Buf(f"qA{i}") for i in range(NQ)]
        qBb = [Buf(f"qB{i}") for i in range(NQ)]
        qGb = [Buf(f"qG{i}") for i in range(NQ)]
        ntmpb = qAb[:3]
        rstdnb = qBb[:2]
        lnvb = qGb[0]
        sgbb = qGb[1:3]
        qstb = [Buf(f"qst{i}") for i in range(NQ)]
        qssb = [Buf(f"qss{i}") for i in range(NQ)]
        vFb = [Buf(f"vF{i}") for i in range(2)]
        cstb = [(Buf(f"cstk{i}"), Buf(f"cstg{i}")) for i in range(4)]
        arecb = [Buf(f"arec{i}") for i in range(6)]
        attb = [Buf(f"att{i}") for i in range(6)]
        atub = [Buf(f"atu{i}") for i in range(6)]
        mixb = [Buf(f"mix{i}") for i in range(4)]
        outb = Buf("dram_out")
        out_bufs = [outb]

        def w_pieces(g, l):
            ps_ = []
            for nb in ("ka", "va", "kbvb", "qa", "qb"):
                ps_.append(("in", l, nb))
            for i in range(2):
                ps_.append(("out", l, i))
            for i in range(11):
                ps_.append(("gu", l, i))
            for m in range(KC):
                ps_.append(("down", l, m))
            return ps_

        IN_COLS = {"qa": (0, 512), "ka": (512, 512), "va": (1024, 512), "qb": (1536, 512),
                   "kbvb": (2048, 256)}
        group_order = ("P", "S")
        pieces = []
        for g in group_order:
            for l in range(DEPTH):
                wp = w_pieces(g, l)
                if g == group_order[0] and l == 0:
                    wp2 = [("mod", 0, pc) for pc in range(4)] + wp[:5] + \
                          [("mod", 0, pc) for pc in range(4, 12)] + wp[5:7]
                    for i in range(11):
                        wp2.append(wp[7 + i])
                        wp2.append(("mod", 1, i))
                    wp2.append(("mod", 1, 11))
                    wp = wp2 + wp[18:]
                pieces += wp
        wstate = {"issued": 0, "cur": 0}

        def issue_piece(i):
            kind, l, idx = pieces[i]
            slot = wslots[i % NW]
            b = wsb[i % NW]
            b2 = wsb2[i % NW]
            if kind == "mod":
                dst = slot[:, 0:4096].rearrange("p (a b) -> p a b", a=8)
                src = w_mod[l].rearrange("(k p) n -> p k n", p=128)[:, :, idx * 512:(idx + 1) * 512]
                P.dma("pool", lambda h, d=dst, s=src: h.dma_start(out=d, in_=s), writes=[b, b2])
            elif kind == "in":
                c0, n = IN_COLS[idx]
                dst = slot[:, 0:8 * n].rearrange("p (a b) -> p a b", a=8)
                src = w_in[l].rearrange("(k p) n -> p k n", p=128)[:, :, c0:c0 + n]
                P.dma("pool", lambda h, d=dst, s=src: h.dma_start(out=d, in_=s), writes=[b, b2])
            elif kind == "out":
                dst = slot[:, 0:4096].rearrange("p (a b) -> p a b", a=8)
                src = w_out[l].rearrange("(k p) n -> p k n", p=128)[:, :, idx * 512:(idx + 1) * 512]
                P.dma("pool", lambda h, d=dst, s=src: h.dma_start(out=d, in_=s), writes=[b, b2])
            elif kind == "gu":
                wv = w_gu[l].rearrange("(k p) n -> p k n", p=128)
                dg = slot[:, 0:2048].rearrange("p (a b) -> p a b", a=8)
                du = slot[:, 2048:4096].rearrange("p (a b) -> p a b", a=8)
                sg_ = wv[:, :, idx * 256:(idx + 1) * 256]
                su_ = wv[:, :, HID + idx * 256:HID + (idx + 1) * 256]
                P.dma("pool", lambda h, d=dg, s=sg_: h.dma_start(out=d, in_=s), writes=[b])
                P.dma("pool", lambda h, d=du, s=su_: h.dma_start(out=d, in_=s), writes=[b2])
            else:
                dst = slot[:, 0:NJ * 128].rearrange("p (a b) -> p a b", a=NJ)
                src = w_down[l].rearrange("(j p) n -> p j n", p=128)[:, :, idx * 128:(idx + 1) * 128]
                P.dma("pool", lambda h, d=dst, s=src: h.dma_start(out=d, in_=s), writes=[b, b2])

        def w_acquire(expect_kind, ahead=0):
            i = wstate["cur"] + ahead
            assert pieces[i][0] == expect_kind, (pieces[i], expect_kind)
            while wstate["issued"] <= i:
                issue_piece(wstate["issued"])
                wstate["issued"] += 1
            wstate["b2"] = wsb2[i % NW]
            return wslots[i % NW], wsb[i % NW]

        def w_release():
            i = wstate["cur"]
            wstate["cur"] += 1
            nxt = i + NW
            if nxt < len(pieces) and wstate["issued"] <= nxt:
                while wstate["issued"] <= nxt:
                    issue_piece(wstate["issued"])
                    wstate["issued"] += 1

        def ld(dst, src):
            P.dma("sp", lambda h, d=dst, s=src: h.dma_start(out=d, in_=s), writes=[cbuf], nowait=True)

        ld(identf[:], ident_d)
        ld(condT[:], condT_d)
        ld(bmodT[:], bmodT_d)
        ld(nrmT[:], nrmT_d)
        ld(ropeC[:], ropeC_d)
        ld(ropeS[:], ropeS_d)
        ld(gains[:], gains_d.partition_broadcast(128))
        ld(gainsT[:], gainsT_d)
        P.dma("sp", lambda h: h.dma_start(out=lamv, in_=lamv_d.partition_broadcast(128)), writes=[qBb[1]])
        ld(sublnb[:], subln_d.partition_broadcast(128))
        cdone = Buf("cdone")
        P.op("dve", lambda h: h.tensor_copy(out=identb[:], in_=identf[:]), reads=[cbuf], writes=[cdone])
        P.op("dve", lambda h: h.memset(onesb[:], 1.0), writes=[cdone])
        P.op("dve", lambda h: h.memset(epst[:], EPS), writes=[cdone])
        P.op("act", lambda h: h.activation(out=scTb[:], in_=condT[:], func=AF.Silu),
             reads=[cbuf], writes=[msb])
        for l in range(DEPTH):
            lam_init = 0.8 - 0.6 * math.exp(-0.3 * l)
            lt = lamt[:, l, :]
            P.op("dve", lambda h, l=l: h.tensor_tensor(out=qA[0][:, 0:64], in0=lamv[:, l, 0, :],
                                                       in1=lamv[:, l, 1, :], op=ALU.mult),
                 reads=[cbuf, qBb[1]], writes=[qAb[0]])
            P.op("dve", lambda h, l=l: h.tensor_tensor(out=qA[0][:, 64:128], in0=lamv[:, l, 2, :],
                                                       in1=lamv[:, l, 3, :], op=ALU.mult),
                 reads=[cbuf, qBb[1]], writes=[qAb[0]])
            P.op("dve", lambda h, lt=lt: h.tensor_reduce(
                out=lt[:, 0:2], in_=qA[0][:, 0:128].rearrange("p (a b) -> p a b", a=2),
                axis=AX.X, op=ALU.add), reads=[qAb[0]], writes=[cdone])
            P.op("act", lambda h, lt=lt: h.activation(out=lt[:, 2:4], in_=lt[:, 0:2], func=AF.Exp),
                 reads=[cdone], writes=[cdone])
            P.op("dve", lambda h, lt=lt: h.tensor_tensor(out=lt[:, 4:5], in0=lt[:, 3:4], in1=lt[:, 2:3],
                                                         op=ALU.subtract), reads=[cdone], writes=[cdone])
            P.op("dve", lambda h, lt=lt, li=lam_init: h.tensor_scalar(
                out=lt[:, 5:6], in0=lt[:, 4:5], scalar1=-li, scalar2=None, op0=ALU.add),
                reads=[cdone], writes=[cdone])
            P.op("dve", lambda h, l=l, li=lam_init: h.tensor_scalar(
                out=sublnb[:, l, :], in0=sublnb[:, l, :], scalar1=1.0 - li, scalar2=None, op0=ALU.mult),
                reads=[cbuf, cdone], writes=[cdone])


        def mod_piece(l, pc, ring):
            mps, mpb = ring.next()
            wslot, wb = w_acquire("mod")
            wv = wslot[:, 0:4096].rearrange("p (a b) -> p a b", a=8)
            for m4 in range(4):
                for k in range(KC):
                    P.op("pe", lambda h, mps=mps, wv=wv, m4=m4, k=k: h.matmul(
                        mps[:, m4 * 2:m4 * 2 + 2], lhsT=wv[:, k, m4 * 128:(m4 + 1) * 128],
                        rhs=scTb[:, k, :], start=(k == 0), stop=(k == KC - 1)),
                        reads=[wb, msb], writes=[mpb])
            w_release()
            msl = MS[:, l, :, :, :].rearrange("p s k n -> p (s k) n")[:, pc * 4:(pc + 1) * 4, :]
            P.op("dve", lambda h, mps=mps, msl=msl, l=l, pc=pc: h.tensor_tensor(
                out=msl, in0=mps[:, 0:8].rearrange("p (a n) -> p a n", n=2),
                in1=bmodT[:, l, pc * 4:(pc + 1) * 4].unsqueeze(2).broadcast_to([128, 4, 2]), op=ALU.add),
                reads=[mpb, cbuf, msb], writes=[msb])
            fold = {3: (1, 0), 9: (4, 1)}.get(pc)
            if fold is not None:
                s_idx, kind = fold
                P.op("dve", lambda h, l=l, s_idx=s_idx, kind=kind: h.scalar_tensor_tensor(
                    out=MS[:, l, s_idx, :, :], in0=MS[:, l, s_idx, :, :], scalar=1.0,
                    in1=nrmT[:, kind, l, :].unsqueeze(2).broadcast_to([128, KC, 2]),
                    op0=ALU.add, op1=ALU.mult), reads=[msb, cbuf], writes=[msb])

        tr_ring = bank_ring([6, 7])

        ldstage = [sqmix[:, :].bitcast(F32)[:, i * D:(i + 1) * D] for i in range(2)]
        ldb = [Buf(f"ldst{i}") for i in range(2)]
        xring = bank_ring([2, 3, 4, 5])

        def load_x(g, tiles=range(NT), alt=False):
            ring = xring
            for t in tiles:
                if alt:
                    xs_, xb_ = ldstage[t % 2], ldb[t % 2]
                else:
                    xs_, xb_ = xstage[t % 2], xsb[t % 2]
                src = xin[g][t * 128:(t + 1) * 128, :]
                P.dma("pool" if alt else "sp", lambda h, d=xs_, s=src: h.dma_start(out=d, in_=s), writes=[xb_])
                for half in range(2):
                    ps_, pb_ = ring.next()
                    for kk in range(4):
                        k = half * 4 + kk
                        P.op("pe", lambda h, ps_=ps_, xs_=xs_, k=k, kk=kk: h.transpose(
                            out=ps_[:, kk * 128:(kk + 1) * 128], in_=xs_[:, k * 128:(k + 1) * 128],
                            identity=identf[:]), reads=[xb_, cbuf], writes=[pb_])
                    dst = xT[:, half * 4:half * 4 + 4, t * 128:(t + 1) * 128]
                    srcp = ps_[:, :].rearrange("p (a b) -> p a b", a=4)
                    eng = "act" if half == 0 else "dve"
                    wr = [xTb[half * 4 + kk][t // 4] for kk in range(4)]
                    if eng == "act":
                        P.op("act", lambda h, d=dst, s=srcp: h.activation(out=d, in_=s, func=AF.Copy),
                             reads=[pb_], writes=wr)
                    else:
                        P.op("dve", lambda h, d=dst, s=srcp: h.tensor_copy(out=d, in_=s),
                             reads=[pb_], writes=wr)

        def store_y(g, tiles):
            ring = xring
            for t in tiles:
                xs_, xb_ = xstage[t % 2], xsb[t % 2]
                for half in range(2):
                    ps_, pb_ = ring.next()
                    for kk in range(4):
                        k = half * 4 + kk
                        P.op("pe", lambda h, ps_=ps_, k=k, kk=kk, t=t: h.transpose(
                            out=ps_[:, kk * 128:(kk + 1) * 128], in_=xT[:, k, t * 128:(t + 1) * 128],
                            identity=identf[:]), reads=[xTb[k][t // 4], cbuf], writes=[pb_])
                    dst = xs_[:, half * 512:(half + 1) * 512]
                    if half == 0:
                        P.op("act", lambda h, d=dst, s=ps_: h.activation(out=d, in_=s[:, :], func=AF.Copy),
                             reads=[pb_], writes=[xb_])
                    else:
                        P.op("dve", lambda h, d=dst, s=ps_: h.tensor_copy(out=d, in_=s[:, :]),
                             reads=[pb_], writes=[xb_])
                dsty = yout[g][t * 128:(t + 1) * 128, :]
                P.dma("sp", lambda h, d=dsty, s=xs_: h.dma_start(out=d, in_=s),
                      reads=[xb_], sembuf=xb_)

        def norm_mod(l, n, s_scale, s_shift, ncb):
            ring = bank_ring([0, 1])
            for cb in range(ncb):
                cs = slice(cb * 512, (cb + 1) * 512)
                P.op("act", lambda h, cs=cs: h.activation(out=sqb[:, 0:5, :], in_=xT[:, 0:5, cs], func=AF.Square),
                     reads=[xTb[k][cb] for k in range(5)], writes=[sqbb] + mixb)
                P.op("dve", lambda h, cs=cs: h.tensor_tensor(out=sqb[:, 5:8, :], in0=xT[:, 5:8, cs],
                                                             in1=xT[:, 5:8, cs], op=ALU.mult),
                     reads=[xTb[k][cb] for k in range(5, 8)], writes=[sqbb2] + mixb)
                ps_, pb_ = ring.next()
                for k in range(KC):
                    P.op("pe", lambda h, ps_=ps_, k=k: h.matmul(ps_[:, :], lhsT=onesb[:], rhs=sqb[:, k, :],
                                                                start=(k == 0), stop=(k == KC - 1)),
                         reads=[sqbb, sqbb2, cdone] + mixb, writes=[pb_])
                rs, rsb = rstdn[cb % 2], rstdnb[cb % 2]
                P.op("act", lambda h, ps_=ps_: h.activation(out=lnv[:], in_=ps_[:, :], func=AF.Ln,
                                                            bias=epst[:, 0:1], scale=1.0 / D),
                     reads=[pb_, cdone], writes=[lnvb])
                P.op("act", lambda h, rs=rs: h.activation(out=rs[:], in_=lnv[:], func=AF.Exp, scale=-0.5),
                     reads=[lnvb], writes=[rsb])
                for k in range(KC):
                    tm, tmb = ntmp[k % 3], ntmpb[k % 3]
                    P.op("dve", lambda h, tm=tm, k=k, cs=cs, rs=rs: h.tensor_tensor(
                        out=tm[:], in0=xT[:, k, cs], in1=rs[:], op=ALU.mult),
                        reads=[xTb[k][cb], rsb], writes=[tmb])
                    if k % 8 in (0, 2, 4, 6, 7):
                        P.op("act", lambda h, tm=tm, k=k, cs=cs: h.activation(
                            out=hT[:, k, cs], in_=tm[:], func=AF.Identity,
                            bias=MS[:, l, s_shift, k, n:n + 1], scale=MS[:, l, s_scale, k, n:n + 1]),
                            reads=[tmb, msb], writes=[hTb[k][cb]])
                    else:
                        P.op("pool", lambda h, tm=tm, k=k, cs=cs: h.tensor_scalar(
                            out=hT[:, k, cs], in0=tm[:], scalar1=MS[:, l, s_scale, k, n:n + 1],
                            scalar2=MS[:, l, s_shift, k, n:n + 1], op0=ALU.mult, op1=ALU.add),
                            reads=[tmb, msb], writes=[hTb[k][cb]])

        def dbg_dump(name, src_ap, reads):
            if dbg and name in dbg_out:
                db = Buf("dbg_" + name)
                out_bufs.append(db)
                P.dma("sp", lambda h: h.dma_start(out=dbg_out[name], in_=src_ap), reads=reads, writes=[db])

        cst_all = [b for pr in cstb for b in pr]

        def prefetch_cache(l):
            for ct in range(4):
                kt = 8 + ct
                cs_, (cbk_, cb_) = cst[ct], cstb[ct]
                rs = slice(ct * 128, (ct + 1) * 128)
                P.dma("pool", lambda h, cs_=cs_, rs=rs: h.dma_start(out=cs_[:, 0:512], in_=cdk[l, rs, :]),
                      writes=[cbk_] + PTb)
                P.dma("pool", lambda h, cs_=cs_, rs=rs: h.dma_start(out=cs_[:, 512:640], in_=cgk[l, rs, :]),
                      writes=[cb_] + PTb)
                vdst = VA[:, kt, 0:516].rearrange("p (h e) -> p h e", h=4)[:, :, 0:128]
                vbdst = VA[:, kt, 516:646].rearrange("p (h e) -> p h e", h=2)[:, :, 0:64]
                P.dma("pool", lambda h, vdst=vdst, rs=rs: h.dma_start(
                    out=vdst, in_=cdv[l, rs, :].rearrange("p (h e) -> p h e", h=4)),
                    writes=[VAa[kt]], sembuf=VAa[kt])
                P.dma("pool", lambda h, vbdst=vbdst, rs=rs: h.dma_start(
                    out=vbdst, in_=cgv[l, rs, :].rearrange("p (h e) -> p h e", h=2)),
                    writes=[VAb[kt]], sembuf=VAb[kt])

        def proj_step(g, l, tq_tiles):
            rope = (g == "S")
            proj_ring = bank_ring([2, 3, 4, 5, 0, 1])
            nkt = 12 if g == "S" else 8
            va4 = VA[:, 0:nkt, 0:516].rearrange("p t (h e) -> p t h e", h=4)[:, :, :, 128:129]
            vb2 = VA[:, 0:nkt, 516:646].rearrange("p t (h e) -> p t h e", h=2)[:, :, :, 64:65]
            P.op("pool", lambda h: h.memset(va4, 1.0), writes=VAa[:nkt] + ffn_bufs)
            P.op("pool", lambda h: h.memset(vb2, 1.0), writes=VAb[:nkt] + ffn_bufs)

            pending = []
            ucount = [0]

            def flush(keep):
                while len(pending) > keep:
                    pending.pop(0)()

            def build_rope_tables(gi):
                g1 = gains[:, l, gi, 0:32].unsqueeze(1).broadcast_to([128, NT, 32])
                g2 = gains[:, l, gi, 32:64].unsqueeze(1).broadcast_to([128, NT, 32])
                gf = gains[:, l, gi, :].unsqueeze(1).broadcast_to([128, NT, 64])
                P.op("dve", lambda h: h.tensor_tensor(out=TCg[:], in0=ropeC[:], in1=gf, op=ALU.mult),
                     reads=[cbuf], writes=[tabb])
                P.op("dve", lambda h: h.tensor_tensor(out=TSg[:, :, 0:32], in0=ropeS[:, :, 0:32], in1=g2,
                                                      op=ALU.mult), reads=[cbuf], writes=[tabb])
                P.op("dve", lambda h: h.tensor_tensor(out=TSg[:, :, 32:64], in0=ropeS[:, :, 32:64], in1=g1,
                                                      op=ALU.mult), reads=[cbuf], writes=[tabb])

            def qk_chain(ps_, pb_, nh, gi, t, slot, is_k, kout):
                w = nh * 64
                A, B, G, S_, SS = qA[slot], qB[slot], qG[slot], qst[slot], qss[slot]
                Ab, Bb, Gb, Sb, SSb = qAb[slot], qBb[slot], qGb[slot], qstb[slot], qssb[slot]
                v3 = lambda ap: ap[:, 0:w].rearrange("p (a b) -> p a b", a=nh)
                if rope:
                    P.op("act", lambda h: h.activation(out=B[:, 0:w], in_=ps_[:, 0:w], func=AF.Copy),
                         reads=[pb_], writes=[Bb])
                P.op("act", lambda h: h.activation(out=A[:, 0:w], in_=ps_[:, 0:w], func=AF.Square),
                     reads=[pb_], writes=[Ab])
                P.op("dve", lambda h: h.tensor_reduce(out=SS[:, 0:nh], in_=v3(A), axis=AX.X, op=ALU.add),
                     reads=[Ab], writes=[SSb])
                P.op("act", lambda h: h.activation(out=SS[:, 8:8 + nh], in_=SS[:, 0:nh], func=AF.Ln,
                                                   bias=epst[:, 0:1], scale=1.0 / 64),
                     reads=[SSb, cdone], writes=[SSb])
                P.op("act", lambda h: h.activation(out=SS[:, 16:16 + nh], in_=SS[:, 8:8 + nh], func=AF.Exp,
                                                   scale=-0.5), reads=[SSb], writes=[SSb])
                rstd_bc = SS[:, 16:16 + nh].unsqueeze(2).broadcast_to([128, nh, 64])
                if not rope:
                    if is_k:
                        def stage2():
                            P.op("dve", lambda h: h.tensor_tensor(out=v3(B), in0=v3(ps_), in1=rstd_bc, op=ALU.mult),
                                 reads=[pb_, SSb], writes=[Bb])
                            P.op("act", lambda h: h.activation(out=S_[:, 0:w], in_=B[:, 0:w], func=AF.Copy),
                                 reads=[Bb], writes=[Sb])
                            gain_bc = gains[:, l, gi, :].unsqueeze(1).broadcast_to([128, nh, 64])
                            P.op("pool", lambda h: h.tensor_tensor(out=v3(G), in0=v3(B), in1=gain_bc, op=ALU.mult),
                                 reads=[Bb, cbuf], writes=[Gb])
                            P.dma("sp", lambda h: h.dma_start(out=kout, in_=G[:, 0:w]), reads=[Gb], sembuf=Gb)
                    else:
                        def stage2():
                            P.op("dve", lambda h: h.tensor_tensor(out=v3(S_), in0=v3(ps_), in1=rstd_bc, op=ALU.mult),
                                 reads=[pb_, SSb], writes=[Sb])
                else:
                    A3, B3, G3 = v3(A), v3(B), v3(G)
                    cC = TCg[:, t, :].unsqueeze(1).broadcast_to([128, nh, 64])
                    s1 = TSg[:, t, 0:32].unsqueeze(1).broadcast_to([128, nh, 32])
                    s2 = TSg[:, t, 32:64].unsqueeze(1).broadcast_to([128, nh, 32])
                    P.op("dve", lambda h: h.tensor_tensor(out=G3, in0=B3, in1=cC, op=ALU.mult),
                         reads=[Bb, tabb], writes=[Gb])
                    P.op("pool", lambda h: h.tensor_tensor(out=A3[:, :, 0:32], in0=B3[:, :, 32:64], in1=s1,
                                                           op=ALU.mult), reads=[Bb, tabb], writes=[Ab])
                    P.op("pool", lambda h: h.tensor_tensor(out=A3[:, :, 32:64], in0=B3[:, :, 0:32], in1=s2,
                                                           op=ALU.mult), reads=[Bb, tabb], writes=[Ab])

                    def stage2():
                        P.op("dve", lambda h: h.tensor_tensor(out=G3, in0=G3, in1=A3, op=ALU.add),
                             reads=[Gb, Ab], writes=[Gb])
                        P.op("dve", lambda h: h.tensor_tensor(out=v3(S_), in0=G3, in1=rstd_bc, op=ALU.mult),
                             reads=[Gb, SSb], writes=[Sb])
                return stage2

            q2 = []

            def defer(stage2_fn, tr_fn):
                q2.append((stage2_fn, tr_fn))
                while len(q2) > 1:
                    s2_, tr_ = q2.pop(0)
                    s2_()
                    pending.append(tr_)

            def drain_q2():
                while q2:
                    s2_, tr_ = q2.pop(0)
                    s2_()
                    pending.append(tr_)

            def transposes(src, srcb, nchunk, dst_fn, dst_bufs, gi=None):
                def run():
                    ps_, pb_ = tr_ring.next()
                    pv = ps_[:, :].bitcast(BF16)
                    for c in range(nchunk):
                        P.op("pe", lambda h, c=c: h.transpose(out=pv[:, c * 128:(c + 1) * 128],
                                                              in_=src[:, c * 128:(c + 1) * 128],
                                                              identity=identb[:]),
                             reads=srcb + [cdone], writes=[pb_])
                    dst = dst_fn()
                    srcv = pv[:, 0:nchunk * 128].rearrange("p (a b) -> p a b", a=nchunk)
                    ucount[0] += 1
                    if rope:
                        ucount[0] = 0
                    if gi is not None:
                        gsc = gainsT[:, l, gi:gi + 1]
                        if ucount[0] % 2 == 0:
                            P.op("act", lambda h: h.activation(out=dst, in_=srcv, func=AF.Copy, scale=gsc),
                                 reads=[pb_, cbuf], writes=dst_bufs + ffn_bufs)
                        else:
                            P.op("dve", lambda h: h.tensor_scalar(out=dst, in0=srcv, scalar1=gsc, scalar2=None,
                                                                  op0=ALU.mult),
                                 reads=[pb_, cbuf], writes=dst_bufs + ffn_bufs)
                    elif ucount[0] % 2 == 0:
                        P.op("act", lambda h: h.activation(out=dst, in_=srcv, func=AF.Copy),
                             reads=[pb_], writes=dst_bufs + ffn_bufs)
                    else:
                        P.op("dve", lambda h: h.tensor_copy(out=dst, in_=srcv),
                             reads=[pb_], writes=dst_bufs + ffn_bufs)
                return run

            slot_ctr = [0]
            vslot_ctr = [0]
            r8, rq = list(range(NT)), list(range(tq_tiles))
            grp_specs = [((("ka", r8), ("va", r8[0:4])), 1),
                         ((("va", r8[4:8]), ("kbvb", r8)), 2),
                         ((("qa", rq),), 1), ((("qb", rq),), 1)]
            for (grp, nrel) in grp_specs:
                views = {}
                for ai, (nb_, _tl) in enumerate(grp):
                    if rope and nb_ != "va":
                        build_rope_tables({"qa": 0, "ka": 1, "qb": 2, "kbvb": 3}[nb_])
                    assert pieces[wstate["cur"] + ai][2] == nb_
                    wslot_, wb_ = w_acquire("in", ahead=ai)
                    nc_ = IN_COLS[nb_][1]
                    views[nb_] = (wslot_[:, 0:8 * nc_].rearrange("p (a b) -> p a b", a=8), wb_, nc_)
                chain_u = [(nb_, t_) for (nb_, tl) in grp if nb_ != "va" for t_ in tl]
                free_u = [(nb_, t_) for (nb_, tl) in grp if nb_ == "va" for t_ in tl]
                step = max(1, len(chain_u) // max(1, len(free_u)))
                unit_list = []
                for ci, u_ in enumerate(chain_u):
                    unit_list.append(u_)
                    if free_u and (ci + 1) % step == 0:
                        unit_list.append(free_u.pop(0))
                unit_list += free_u
                for (nb, t) in unit_list:
                    wv, wb, ncols = views[nb]
                    ps_, pb_ = proj_ring.next()
                    for k in range(KC):
                        P.op("pe", lambda h, ps_=ps_, k=k, t=t, wv=wv, ncols=ncols: h.matmul(
                            ps_[:, 0:ncols], lhsT=hT[:, k, t * 128:(t + 1) * 128], rhs=wv[:, k, :],
                            start=(k == 0), stop=(k == KC - 1)),
                            reads=[hTb[k][t // 4], wb], writes=[pb_])
                    seq, r0 = t // 2, (t % 2) * 128
                    if nb in ("qa", "qb", "ka"):
                        slot = slot_ctr[0] % NQ
                        slot_ctr[0] += 1
                        gi = {"qa": 0, "ka": 1, "qb": 2}[nb]
                        kout = ndk[seq, l, r0:r0 + 128, :] if nb == "ka" else None
                        st2 = qk_chain(ps_, pb_, 8, gi, t, slot, nb == "ka" and g == "P", kout)
                        if nb == "ka":
                            dst_fn = (lambda t=t: KT[:, 0:4, t * 128:(t + 1) * 128])
                            dbs = [KTb[0][t]]
                        else:
                            c0 = 0 if nb == "qa" else 4
                            dst_fn = (lambda t=t, c0=c0: QT[:, c0:c0 + 4, t * 128:(t + 1) * 128])
                            dbs = [QTb[0 if nb == "qa" else 1][t]]
                        defer(st2, transposes(qst[slot], [qstb[slot]], 4, dst_fn, dbs, gi=(None if rope else gi)))
                    elif nb == "va":
                        vdst = VA[:, t, 0:516].rearrange("p (h e) -> p h e", h=4)[:, :, 0:128]
                        if g == "P":
                            vs = vslot_ctr[0] % 2
                            vslot_ctr[0] += 1
                            P.op("act", lambda h, vs=vs, ps_=ps_: h.activation(out=vF[vs][:], in_=ps_[:, :],
                                                                              func=AF.Copy),
                                 reads=[pb_], writes=[vFb[vs]])
                            vo = ndv[seq, l, r0:r0 + 128, :]
                            P.dma("sp", lambda h, vs=vs, vo=vo: h.dma_start(out=vo, in_=vF[vs][:]),
                                  reads=[vFb[vs]], sembuf=vFb[vs])
                            P.op("pool", lambda h, vs=vs, vdst=vdst: h.tensor_copy(
                                out=vdst, in_=vF[vs][:].rearrange("p (h e) -> p h e", h=4)),
                                reads=[vFb[vs]], writes=[VAa[t]] + ffn_bufs)
                        else:
                            P.op("act", lambda h, ps_=ps_, vdst=vdst: h.activation(
                                out=vdst, in_=ps_[:, :].rearrange("p (h e) -> p h e", h=4), func=AF.Copy),
                                reads=[pb_], writes=[VAa[t]] + ffn_bufs)
                    else:
                        slot = slot_ctr[0] % NQ
                        slot_ctr[0] += 1
                        kout = ngk[seq, l, r0:r0 + 128, :]
                        vbdst = VA[:, t, 516:646].rearrange("p (h e) -> p h e", h=2)[:, :, 0:64]
                        if g == "P":
                            vs = vslot_ctr[0] % 2
                            vslot_ctr[0] += 1
                            P.op("act", lambda h, vs=vs, ps_=ps_: h.activation(
                                out=vF[vs][:, 0:128], in_=ps_[:, 128:256], func=AF.Copy),
                                reads=[pb_], writes=[vFb[vs]])
                            vo = ngv[seq, l, r0:r0 + 128, :]
                            P.dma("sp", lambda h, vs=vs, vo=vo: h.dma_start(out=vo, in_=vF[vs][:, 0:128]),
                                  reads=[vFb[vs]], sembuf=vFb[vs])
                            P.op("pool", lambda h, vs=vs, vbdst=vbdst: h.tensor_copy(
                                out=vbdst, in_=vF[vs][:, 0:128].rearrange("p (h e) -> p h e", h=2)),
                                reads=[vFb[vs]], writes=[VAb[t]] + ffn_bufs)
                        else:
                            P.op("act", lambda h, ps_=ps_, vbdst=vbdst: h.activation(
                                out=vbdst, in_=ps_[:, 128:256].rearrange("p (h e) -> p h e", h=2),
                                func=AF.Copy), reads=[pb_], writes=[VAb[t]] + ffn_bufs)
                        st2k = qk_chain(ps_, pb_, 2, 3, t, slot, g == "P", kout)

                        def st2(st2k=st2k, slot=slot):
                            st2k()
                            S_ = qst[slot]
                            P.op("pool", lambda h, S_=S_: h.tensor_copy(
                                out=S_[:, 128:256].rearrange("p (a b) -> p a b", a=2),
                                in_=S_[:, 64:128].unsqueeze(1).broadcast_to([128, 2, 64])),
                                reads=[qstb[slot]], writes=[qstb[slot]])
                            P.op("pool", lambda h, S_=S_: h.tensor_copy(out=S_[:, 64:128], in_=S_[:, 0:64]),
                                 reads=[qstb[slot]], writes=[qstb[slot]])
                        dst_fn = (lambda t=t: KT[:, 4:6, t * 128:(t + 1) * 128])
                        defer(st2, transposes(qst[slot], [qstb[slot]], 2, dst_fn, [KTb[1][t]], gi=(None if rope else 3)))
                    flush(NQ - 2)
                drain_q2()
                flush(NQ - 2)
                for _ in range(nrel):
                    w_release()
            if g == "S":
                for ct in range(4):
                    kt = 8 + ct
                    cs_, (cbk_, cb_) = cst[ct], cstb[ct]
                    P.op("pool", lambda h, cs_=cs_: h.tensor_copy(
                        out=cs_[:, 640:768].rearrange("p (a b) -> p a b", a=2),
                        in_=cs_[:, 576:640].unsqueeze(1).broadcast_to([128, 2, 64])),
                        reads=[cb_], writes=[cb_])
                    P.op("pool", lambda h, cs_=cs_: h.tensor_copy(out=cs_[:, 576:640], in_=cs_[:, 512:576]),
                         reads=[cb_], writes=[cb_])
                    dst_fn = (lambda kt=kt: KT[:, 0:6, kt * 128:(kt + 1) * 128])
                    pending.append(transposes(cs_, [cbk_, cb_], 6, dst_fn, [KTb[0][kt], KTb[1][kt]]))
                    flush(1)
            flush(0)

        def attention(g, l, tq_tiles):
            if g == "P":
                s_ring = bank_ring([0, 1])
                o_ring = bank_ring([2, 3, 4, 5])
                mt_ring = bank_ring([6, 7])
            else:
                s_ring = bank_ring([0, 1, 2])
                o_ring = bank_ring([3, 4, 5, 6])
                mt_ring = bank_ring([7])
            if g == "P":
                blocks = [(s_ * 2, [s_ * 2, s_ * 2 + 1]) for s_ in range(4)]
            else:
                blocks = [(qb * 2, list(range(12))) for qb in range(tq_tiles // 2)]
            pctr = [0]
            att_alias = qAb[0:3] + attb + atub
            P.op("pool", lambda h: h.memset(qA[0][:, 0:8], 0.0), writes=att_alias)

            def mix_transposes(qt0, mslots):
                def run():
                    for i, ms in enumerate(mslots):
                        qt = qt0 + i
                        ps_, pb_ = mt_ring.next()
                        pv = ps_[:, :].bitcast(BF16)
                        for c in range(KC):
                            P.op("pe", lambda h, c=c, ms=ms, pv=pv: h.transpose(
                                out=pv[:, c * 128:(c + 1) * 128], in_=mixtok[ms][:, c * 128:(c + 1) * 128],
                                identity=identb[:]), reads=[mixb[ms], cdone], writes=[pb_])
                        dst = hT[:, :, qt * 128:(qt + 1) * 128]
                        srcv = pv[:, :].rearrange("p (a b) -> p a b", a=KC)
                        if g == "P" and i == 0:
                            P.op("act", lambda h, dst=dst, srcv=srcv: h.activation(out=dst, in_=srcv, func=AF.Copy),
                                 reads=[pb_], writes=[hTb[k][qt // 4] for k in range(KC)])
                        else:
                            P.op("dve", lambda h, dst=dst, srcv=srcv: h.tensor_copy(out=dst, in_=srcv),
                                 reads=[pb_], writes=[hTb[k][qt // 4] for k in range(KC)])
                return run

            units = []
            for bi, (qt0, ktiles) in enumerate(blocks):
                hc = [("d", h_, 0) for h_ in range(4)] + [("g", g_, rp_) for g_ in range(2) for rp_ in range(2)]
                for ui, (kind, a_, b_) in enumerate(hc):
                    units.append(dict(bi=bi, qt0=qt0, ktiles=ktiles, kind=kind, a=a_, b=b_, first=(ui == 0),
                                      last=(ui == len(hc) - 1), mslots=[(bi % 2) * 2 + i for i in range(2)]))
            for ui, u in enumerate(units):
                u["pslot"] = ui % 2
            obanks = {}
            qbd_b = xsb
            post_fin = []
            mod_tail = list(range(4, 12)) if (g == group_order[0] and l == 0) else []

            def build_qbd(u):
                qt0 = u["qt0"]
                qcs = slice(qt0 * 128, qt0 * 128 + 256)
                rd = [QTb[0][qt0], QTb[0][qt0 + 1], QTb[1][qt0], QTb[1][qt0 + 1]]
                P.op("dve", lambda h, qcs=qcs: h.tensor_copy(out=Qbd[0:64, :, 0:256], in_=QT[0:64, :, qcs]),
                     reads=rd, writes=qbd_b)
                P.op("dve", lambda h, qcs=qcs: h.tensor_copy(out=Qbd[64:128, :, 256:512], in_=QT[64:128, :, qcs]),
                     reads=rd, writes=qbd_b)

            def S_step(u, ki):
                kind, a_, b_ = u["kind"], u["a"], u["b"]
                kt = u["ktiles"][ki]
                if kind == "d":
                    qch, kch, kgrp = a_, a_, 0
                else:
                    qch, kch, kgrp = 4 + 2 * a_ + b_, 4 + a_, 1
                pslot = u["pslot"]
                ps_, pb_ = s_ring.next()
                P.op("pe", lambda h, ps_=ps_, kch=kch, kt=kt, qch=qch: h.matmul(
                    ps_[:, :], lhsT=KT[:, kch, kt * 128:(kt + 1) * 128], rhs=Qbd[:, qch, :],
                    start=True, stop=True),
                    reads=[KTb[kgrp][kt]] + qbd_b, writes=[pb_])
                P.op("act", lambda h, ps_=ps_, pslot=pslot, ki=ki: h.activation(
                    out=PT[:, pslot, ki, :], in_=ps_[:, :], func=AF.Exp, scale=0.125),
                    reads=[pb_], writes=[PTb[pslot]] + ffn_bufs + cst_all)
                if g == "S" and PE_WARM:
                    wps, wpb = mt_ring.next()
                    for _ in range(PE_WARM):
                        P.op("pe", lambda h, wps=wps, kch=kch, kt=kt, qch=qch: h.matmul(
                            wps[:, :], lhsT=KT[:, kch, kt * 128:(kt + 1) * 128], rhs=Qbd[:, qch, :],
                            start=True, stop=True),
                            reads=[KTb[kgrp][kt]] + qbd_b, writes=[wpb])

            def O_group(u, i, cc, k0=0, k1=None):
                kind, a_, b_ = u["kind"], u["a"], u["b"]
                nk = len(u["ktiles"])
                k1 = nk if k1 is None else k1
                pslot = u["pslot"]
                key = (u["bi"], kind, a_, i)
                if b_ == 0 and cc == 0 and k0 == 0:
                    obanks[key] = o_ring.next()
                ob, obb = obanks[key]
                for ki, kt in list(enumerate(u["ktiles"]))[k0:k1]:
                    if kind == "d":
                        oap = ob[:, cc * 129:(cc + 1) * 129]
                        rhs = VA[:, kt, a_ * 129:(a_ + 1) * 129]
                        vb_ = VAa
                    else:
                        r = 2 * b_ + cc
                        oap = ob[:, r * 65:(r + 1) * 65]
                        rhs = VA[:, kt, 516 + a_ * 65:516 + (a_ + 1) * 65]
                        vb_ = VAb
                    P.op("pe", lambda h, oap=oap, pslot=pslot, ki=ki, i=i, cc=cc, rhs=rhs, nk=nk: h.matmul(
                        oap, lhsT=PT[:, pslot, ki, cc * 256 + i * 128:cc * 256 + (i + 1) * 128], rhs=rhs,
                        start=(ki == 0), stop=(ki == nk - 1)),
                        reads=[PTb[pslot], vb_[kt]], writes=[obb])

            def post(u):
                kind, a_, b_ = u["kind"], u["a"], u["b"]
                mslots = u["mslots"]
                if kind == "d":
                    ctx = []
                    for i in range(2):
                        ob, obb = obanks[(u["bi"], "d", a_, i)]
                        ps2 = pctr[0] % 6
                        pctr[0] += 1
                        ctx.append(dict(ob=ob, obb=obb, ms=mslots[i], rc=arec[ps2], rcb=arecb[ps2],
                                        tt=at_t[ps2], ttb=attb[ps2], uu=at_u[ps2], uub=atub[ps2],
                                        o3=ob[:, 0:258].rearrange("p (c e) -> p c e", c=2)))
                    for c in ctx:
                        P.op("dve", lambda h, c=c: h.reciprocal(out=c["rc"][:, 0:2], in_=c["o3"][:, :, 128]),
                             reads=[c["obb"]], writes=[c["rcb"]])
                    for c in ctx:
                        P.op("dve", lambda h, c=c: h.tensor_tensor(out=c["rc"][:, 2:3], in0=c["rc"][:, 1:2],
                                                                   in1=lamt[:, l, 5:6], op=ALU.mult),
                             reads=[c["rcb"], cdone], writes=[c["rcb"]])
                    for c in ctx:
                        if g == "P":
                            P.op("act", lambda h, c=c: h.activation(
                                out=c["tt"][:], in_=c["ob"][:, 0:128], func=AF.Copy, scale=c["rc"][:, 0:1]),
                                reads=[c["obb"], c["rcb"]], writes=[c["ttb"]])
                        else:
                            P.op("dve", lambda h, c=c: h.tensor_scalar(
                                out=c["tt"][:], in0=c["ob"][:, 0:128], scalar1=c["rc"][:, 0:1], scalar2=None,
                                op0=ALU.mult), reads=[c["obb"], c["rcb"]], writes=[c["ttb"]])
                    for c in ctx:
                        P.op("dve", lambda h, c=c: h.scalar_tensor_tensor(
                            out=c["uu"][:], in0=c["ob"][:, 129:257], scalar=c["rc"][:, 2:3], in1=c["tt"][:],
                            op0=ALU.mult, op1=ALU.add), reads=[c["obb"], c["rcb"], c["ttb"]], writes=[c["uub"]])
                    for c in ctx:
                        if g == "P":
                            P.op("act", lambda h, c=c: h.activation(
                                out=c["tt"][:], in_=c["uu"][:], func=AF.Square, accum_out=c["rc"][:, 4:5]),
                                reads=[c["uub"]], writes=[c["ttb"], c["rcb"]])
                        else:
                            P.op("dve", lambda h, c=c: h.scalar_tensor_tensor(
                                out=c["tt"][:], in0=c["uu"][:], scalar=1.0, in1=c["uu"][:], op0=ALU.mult,
                                op1=ALU.mult, accum_out=c["rc"][:, 4:5]),
                                reads=[c["uub"]], writes=[c["ttb"], c["rcb"]])
                    for c in ctx:
                        P.op("act", lambda h, c=c: h.activation(out=c["rc"][:, 5:6], in_=c["rc"][:, 4:5],
                                                                func=AF.Ln, bias=epst[:, 0:1], scale=1.0 / 128),
                             reads=[c["rcb"], cdone], writes=[c["rcb"]])
                        P.op("act", lambda h, c=c: h.activation(out=c["rc"][:, 6:7], in_=c["rc"][:, 5:6],
                                                                func=AF.Exp, scale=-0.5),
                             reads=[c["rcb"]], writes=[c["rcb"]])
                    def fin(ctx=ctx, a_=a_):
                        for c in ctx:
                            P.op("dve", lambda h, c=c: h.scalar_tensor_tensor(
                                out=mixtok[c["ms"]][:, a_ * 128:(a_ + 1) * 128], in0=c["uu"][:],
                                scalar=c["rc"][:, 6:7], in1=sublnb[:, l, :], op0=ALU.mult, op1=ALU.mult),
                                reads=[c["uub"], c["rcb"], cdone], writes=[mixb[c["ms"]], sqbb, sqbb2])
                    post_fin.append(fin)
                if kind == "g" and b_ == 1:
                    for i in range(2):
                        ob, obb = obanks[(u["bi"], "g", a_, i)]
                        ms = mslots[i]
                        ps2 = pctr[0] % 6
                        pctr[0] += 1
                        rc, rcb = arec[ps2], arecb[ps2]
                        o3 = ob[:, 0:260].rearrange("p (r e) -> p r e", r=4)
                        P.op("dve", lambda h, rc=rc, o3=o3: h.reciprocal(out=rc[:, 8:12], in_=o3[:, :, 64]),
                             reads=[obb], writes=[rcb])
                        mdst = mixtok[ms][:, 512 + a_ * 256:512 + (a_ + 1) * 256].rearrange(
                            "p (r e) -> p r e", r=4)
                        P.op("dve", lambda h, rc=rc, o3=o3, mdst=mdst: h.tensor_tensor(
                            out=mdst, in0=o3[:, :, 0:64],
                            in1=rc[:, 8:12].unsqueeze(2).broadcast_to([128, 4, 64]), op=ALU.mult),
                            reads=[obb, rcb], writes=[mixb[ms], sqbb, sqbb2])

            pending = []
            nu = len(units)
            P.op("pool", lambda h: h.memset(Qbd[64:128, :, 0:256], 0.0), writes=qbd_b)
            P.op("pool", lambda h: h.memset(Qbd[0:64, :, 256:512], 0.0), writes=qbd_b)
            ogroups = [(i, cc) for i in range(2) for cc in range(2)]
            for idx in range(nu + 1):
                cur = units[idx] if idx < nu else None
                prv = units[idx - 1] if idx >= 1 else None
                nk = len((cur or prv)["ktiles"])
                if cur is not None and cur["first"] and idx == 0:
                    build_qbd(cur)
                chunks = [list(range(c0, min(c0 + SCHUNK, nk))) for c0 in range(0, nk, SCHUNK)]
                opieces = []
                if prv is not None:
                    nkp = len(prv["ktiles"])
                    for (i_, cc_) in ogroups:
                        for k0 in range(0, nkp, OSUB):
                            opieces.append((i_, cc_, k0, min(k0 + OSUB, nkp)))
                gi = 0
                for ci, ch in enumerate(chunks):
                    if cur is not None:
                        for ki in ch:
                            S_step(cur, ki)
                    if prv is not None:
                        ng = -(-len(opieces) * (ci + 1) // len(chunks))
                        while gi < ng:
                            O_group(prv, *opieces[gi])
                            gi += 1
                if cur is not None and cur["last"] and idx + 1 < nu:
                    build_qbd(units[idx + 1])
                if prv is not None:
                    while gi < len(opieces):
                        O_group(prv, *opieces[gi])
                        gi += 1
                    nfin = len(post_fin)
                    post(prv)
                    for _ in range(nfin):
                        post_fin.pop(0)()
                    if prv["last"]:
                        while post_fin:
                            post_fin.pop(0)()
                        if mod_tail:
                            for _ in range(2):
                                mod_piece(0, mod_tail.pop(0), s_ring)
                    for pnd in pending:
                        pnd[0] -= 1
                    if prv["last"]:
                        pending.append([MIX_DEFER, mix_transposes(prv["qt0"], prv["mslots"])])
                    while pending and pending[0][0] <= 0:
                        pending.pop(0)[1]()
            while pending:
                pending.pop(0)[1]()
            P.op("pool", lambda h: h.memset(qA[0][:, 0:8], 0.0), writes=att_alias)

        def out_proj(l, n, ntb):
            ring = bank_ring([0, 1, 2, 3])
            for pc in range(2):
                wslot, wb = w_acquire("out")
                wv = wslot[:, 0:4096].rearrange("p (a b) -> p a b", a=8)
                for tb in range(ntb):
                    cs = slice(tb * 512, (tb + 1) * 512)
                    for mm in range(4):
                        m = pc * 4 + mm
                        ps_, pb_ = ring.next()
                        for k in range(KC):
                            P.op("pe", lambda h, ps_=ps_, k=k, mm=mm, cs=cs, wv=wv: h.matmul(
                                ps_[:, :], lhsT=wv[:, k, mm * 128:(mm + 1) * 128], rhs=hT[:, k, cs],
                                start=(k == 0), stop=(k == KC - 1)),
                                reads=[wb, hTb[k][tb]], writes=[pb_])
                        P.op("dve", lambda h, ps_=ps_, m=m, cs=cs: h.scalar_tensor_tensor(
                            out=xT[:, m, cs], in0=ps_[:, :], scalar=MS[:, l, 2, m, n:n + 1], in1=xT[:, m, cs],
                            op0=ALU.mult, op1=ALU.add), reads=[pb_, msb, xTb[m][tb]], writes=[xTb[m][tb]])
                w_release()

        def ffn(l, n, ntb, with_mod=None):
            if with_mod is not None:
                mring = bank_ring([0, 1])
                ring = bank_ring([2, 3, 4, 5, 6, 7])
            else:
                ring = bank_ring([0, 1, 2, 3, 4, 5, 6, 7])
            sctr = [0]
            for pc in range(11):
                wslot, wb = w_acquire("gu")
                wb2 = wstate["b2"]
                wg = wslot[:, 0:2048].rearrange("p (a b) -> p a b", a=8)
                wu = wslot[:, 2048:4096].rearrange("p (a b) -> p a b", a=8)
                for tb in range(ntb):
                    cs = slice(tb * 512, (tb + 1) * 512)
                    for jj in range(2):
                        j = pc * 2 + jj
                        pg, pgb = ring.next()
                        pu, pub = ring.next()
                        for (pp, ppb, wv, wbx) in ((pg, pgb, wg, wb), (pu, pub, wu, wb2)):
                            for k in range(KC):
                                P.op("pe", lambda h, pp=pp, k=k, jj=jj, cs=cs, wv=wv: h.matmul(
                                    pp[:, :], lhsT=wv[:, k, jj * 128:(jj + 1) * 128], rhs=hT[:, k, cs],
                                    start=(k == 0), stop=(k == KC - 1)),
                                    reads=[wbx, hTb[k][tb]], writes=[ppb])
                        ss = sctr[0] % 2
                        sctr[0] += 1
                        P.op("act", lambda h, ss=ss, pg=pg: h.activation(out=sgb[ss][:], in_=pg[:, :], func=AF.Silu),
                             reads=[pgb], writes=[sgbb[ss]])
                        P.op("dve", lambda h, ss=ss, pu=pu, j=j, cs=cs: h.tensor_tensor(
                            out=actT[:, j, cs], in0=sgb[ss][:], in1=pu[:, :], op=ALU.mult),
                            reads=[sgbb[ss], pub], writes=[actb[j][tb]] + attn_bufs)
                w_release()
                if with_mod is not None:
                    mod_piece(with_mod, pc, mring)
                    if pc == 10:
                        mod_piece(with_mod, 11, mring)
            ring2 = bank_ring([0, 1, 2, 3])
            for m in range(KC):
                wslot, wb = w_acquire("down")
                wv = wslot[:, 0:NJ * 128].rearrange("p (a b) -> p a b", a=NJ)
                for tb in range(ntb):
                    cs = slice(tb * 512, (tb + 1) * 512)
                    ps_, pb_ = ring2.next()
                    for j in range(NJ):
                        P.op("pe", lambda h, ps_=ps_, j=j, cs=cs, wv=wv: h.matmul(
                            ps_[:, :], lhsT=wv[:, j, :], rhs=actT[:, j, cs], start=(j == 0), stop=(j == NJ - 1)),
                            reads=[wb, actb[j][tb]], writes=[pb_])
                    P.op("dve", lambda h, ps_=ps_, m=m, cs=cs: h.scalar_tensor_tensor(
                        out=xT[:, m, cs], in0=ps_[:, :], scalar=MS[:, l, 5, m, n:n + 1], in1=xT[:, m, cs],
                        op0=ALU.mult, op1=ALU.add), reads=[pb_, msb, xTb[m][tb]], writes=[xTb[m][tb]])
                w_release()

        for g in group_order:
            n = 0 if g == "P" else 1
            mark(f"{g}_loadx")
            if g == group_order[0]:
                load_x(g)
            if g == group_order[0]:
                ring0 = bank_ring([0, 1])
                for pc in range(4):
                    mod_piece(0, pc, ring0)
            for l in range(DEPTH):
                tq_tiles = 4 if (g == "S" and l == DEPTH - 1) else NT
                ntb = tq_tiles // 4
                all_h = [b for row in hTb for b in row]
                all_x = [b for row in xTb for b in row]
                mark(f"{g}{l}_norm1")
                if g == "S" and l == 0:
                    prefetch_cache(0)
                norm_mod(l, n, 1, 0, 2)
                dbg_dump(f"{g}{l}_h", hT[:], all_h)
                mark(f"{g}{l}_proj")
                proj_step(g, l, tq_tiles)
                dbg_dump(f"{g}{l}_QT", QT, attn_bufs)
                dbg_dump(f"{g}{l}_KT", KT, attn_bufs)
                dbg_dump(f"{g}{l}_VA", VA, attn_bufs)
                mark(f"{g}{l}_attn")
                attention(g, l, tq_tiles)
                dbg_dump(f"{g}{l}_mix", hT[:], all_h)
                mark(f"{g}{l}_outproj")
                if g == "S" and l + 1 < DEPTH:
                    prefetch_cache(l + 1)
                out_proj(l, n, ntb)
                dbg_dump(f"{g}{l}_x1", xT[:], all_x)
                mark(f"{g}{l}_norm2")
                norm_mod(l, n, 4, 3, ntb)
                dbg_dump(f"{g}{l}_h2", hT[:], all_h)
                mark(f"{g}{l}_ffn")
                ffn(l, n, ntb, with_mod=(1 if (g == group_order[0] and l == 0) else None))
                dbg_dump(f"{g}{l}_x2", xT[:], all_x)
            mark(f"{g}_store")
            gi_ = group_order.index(g)
            if gi_ + 1 < len(group_order):
                gn = group_order[gi_ + 1]
                lal = [sqbb, sqbb2] + mixb + ldb
                P.op("pool", lambda h: h.memset(sqmix[:, 0:8], 0.0), writes=lal)
                store_y(g, range(0, 4))
                load_x(gn, range(0, 4), alt=True)
                store_y(g, range(4, 8))
                load_x(gn, range(4, 8), alt=True)
                P.op("pool", lambda h: h.memset(sqmix[:, 0:8], 0.0), writes=lal)
            else:
                store_y(g, range(NT if g == "P" else NT // 2))

        mark("end")
        P.final_wait("sp", out_bufs + xsb + qGb + vFb)
        P.emit()
    return nc


def _rope_tables(perm):
    t = np.asarray(perm, dtype=np.int64)
    row = (t // 64).astype(np.float32)
    col = (t % 64).astype(np.float32)
    freqs = (10000.0 ** (-np.arange(0, 32, 2, dtype=np.float32) / 32)).astype(np.float32)
    ang = np.concatenate([row[:, None] * freqs, col[:, None] * freqs], axis=-1).astype(np.float32)
    c, s = np.cos(ang).astype(np.float32), np.sin(ang).astype(np.float32)
    C = np.concatenate([c, c], axis=-1)
    S = np.concatenate([-s, s], axis=-1)
    C = np.ascontiguousarray(C.reshape(NT, 128, 64).transpose(1, 0, 2))
    S = np.ascontiguousarray(S.reshape(NT, 128, 64).transpose(1, 0, 2))
    return C, S


def make_in_maps(inp, cores):
    f = lambda a: np.ascontiguousarray(np.asarray(a, dtype=np.float32))
    x_prompt, x_sample = f(inp["x_prompt"]), f(inp["x_sample"])
    shared = {
        "w_mod": f(inp["w_mod"]), "w_in": f(inp["w_in"]), "w_out": f(inp["w_out"]),
        "w_gu": f(inp["w_gate_up"]), "w_down": f(inp["w_down"]),
        "bmodT": f(np.asarray(inp["b_mod"]).reshape(DEPTH, 48, 128).transpose(2, 0, 1)),
        "nrmT": f(np.stack([np.asarray(inp["norm_attn"]), np.asarray(inp["norm_ffn"])], 0)
                  .reshape(2, DEPTH, KC, 128).transpose(3, 0, 1, 2)),
        "gains": f(np.stack([np.asarray(inp["q_norm_a"]), np.asarray(inp["k_norm_a"]),
                             np.asarray(inp["q_norm_b"]), np.asarray(inp["k_norm_b"])], 1)),
        "gainsT": f(np.tile(np.stack([np.asarray(inp["q_norm_a"]), np.asarray(inp["k_norm_a"]),
                                      np.asarray(inp["q_norm_b"]), np.asarray(inp["k_norm_b"])], 1), (1, 1, 2))
                    .transpose(2, 0, 1)),
        "lamv": f(np.stack([np.asarray(inp["lambda_q1"]), np.asarray(inp["lambda_k1"]),
                            np.asarray(inp["lambda_q2"]), np.asarray(inp["lambda_k2"])], 1)),
        "subln": f(inp["subln"]),
        "identf": np.eye(128, dtype=np.float32),
    }
    maps = []
    for c in cores:
        b, hh = c // 2, c % 2
        perm = np.concatenate([np.arange(hh * 512, hh * 512 + 512), np.arange((1 - hh) * 512, (1 - hh) * 512 + 512)])
        C, S = _rope_tables(perm)
        cond = np.stack([np.asarray(inp["c_ctx"]), np.asarray(inp["c"])[b]], 0)
        m = dict(shared)
        m.update({
            "xp": f(x_prompt[4 * c:4 * c + 4].reshape(T, D)),
            "xs": f(x_sample[b][perm]),
            "cdk": f(np.asarray(inp["cache_diff_k"])[b].reshape(DEPTH, PAST, 512)),
            "cdv": f(np.asarray(inp["cache_diff_v"])[b].reshape(DEPTH, PAST, 512)),
            "cgk": f(np.asarray(inp["cache_gqa_k"])[b].reshape(DEPTH, PAST, 128)),
            "cgv": f(np.asarray(inp["cache_gqa_v"])[b].reshape(DEPTH, PAST, 128)),
            "condT": f(cond.reshape(2, KC, 128).transpose(2, 1, 0)),
            "ropeC": C, "ropeS": S,
        })
        maps.append(m)
    return maps


_NC_CACHE = {}


def kernel(**inputs):
    if "nc" not in _NC_CACHE:
        _NC_CACHE["nc"] = build_program()
    nc = _NC_CACHE["nc"]
    cores = list(range(NCORES))
    in_maps = make_in_maps(inputs, cores)
    res = run_bass_kernel_spmd(nc, in_maps, core_ids=cores)
    R = res.results
    yp = np.concatenate([np.asarray(R[c]["yp"]).reshape(4, 256, D) for c in cores], 0)
    ys = np.zeros((4, 1024, D), np.float32)
    for c in cores:
        b, hh = c // 2, c % 2
        ys[b, hh * 512:(hh + 1) * 512] = np.asarray(R[c]["ys"])
    ndk = np.concatenate([np.asarray(R[c]["ndk"]) for c in cores], 0).reshape(32, DEPTH, 256, 4, 2, 64)
    ndv = np.concatenate([np.asarray(R[c]["ndv"]) for c in cores], 0).reshape(32, DEPTH, 256, 4, 128)
    ngk = np.concatenate([np.asarray(R[c]["ngk"]) for c in cores], 0).reshape(32, DEPTH, 256, 2, 64)
    ngv = np.concatenate([np.asarray(R[c]["ngv"]) for c in cores], 0).reshape(32, DEPTH, 256, 2, 64)
    return (yp.astype(np.float32), ys, ndk.astype(np.float32), ndv.astype(np.float32),
            ngk.astype(np.float32), ngv.astype(np.float32))
```

```python
import math
from contextlib import ExitStack

import numpy as np
import concourse.bass as bass
import concourse.mybir as mybir
from concourse.bass_utils import run_bass_kernel_spmd

F32 = mybir.dt.float32
BF16 = mybir.dt.bfloat16
AF = mybir.ActivationFunctionType
ALU = mybir.AluOpType
AX = mybir.AxisListType

D = 1024
KC = 8
DEPTH = 2
HID = 2816
NJ = 22
INC = 2304
PAST = 512
EPS = 1e-6
NCORES = 8
T = 1024
MIX_DEFER = 7
SCHUNK = 3
OSUB = 12
PE_WARM = 0
NT = 8


class Buf:
    __slots__ = ("name", "last_w", "readers", "dsem", "dcount")

    def __init__(self, name):
        self.name = name
        self.last_w = None
        self.readers = []
        self.dsem = None
        self.dcount = 0


class Eng:
    def __init__(self, name, sem):
        self.name = name
        self.sem = sem
        self.count = 0
        self.known = {}
        self.ops = []


class Prog:
    def __init__(self, nc, stack):
        self.nc = nc
        self.stack = stack
        self.engs = {}
        for n in ("pe", "act", "dve", "pool", "sp"):
            self.engs[n] = Eng(n, self.new_sem("e_" + n))

    def new_sem(self, name):
        return self.stack.enter_context(self.nc.semaphore(name))

    def _deps(self, eng, reads, writes):
        need = {}

        def add(tok):
            if tok is None:
                return
            sem, val, en = tok
            if en == "pe" and eng.name == "pe":
                return
            k = id(sem)
            if k not in need or need[k][1] < val:
                need[k] = (sem, val)

        for b in reads:
            add(b.last_w)
        for b in writes:
            add(b.last_w)
            for r in b.readers:
                add(r)
        out = []
        for k, (sem, val) in need.items():
            if eng.known.get(k, 0) < val:
                eng.known[k] = val
                out.append((sem, val))
        return out

    @staticmethod
    def _mark(tok, reads, writes):
        for b in reads:
            b.readers.append(tok)
        for b in writes:
            b.last_w = tok
            b.readers = []

    def op(self, engname, fn, reads=(), writes=()):
        eng = self.engs[engname]
        waits = self._deps(eng, reads, writes)
        eng.count += 1
        sem = eng.sem

        def run(h, waits=waits, fn=fn, sem=sem):
            for s, v in waits:
                h.wait_ge(s, v)
            fn(h).then_inc(sem, 1)

        eng.ops.append(run)
        tok = (sem, eng.count, engname)
        self._mark(tok, reads, writes)
        return tok

    def dma(self, qname, fn, reads=(), writes=(), sembuf=None, nowait=False):
        eng = self.engs[qname]
        waits = [] if nowait else self._deps(eng, reads, writes)
        sb = sembuf if sembuf is not None else (writes[0] if writes else reads[0])
        if sb.dsem is None:
            sb.dsem = self.new_sem("d_" + sb.name)
        sb.dcount += 16
        sem, val = sb.dsem, sb.dcount

        def run(h, waits=waits, fn=fn, sem=sem):
            for s, v in waits:
                h.wait_ge(s, v)
            fn(h).then_inc(sem, 16)

        eng.ops.append(run)
        tok = (sem, val, "dma")
        self._mark(tok, reads, writes)
        return tok

    def final_wait(self, engname, bufs):
        eng = self.engs[engname]
        waits = self._deps(eng, [], bufs)

        def run(h, waits=waits):
            for s, v in waits:
                h.wait_ge(s, v)

        eng.ops.append(run)

    def emit(self):
        nc = self.nc
        E = self.engs
        with nc.Block() as block:
            @block.tensor
            def _(h):
                for f in E["pe"].ops:
                    f(h)

            @block.scalar
            def _(h):
                for f in E["act"].ops:
                    f(h)

            @block.vector
            def _(h):
                for f in E["dve"].ops:
                    f(h)

            @block.gpsimd
            def _(h):
                for f in E["pool"].ops:
                    f(h)

            @block.sync
            def _(h):
                for f in E["sp"].ops:
                    f(h)


class Ring:
    def __init__(self, items):
        self.items = items
        self.i = 0

    def next(self):
        it = self.items[self.i % len(self.items)]
        self.i += 1
        return it


def build_program(dbg=None):
    nc = bass.Bass("TRN2", target_bir_lowering=False)

    def din(name, shape, dt=F32):
        return nc.dram_tensor(name, list(shape), dt, kind="ExternalInput").ap()

    def dout(name, shape, dt=F32):
        return nc.dram_tensor(name, list(shape), dt, kind="ExternalOutput").ap()

    xin = {"P": din("xp", [T, D]), "S": din("xs", [T, D])}
    cdk = din("cdk", [DEPTH, PAST, 512])
    cdv = din("cdv", [DEPTH, PAST, 512])
    cgk = din("cgk", [DEPTH, PAST, 128])
    cgv = din("cgv", [DEPTH, PAST, 128])
    condT_d = din("condT", [128, KC, 2])
    w_mod = din("w_mod", [DEPTH, D, 6 * D])
    bmodT_d = din("bmodT", [128, DEPTH, 48])
    nrmT_d = din("nrmT", [128, 2, DEPTH, KC])
    w_in = din("w_in", [DEPTH, D, INC])
    gains_d = din("gains", [DEPTH, 4, 64])
    gainsT_d = din("gainsT", [128, DEPTH, 4])
    lamv_d = din("lamv", [DEPTH, 4, 64])
    subln_d = din("subln", [DEPTH, 128])
    w_out = din("w_out", [DEPTH, D, D])
    w_gu = din("w_gu", [DEPTH, D, 2 * HID])
    w_down = din("w_down", [DEPTH, HID, D])
    ropeC_d = din("ropeC", [128, NT, 64])
    ropeS_d = din("ropeS", [128, NT, 64])
    ident_d = din("identf", [128, 128])

    yout = {"P": dout("yp", [T, D]), "S": dout("ys", [T // 2, D])}
    ndk = dout("ndk", [4, DEPTH, 256, 512])
    ndv = dout("ndv", [4, DEPTH, 256, 512])
    ngk = dout("ngk", [4, DEPTH, 256, 128])
    ngv = dout("ngv", [4, DEPTH, 256, 128])
    dbg_out = {}
    if dbg:
        for name, (shape, dt_) in dbg.items():
            dbg_out[name] = dout("dbg_" + name, shape, dt_)

    st = ExitStack()
    with st:
        P = Prog(nc, st)
        nc._marks = []

        def mark(label):
            nc._marks.append((label, {n: len(e.ops) for n, e in P.engs.items()}))

        def sb(name, shape, dt=F32):
            return st.enter_context(nc.sbuf_tensor(name, list(shape), dt))

        xT = sb("xT", [128, KC, T], F32)
        hT = sb("hT", [128, KC, T], BF16)
        ARENA_N = 37504
        arena = sb("arena", [128, ARENA_N], BF16)
        QT = arena[:, 0:8192].rearrange("p (a b) -> p a b", a=8)
        KT = arena[:, 8192:17408].rearrange("p (a b) -> p a b", a=6)
        VA = arena[:, 17408:25160].rearrange("p (a b) -> p a b", a=12)
        PT = arena[:, 25160:37448].rearrange("p (s a b) -> p s a b", s=2, a=12)
        actT = arena[:, 0:22528].rearrange("p (a b) -> p a b", a=NJ)
        arena_f = arena[:, 0:16384].bitcast(F32)
        wm_slots = [arena_f[:, i * 4096:(i + 1) * 4096].rearrange("p (a b) -> p a b", a=8)
                    for i in range(2)]
        NW = 3
        wslots = [sb(f"wslot{i}", [128, 4096], BF16) for i in range(NW)]
        xq = sb("xq", [128, 2 * D], F32)
        xstage = [xq[:, i * D:(i + 1) * D] for i in range(2)]
        Qbd = xq[:, :].bitcast(BF16).rearrange("p (a b) -> p a b", a=8)
        identf = sb("identf_s", [128, 128], F32)
        identb = sb("identb", [128, 128], BF16)
        onesb = sb("onesb", [128, 128], BF16)
        ropeC = sb("ropeC_s", [128, NT, 64], F32)
        ropeS = sb("ropeS_s", [128, NT, 64], F32)
        gains = sb("gains_s", [128, DEPTH, 4, 64], F32)
        gainsT = sb("gainsT_s", [128, DEPTH, 4], F32)
        TCg = sb("TCg", [128, NT, 64], F32)
        TSg = sb("TSg", [128, NT, 64], F32)
        sublnb = sb("subln_s", [128, DEPTH, 128], F32)
        condT = sb("condT_s", [128, KC, 2], F32)
        scT = sb("scT", [128, KC, 2], F32)
        scTb = sb("scTb", [128, KC, 2], BF16)
        bmodT = sb("bmodT_s", [128, DEPTH, 48], F32)
        nrmT = sb("nrmT_s", [128, 2, DEPTH, KC], F32)
        MS = sb("MS", [128, DEPTH, 6, KC, 2], F32)
        lamt = sb("lamt", [128, DEPTH, 8], F32)
        epst = sb("epst", [128, 1], F32)
        sqmix = sb("sqmix", [128, 4096], BF16)
        sqb = sqmix[:, :].rearrange("p (a b) -> p a b", a=KC)
        NQ = 4
        qA = [sb(f"qA{i}", [128, 512], F32) for i in range(NQ)]
        qB = [sb(f"qB{i}", [128, 512], F32) for i in range(NQ)]
        qG = [sb(f"qG{i}", [128, 512], F32) for i in range(NQ)]
        qst = [sb(f"qst{i}", [128, 512], BF16) for i in range(NQ)]
        lamv = qB[1][:, 0:512].rearrange("p (l g d) -> p l g d", l=DEPTH, g=4)
        ntmp = qA[:3]
        rstdn = qB[:2]
        lnv = qG[0]
        sgb = qG[1:3]
        qss = [sb(f"qss{i}", [128, 24], F32) for i in range(NQ)]
        vF = [sb(f"vF{i}", [128, 512], F32) for i in range(2)]
        arec = [sb(f"arec{i}", [128, 16], F32) for i in range(6)]
        _atv = [qA[j][:, c * 128:(c + 1) * 128] for j in range(3) for c in range(4)]
        at_t = _atv[0:6]
        at_u = _atv[6:12]
        mixtok = [sqmix[:, i * D:(i + 1) * D] for i in range(4)]
        cst = [arena[:, 25160 + i * 768:25160 + (i + 1) * 768] for i in range(4)]

        psum = [st.enter_context(nc.psum_tensor(f"ps{i}", [128, 512], F32)) for i in range(8)]
        psb = [Buf(f"ps{i}") for i in range(8)]

        def bank_ring(ids):
            return Ring([(psum[i], psb[i]) for i in ids])

        xTb = [[Buf(f"xT{k}_{c}") for c in range(2)] for k in range(KC)]
        hTb = [[Buf(f"hT{k}_{c}") for c in range(2)] for k in range(KC)]
        QTb = [[Buf(f"QT{g}_{t}") for t in range(NT)] for g in range(2)]
        KTb = [[Buf(f"KT{g}_{t}") for t in range(12)] for g in range(2)]
        VAa = [Buf(f"VAa{t}") for t in range(12)]
        VAb = [Buf(f"VAb{t}") for t in range(12)]
        PTb = [Buf(f"PT{i}") for i in range(2)]
        actb = [[Buf(f"act{j}_{c}") for c in range(2)] for j in range(NJ)]
        wmb = [Buf(f"wm{i}") for i in range(2)]
        attn_bufs = [b for row in QTb for b in row] + [b for row in KTb for b in row] + VAa + VAb + PTb
        ffn_bufs = [b for row in actb for b in row] + wmb
        wsb = [Buf(f"ws{i}") for i in range(NW)]
        wsb2 = [Buf(f"wsu{i}") for i in range(NW)]
        xsb = [Buf(f"xs{i}") for i in range(2)]
        cbuf = Buf("consts")
        tabb = Buf("ropetab")
        msb = Buf("MS")
        sqbb = Buf("sqb")
        sqbb2 = Buf("sqb2")
        qAb = [Buf(f"qA{i}") for i in range(NQ)]
        qBb = [Buf(f"qB{i}") for i in range(NQ)]
        qGb = [Buf(f"qG{i}") for i in range(NQ)]
        ntmpb = qAb[:3]
        rstdnb = qBb[:2]
        lnvb = qGb[0]
        sgbb = qGb[1:3]
        qstb = [Buf(f"qst{i}") for i in range(NQ)]
        qssb = [Buf(f"qss{i}") for i in range(NQ)]
        vFb = [Buf(f"vF{i}") for i in range(2)]
        cstb = [(Buf(f"cstk{i}"), Buf(f"cstg{i}")) for i in range(4)]
        arecb = [Buf(f"arec{i}") for i in range(6)]
        attb = [Buf(f"att{i}") for i in range(6)]
        atub = [Buf(f"atu{i}") for i in range(6)]
        mixb = [Buf(f"mix{i}") for i in range(4)]
        outb = Buf("dram_out")
        out_bufs = [outb]

        def w_pieces(g, l):
            ps_ = []
            for nb in ("ka", "va", "kbvb", "qa", "qb"):
                ps_.append(("in", l, nb))
            for i in range(2):
                ps_.append(("out", l, i))
            for i in range(11):
                ps_.append(("gu", l, i))
            for m in range(KC):
                ps_.append(("down", l, m))
            return ps_

        IN_COLS = {"qa": (0, 512), "ka": (512, 512), "va": (1024, 512), "qb": (1536, 512),
                   "kbvb": (2048, 256)}
        group_order = ("P", "S")
        pieces = []
        for g in group_order:
            for l in range(DEPTH):
                wp = w_pieces(g, l)
                if g == group_order[0] and l == 0:
                    wp2 = [("mod", 0, pc) for pc in range(4)] + wp[:5] + \
                          [("mod", 0, pc) for pc in range(4, 12)] + wp[5:7]
                    for i in range(11):
                        wp2.append(wp[7 + i])
                        wp2.append(("mod", 1, i))
                    wp2.append(("mod", 1, 11))
                    wp = wp2 + wp[18:]
                pieces += wp
        wstate = {"issued": 0, "cur": 0}

        def issue_piece(i):
            kind, l, idx = pieces[i]
            slot = wslots[i % NW]
            b = wsb[i % NW]
            b2 = wsb2[i % NW]
            if kind == "mod":
                dst = slot[:, 0:4096].rearrange("p (a b) -> p a b", a=8)
                src = w_mod[l].rearrange("(k p) n -> p k n", p=128)[:, :, idx * 512:(idx + 1) * 512]
                P.dma("pool", lambda h, d=dst, s=src: h.dma_start(out=d, in_=s), writes=[b, b2])
            elif kind == "in":
                c0, n = IN_COLS[idx]
                dst = slot[:, 0:8 * n].rearrange("p (a b) -> p a b", a=8)
                src = w_in[l].rearrange("(k p) n -> p k n", p=128)[:, :, c0:c0 + n]
                P.dma("pool", lambda h, d=dst, s=src: h.dma_start(out=d, in_=s), writes=[b, b2])
            elif kind == "out":
                dst = slot[:, 0:4096].rearrange("p (a b) -> p a b", a=8)
                src = w_out[l].rearrange("(k p) n -> p k n", p=128)[:, :, idx * 512:(idx + 1) * 512]
                P.dma("pool", lambda h, d=dst, s=src: h.dma_start(out=d, in_=s), writes=[b, b2])
            elif kind == "gu":
                wv = w_gu[l].rearrange("(k p) n -> p k n", p=128)
                dg = slot[:, 0:2048].rearrange("p (a b) -> p a b", a=8)
                du = slot[:, 2048:4096].rearrange("p (a b) -> p a b", a=8)
                sg_ = wv[:, :, idx * 256:(idx + 1) * 256]
                su_ = wv[:, :, HID + idx * 256:HID + (idx + 1) * 256]
                P.dma("pool", lambda h, d=dg, s=sg_: h.dma_start(out=d, in_=s), writes=[b])
                P.dma("pool", lambda h, d=du, s=su_: h.dma_start(out=d, in_=s), writes=[b2])
            else:
                dst = slot[:, 0:NJ * 128].rearrange("p (a b) -> p a b", a=NJ)
                src = w_down[l].rearrange("(j p) n -> p j n", p=128)[:, :, idx * 128:(idx + 1) * 128]
                P.dma("pool", lambda h, d=dst, s=src: h.dma_start(out=d, in_=s), writes=[b, b2])

        def w_acquire(expect_kind, ahead=0):
            i = wstate["cur"] + ahead
            assert pieces[i][0] == expect_kind, (pieces[i], expect_kind)
            while wstate["issued"] <= i:
                issue_piece(wstate["issued"])
                wstate["issued"] += 1
            wstate["b2"] = wsb2[i % NW]
            return wslots[i % NW], wsb[i % NW]

        def w_release():
            i = wstate["cur"]
            wstate["cur"] += 1
            nxt = i + NW
            if nxt < len(pieces) and wstate["issued"] <= nxt:
                while wstate["issued"] <= nxt:
                    issue_piece(wstate["issued"])
                    wstate["issued"] += 1

        def ld(dst, src):
            P.dma("sp", lambda h, d=dst, s=src: h.dma_start(out=d, in_=s), writes=[cbuf], nowait=True)

        ld(identf[:], ident_d)
        ld(condT[:], condT_d)
        ld(bmodT[:], bmodT_d)
        ld(nrmT[:], nrmT_d)
        ld(ropeC[:], ropeC_d)
        ld(ropeS[:], ropeS_d)
        ld(gains[:], gains_d.partition_broadcast(128))
        ld(gainsT[:], gainsT_d)
        P.dma("sp", lambda h: h.dma_start(out=lamv, in_=lamv_d.partition_broadcast(128)), writes=[qBb[1]])
        ld(sublnb[:], subln_d.partition_broadcast(128))
        cdone = Buf("cdone")
        P.op("dve", lambda h: h.tensor_copy(out=identb[:], in_=identf[:]), reads=[cbuf], writes=[cdone])
        P.op("dve", lambda h: h.memset(onesb[:], 1.0), writes=[cdone])
        P.op("dve", lambda h: h.memset(epst[:], EPS), writes=[cdone])
        P.op("act", lambda h: h.activation(out=scTb[:], in_=condT[:], func=AF.Silu),
             reads=[cbuf], writes=[msb])
        for l in range(DEPTH):
            lam_init = 0.8 - 0.6 * math.exp(-0.3 * l)
            lt = lamt[:, l, :]
            P.op("dve", lambda h, l=l: h.tensor_tensor(out=qA[0][:, 0:64], in0=lamv[:, l, 0, :],
                                                       in1=lamv[:, l, 1, :], op=ALU.mult),
                 reads=[cbuf, qBb[1]], writes=[qAb[0]])
            P.op("dve", lambda h, l=l: h.tensor_tensor(out=qA[0][:, 64:128], in0=lamv[:, l, 2, :],
                                                       in1=lamv[:, l, 3, :], op=ALU.mult),
                 reads=[cbuf, qBb[1]], writes=[qAb[0]])
            P.op("dve", lambda h, lt=lt: h.tensor_reduce(
                out=lt[:, 0:2], in_=qA[0][:, 0:128].rearrange("p (a b) -> p a b", a=2),
                axis=AX.X, op=ALU.add), reads=[qAb[0]], writes=[cdone])
            P.op("act", lambda h, lt=lt: h.activation(out=lt[:, 2:4], in_=lt[:, 0:2], func=AF.Exp),
                 reads=[cdone], writes=[cdone])
            P.op("dve", lambda h, lt=lt: h.tensor_tensor(out=lt[:, 4:5], in0=lt[:, 3:4], in1=lt[:, 2:3],
                                                         op=ALU.subtract), reads=[cdone], writes=[cdone])
            P.op("dve", lambda h, lt=lt, li=lam_init: h.tensor_scalar(
                out=lt[:, 5:6], in0=lt[:, 4:5], scalar1=-li, scalar2=None, op0=ALU.add),
                reads=[cdone], writes=[cdone])
            P.op("dve", lambda h, l=l, li=lam_init: h.tensor_scalar(
                out=sublnb[:, l, :], in0=sublnb[:, l, :], scalar1=1.0 - li, scalar2=None, op0=ALU.mult),
                reads=[cbuf, cdone], writes=[cdone])


        def mod_piece(l, pc, ring):
            mps, mpb = ring.next()
            wslot, wb = w_acquire("mod")
            wv = wslot[:, 0:4096].rearrange("p (a b) -> p a b", a=8)
            for m4 in range(4):
                for k in range(KC):
                    P.op("pe", lambda h, mps=mps, wv=wv, m4=m4, k=k: h.matmul(
                        mps[:, m4 * 2:m4 * 2 + 2], lhsT=wv[:, k, m4 * 128:(m4 + 1) * 128],
                        rhs=scTb[:, k, :], start=(k == 0), stop=(k == KC - 1)),
                        reads=[wb, msb], writes=[mpb])
            w_release()
            msl = MS[:, l, :, :, :].rearrange("p s k n -> p (s k) n")[:, pc * 4:(pc + 1) * 4, :]
            P.op("dve", lambda h, mps=mps, msl=msl, l=l, pc=pc: h.tensor_tensor(
                out=msl, in0=mps[:, 0:8].rearrange("p (a n) -> p a n", n=2),
                in1=bmodT[:, l, pc * 4:(pc + 1) * 4].unsqueeze(2).broadcast_to([128, 4, 2]), op=ALU.add),
                reads=[mpb, cbuf, msb], writes=[msb])
            fold = {3: (1, 0), 9: (4, 1)}.get(pc)
            if fold is not None:
                s_idx, kind = fold
                P.op("dve", lambda h, l=l, s_idx=s_idx, kind=kind: h.scalar_tensor_tensor(
                    out=MS[:, l, s_idx, :, :], in0=MS[:, l, s_idx, :, :], scalar=1.0,
                    in1=nrmT[:, kind, l, :].unsqueeze(2).broadcast_to([128, KC, 2]),
                    op0=ALU.add, op1=ALU.mult), reads=[msb, cbuf], writes=[msb])

        tr_ring = bank_ring([6, 7])

        ldstage = [sqmix[:, :].bitcast(F32)[:, i * D:(i + 1) * D] for i in range(2)]
        ldb = [Buf(f"ldst{i}") for i in range(2)]
        xring = bank_ring([2, 3, 4, 5])

        def load_x(g, tiles=range(NT), alt=False):
            ring = xring
            for t in tiles:
                if alt:
                    xs_, xb_ = ldstage[t % 2], ldb[t % 2]
                else:
                    xs_, xb_ = xstage[t % 2], xsb[t % 2]
                src = xin[g][t * 128:(t + 1) * 128, :]
                P.dma("pool" if alt else "sp", lambda h, d=xs_, s=src: h.dma_start(out=d, in_=s), writes=[xb_])
                for half in range(2):
                    ps_, pb_ = ring.next()
                    for kk in range(4):
                        k = half * 4 + kk
                        P.op("pe", lambda h, ps_=ps_, xs_=xs_, k=k, kk=kk: h.transpose(
                            out=ps_[:, kk * 128:(kk + 1) * 128], in_=xs_[:, k * 128:(k + 1) * 128],
                            identity=identf[:]), reads=[xb_, cbuf], writes=[pb_])
                    dst = xT[:, half * 4:half * 4 + 4, t * 128:(t + 1) * 128]
                    srcp = ps_[:, :].rearrange("p (a b) -> p a b", a=4)
                    eng = "act" if half == 0 else "dve"
                    wr = [xTb[half * 4 + kk][t // 4] for kk in range(4)]
                    if eng == "act":
                        P.op("act", lambda h, d=dst, s=srcp: h.activation(out=d, in_=s, func=AF.Copy),
                             reads=[pb_], writes=wr)
                    else:
                        P.op("dve", lambda h, d=dst, s=srcp: h.tensor_copy(out=d, in_=s),
                             reads=[pb_], writes=wr)

        def store_y(g, tiles):
            ring = xring
            for t in tiles:
                xs_, xb_ = xstage[t % 2], xsb[t % 2]
                for half in range(2):
                    ps_, pb_ = ring.next()
                    for kk in range(4):
                        k = half * 4 + kk
                        P.op("pe", lambda h, ps_=ps_, k=k, kk=kk, t=t: h.transpose(
                            out=ps_[:, kk * 128:(kk + 1) * 128], in_=xT[:, k, t * 128:(t + 1) * 128],
                            identity=identf[:]), reads=[xTb[k][t // 4], cbuf], writes=[pb_])
                    dst = xs_[:, half * 512:(half + 1) * 512]
                    if half == 0:
                        P.op("act", lambda h, d=dst, s=ps_: h.activation(out=d, in_=s[:, :], func=AF.Copy),
                             reads=[pb_], writes=[xb_])
                    else:
                        P.op("dve", lambda h, d=dst, s=ps_: h.tensor_copy(out=d, in_=s[:, :]),
                             reads=[pb_], writes=[xb_])
                dsty = yout[g][t * 128:(t + 1) * 128, :]
                P.dma("sp", lambda h, d=dsty, s=xs_: h.dma_start(out=d, in_=s),
                      reads=[xb_], sembuf=xb_)

        def norm_mod(l, n, s_scale, s_shift, ncb):
            ring = bank_ring([0, 1])
            for cb in range(ncb):
                cs = slice(cb * 512, (cb + 1) * 512)
                P.op("act", lambda h, cs=cs: h.activation(out=sqb[:, 0:5, :], in_=xT[:, 0:5, cs], func=AF.Square),
                     reads=[xTb[k][cb] for k in range(5)], writes=[sqbb] + mixb)
                P.op("dve", lambda h, cs=cs: h.tensor_tensor(out=sqb[:, 5:8, :], in0=xT[:, 5:8, cs],
                                                             in1=xT[:, 5:8, cs], op=ALU.mult),
                     reads=[xTb[k][cb] for k in range(5, 8)], writes=[sqbb2] + mixb)
                ps_, pb_ = ring.next()
                for k in range(KC):
                    P.op("pe", lambda h, ps_=ps_, k=k: h.matmul(ps_[:, :], lhsT=onesb[:], rhs=sqb[:, k, :],
                                                                start=(k == 0), stop=(k == KC - 1)),
                         reads=[sqbb, sqbb2, cdone] + mixb, writes=[pb_])
                rs, rsb = rstdn[cb % 2], rstdnb[cb % 2]
                P.op("act", lambda h, ps_=ps_: h.activation(out=lnv[:], in_=ps_[:, :], func=AF.Ln,
                                                            bias=epst[:, 0:1], scale=1.0 / D),
                     reads=[pb_, cdone], writes=[lnvb])
                P.op("act", lambda h, rs=rs: h.activation(out=rs[:], in_=lnv[:], func=AF.Exp, scale=-0.5),
                     reads=[lnvb], writes=[rsb])
                for k in range(KC):
                    tm, tmb = ntmp[k % 3], ntmpb[k % 3]
                    P.op("dve", lambda h, tm=tm, k=k, cs=cs, rs=rs: h.tensor_tensor(
                        out=tm[:], in0=xT[:, k, cs], in1=rs[:], op=ALU.mult),
                        reads=[xTb[k][cb], rsb], writes=[tmb])
                    if k % 8 in (0, 2, 4, 6, 7):
                        P.op("act", lambda h, tm=tm, k=k, cs=cs: h.activation(
                            out=hT[:, k, cs], in_=tm[:], func=AF.Identity,
                            bias=MS[:, l, s_shift, k, n:n + 1], scale=MS[:, l, s_scale, k, n:n + 1]),
                            reads=[tmb, msb], writes=[hTb[k][cb]])
                    else:
                        P.op("pool", lambda h, tm=tm, k=k, cs=cs: h.tensor_scalar(
                            out=hT[:, k, cs], in0=tm[:], scalar1=MS[:, l, s_scale, k, n:n + 1],
                            scalar2=MS[:, l, s_shift, k, n:n + 1], op0=ALU.mult, op1=ALU.add),
                            reads=[tmb, msb], writes=[hTb[k][cb]])

        def dbg_dump(name, src_ap, reads):
            if dbg and name in dbg_out:
                db = Buf("dbg_" + name)
                out_bufs.append(db)
                P.dma("sp", lambda h: h.dma_start(out=dbg_out[name], in_=src_ap), reads=reads, writes=[db])

        cst_all = [b for pr in cstb for b in pr]

        def prefetch_cache(l):
            for ct in range(4):
                kt = 8 + ct
                cs_, (cbk_, cb_) = cst[ct], cstb[ct]
                rs = slice(ct * 128, (ct + 1) * 128)
                P.dma("pool", lambda h, cs_=cs_, rs=rs: h.dma_start(out=cs_[:, 0:512], in_=cdk[l, rs, :]),
                      writes=[cbk_] + PTb)
                P.dma("pool", lambda h, cs_=cs_, rs=rs: h.dma_start(out=cs_[:, 512:640], in_=cgk[l, rs, :]),
                      writes=[cb_] + PTb)
                vdst = VA[:, kt, 0:516].rearrange("p (h e) -> p h e", h=4)[:, :, 0:128]
                vbdst = VA[:, kt, 516:646].rearrange("p (h e) -> p h e", h=2)[:, :, 0:64]
                P.dma("pool", lambda h, vdst=vdst, rs=rs: h.dma_start(
                    out=vdst, in_=cdv[l, rs, :].rearrange("p (h e) -> p h e", h=4)),
                    writes=[VAa[kt]], sembuf=VAa[kt])
                P.dma("pool", lambda h, vbdst=vbdst, rs=rs: h.dma_start(
                    out=vbdst, in_=cgv[l, rs, :].rearrange("p (h e) -> p h e", h=2)),
                    writes=[VAb[kt]], sembuf=VAb[kt])

        def proj_step(g, l, tq_tiles):
            rope = (g == "S")
            proj_ring = bank_ring([2, 3, 4, 5])
            nkt = 12 if g == "S" else 8
            va4 = VA[:, 0:nkt, 0:516].rearrange("p t (h e) -> p t h e", h=4)[:, :, :, 128:129]
            vb2 = VA[:, 0:nkt, 516:646].rearrange("p t (h e) -> p t h e", h=2)[:, :, :, 64:65]
            P.op("pool", lambda h: h.memset(va4, 1.0), writes=VAa[:nkt] + ffn_bufs)
            P.op("pool", lambda h: h.memset(vb2, 1.0), writes=VAb[:nkt] + ffn_bufs)

            pending = []
            ucount = [0]

            def flush(keep):
                while len(pending) > keep:
                    pending.pop(0)()

            def build_rope_tables(gi):
                g1 = gains[:, l, gi, 0:32].unsqueeze(1).broadcast_to([128, NT, 32])
                g2 = gains[:, l, gi, 32:64].unsqueeze(1).broadcast_to([128, NT, 32])
                gf = gains[:, l, gi, :].unsqueeze(1).broadcast_to([128, NT, 64])
                P.op("dve", lambda h: h.tensor_tensor(out=TCg[:], in0=ropeC[:], in1=gf, op=ALU.mult),
                     reads=[cbuf], writes=[tabb])
                P.op("dve", lambda h: h.tensor_tensor(out=TSg[:, :, 0:32], in0=ropeS[:, :, 0:32], in1=g2,
                                                      op=ALU.mult), reads=[cbuf], writes=[tabb])
                P.op("dve", lambda h: h.tensor_tensor(out=TSg[:, :, 32:64], in0=ropeS[:, :, 32:64], in1=g1,
                                                      op=ALU.mult), reads=[cbuf], writes=[tabb])

            def qk_chain(ps_, pb_, nh, gi, t, slot, is_k, kout):
                w = nh * 64
                A, B, G, S_, SS = qA[slot], qB[slot], qG[slot], qst[slot], qss[slot]
                Ab, Bb, Gb, Sb, SSb = qAb[slot], qBb[slot], qGb[slot], qstb[slot], qssb[slot]
                v3 = lambda ap: ap[:, 0:w].rearrange("p (a b) -> p a b", a=nh)
                if rope:
                    P.op("act", lambda h: h.activation(out=B[:, 0:w], in_=ps_[:, 0:w], func=AF.Copy),
                         reads=[pb_], writes=[Bb])
                P.op("act", lambda h: h.activation(out=A[:, 0:w], in_=ps_[:, 0:w], func=AF.Square),
                     reads=[pb_], writes=[Ab])
                P.op("dve", lambda h: h.tensor_reduce(out=SS[:, 0:nh], in_=v3(A), axis=AX.X, op=ALU.add),
                     reads=[Ab], writes=[SSb])
                P.op("act", lambda h: h.activation(out=SS[:, 8:8 + nh], in_=SS[:, 0:nh], func=AF.Ln,
                                                   bias=epst[:, 0:1], scale=1.0 / 64),
                     reads=[SSb, cdone], writes=[SSb])
                P.op("act", lambda h: h.activation(out=SS[:, 16:16 + nh], in_=SS[:, 8:8 + nh], func=AF.Exp,
                                                   scale=-0.5), reads=[SSb], writes=[SSb])
                rstd_bc = SS[:, 16:16 + nh].unsqueeze(2).broadcast_to([128, nh, 64])
                if not rope:
                    if is_k:
                        def stage2():
                            P.op("dve", lambda h: h.tensor_tensor(out=v3(B), in0=v3(ps_), in1=rstd_bc, op=ALU.mult),
                                 reads=[pb_, SSb], writes=[Bb])
                            P.op("act", lambda h: h.activation(out=S_[:, 0:w], in_=B[:, 0:w], func=AF.Copy),
                                 reads=[Bb], writes=[Sb])
                            gain_bc = gains[:, l, gi, :].unsqueeze(1).broadcast_to([128, nh, 64])
                            P.op("pool", lambda h: h.tensor_tensor(out=v3(G), in0=v3(B), in1=gain_bc, op=ALU.mult),
                                 reads=[Bb, cbuf], writes=[Gb])
                            P.dma("sp", lambda h: h.dma_start(out=kout, in_=G[:, 0:w]), reads=[Gb], sembuf=Gb)
                    else:
                        def stage2():
                            P.op("dve", lambda h: h.tensor_tensor(out=v3(S_), in0=v3(ps_), in1=rstd_bc, op=ALU.mult),
                                 reads=[pb_, SSb], writes=[Sb])
                else:
                    A3, B3, G3 = v3(A), v3(B), v3(G)
                    cC = TCg[:, t, :].unsqueeze(1).broadcast_to([128, nh, 64])
                    s1 = TSg[:, t, 0:32].unsqueeze(1).broadcast_to([128, nh, 32])
                    s2 = TSg[:, t, 32:64].unsqueeze(1).broadcast_to([128, nh, 32])
                    P.op("dve", lambda h: h.tensor_tensor(out=G3, in0=B3, in1=cC, op=ALU.mult),
                         reads=[Bb, tabb], writes=[Gb])
                    P.op("pool", lambda h: h.tensor_tensor(out=A3[:, :, 0:32], in0=B3[:, :, 32:64], in1=s1,
                                                           op=ALU.mult), reads=[Bb, tabb], writes=[Ab])
                    P.op("pool", lambda h: h.tensor_tensor(out=A3[:, :, 32:64], in0=B3[:, :, 0:32], in1=s2,
                                                           op=ALU.mult), reads=[Bb, tabb], writes=[Ab])

                    def stage2():
                        P.op("dve", lambda h: h.tensor_tensor(out=G3, in0=G3, in1=A3, op=ALU.add),
                             reads=[Gb, Ab], writes=[Gb])
                        P.op("dve", lambda h: h.tensor_tensor(out=v3(S_), in0=G3, in1=rstd_bc, op=ALU.mult),
                             reads=[Gb, SSb], writes=[Sb])
                return stage2

            q2 = []

            def defer(stage2_fn, tr_fn):
                q2.append((stage2_fn, tr_fn))
                while len(q2) > 1:
                    s2_, tr_ = q2.pop(0)
                    s2_()
                    pending.append(tr_)

            def drain_q2():
                while q2:
                    s2_, tr_ = q2.pop(0)
                    s2_()
                    pending.append(tr_)

            def transposes(src, srcb, nchunk, dst_fn, dst_bufs, gi=None):
                def run():
                    ps_, pb_ = tr_ring.next()
                    pv = ps_[:, :].bitcast(BF16)
                    for c in range(nchunk):
                        P.op("pe", lambda h, c=c: h.transpose(out=pv[:, c * 128:(c + 1) * 128],
                                                              in_=src[:, c * 128:(c + 1) * 128],
                                                              identity=identb[:]),
                             reads=srcb + [cdone], writes=[pb_])
                    dst = dst_fn()
                    srcv = pv[:, 0:nchunk * 128].rearrange("p (a b) -> p a b", a=nchunk)
                    ucount[0] += 1
                    if rope:
                        ucount[0] = 0
                    if gi is not None:
                        gsc = gainsT[:, l, gi:gi + 1]
                        if ucount[0] % 2 == 0:
                            P.op("act", lambda h: h.activation(out=dst, in_=srcv, func=AF.Copy, scale=gsc),
                                 reads=[pb_, cbuf], writes=dst_bufs + ffn_bufs)
                        else:
                            P.op("dve", lambda h: h.tensor_scalar(out=dst, in0=srcv, scalar1=gsc, scalar2=None,
                                                                  op0=ALU.mult),
                                 reads=[pb_, cbuf], writes=dst_bufs + ffn_bufs)
                    elif ucount[0] % 2 == 0:
                        P.op("act", lambda h: h.activation(out=dst, in_=srcv, func=AF.Copy),
                             reads=[pb_], writes=dst_bufs + ffn_bufs)
                    else:
                        P.op("dve", lambda h: h.tensor_copy(out=dst, in_=srcv),
                             reads=[pb_], writes=dst_bufs + ffn_bufs)
                return run

            slot_ctr = [0]
            vslot_ctr = [0]
            r8, rq = list(range(NT)), list(range(tq_tiles))
            grp_specs = [((("ka", r8), ("va", r8[0:4])), 1),
                         ((("va", r8[4:8]), ("kbvb", r8)), 2),
                         ((("qa", rq),), 1), ((("qb", rq),), 1)]
            for (grp, nrel) in grp_specs:
                views = {}
                for ai, (nb_, _tl) in enumerate(grp):
                    if rope and nb_ != "va":
                        build_rope_tables({"qa": 0, "ka": 1, "qb": 2, "kbvb": 3}[nb_])
                    assert pieces[wstate["cur"] + ai][2] == nb_
                    wslot_, wb_ = w_acquire("in", ahead=ai)
                    nc_ = IN_COLS[nb_][1]
                    views[nb_] = (wslot_[:, 0:8 * nc_].rearrange("p (a b) -> p a b", a=8), wb_, nc_)
                chain_u = [(nb_, t_) for (nb_, tl) in grp if nb_ != "va" for t_ in tl]
                free_u = [(nb_, t_) for (nb_, tl) in grp if nb_ == "va" for t_ in tl]
                step = max(1, len(chain_u) // max(1, len(free_u)))
                unit_list = []
                for ci, u_ in enumerate(chain_u):
                    unit_list.append(u_)
                    if free_u and (ci + 1) % step == 0:
                        unit_list.append(free_u.pop(0))
                unit_list += free_u
                for (nb, t) in unit_list:
                    wv, wb, ncols = views[nb]
                    ps_, pb_ = proj_ring.next()
                    for k in range(KC):
                        P.op("pe", lambda h, ps_=ps_, k=k, t=t, wv=wv, ncols=ncols: h.matmul(
                            ps_[:, 0:ncols], lhsT=hT[:, k, t * 128:(t + 1) * 128], rhs=wv[:, k, :],
                            start=(k == 0), stop=(k == KC - 1)),
                            reads=[hTb[k][t // 4], wb], writes=[pb_])
                    seq, r0 = t // 2, (t % 2) * 128
                    if nb in ("qa", "qb", "ka"):
                        slot = slot_ctr[0] % NQ
                        slot_ctr[0] += 1
                        gi = {"qa": 0, "ka": 1, "qb": 2}[nb]
                        kout = ndk[seq, l, r0:r0 + 128, :] if nb == "ka" else None
                        st2 = qk_chain(ps_, pb_, 8, gi, t, slot, nb == "ka" and g == "P", kout)
                        if nb == "ka":
                            dst_fn = (lambda t=t: KT[:, 0:4, t * 128:(t + 1) * 128])
                            dbs = [KTb[0][t]]
                        else:
                            c0 = 0 if nb == "qa" else 4
                            dst_fn = (lambda t=t, c0=c0: QT[:, c0:c0 + 4, t * 128:(t + 1) * 128])
                            dbs = [QTb[0 if nb == "qa" else 1][t]]
                        defer(st2, transposes(qst[slot], [qstb[slot]], 4, dst_fn, dbs, gi=(None if rope else gi)))
                    elif nb == "va":
                        vdst = VA[:, t, 0:516].rearrange("p (h e) -> p h e", h=4)[:, :, 0:128]
                        if g == "P":
                            vs = vslot_ctr[0] % 2
                            vslot_ctr[0] += 1
                            P.op("act", lambda h, vs=vs, ps_=ps_: h.activation(out=vF[vs][:], in_=ps_[:, :],
                                                                              func=AF.Copy),
                                 reads=[pb_], writes=[vFb[vs]])
                            vo = ndv[seq, l, r0:r0 + 128, :]
                            P.dma("sp", lambda h, vs=vs, vo=vo: h.dma_start(out=vo, in_=vF[vs][:]),
                                  reads=[vFb[vs]], sembuf=vFb[vs])
                            P.op("pool", lambda h, vs=vs, vdst=vdst: h.tensor_copy(
                                out=vdst, in_=vF[vs][:].rearrange("p (h e) -> p h e", h=4)),
                                reads=[vFb[vs]], writes=[VAa[t]] + ffn_bufs)
                        else:
                            P.op("act", lambda h, ps_=ps_, vdst=vdst: h.activation(
                                out=vdst, in_=ps_[:, :].rearrange("p (h e) -> p h e", h=4), func=AF.Copy),
                                reads=[pb_], writes=[VAa[t]] + ffn_bufs)
                    else:
                        slot = slot_ctr[0] % NQ
                        slot_ctr[0] += 1
                        kout = ngk[seq, l, r0:r0 + 128, :]
                        vbdst = VA[:, t, 516:646].rearrange("p (h e) -> p h e", h=2)[:, :, 0:64]
                        if g == "P":
                            vs = vslot_ctr[0] % 2
                            vslot_ctr[0] += 1
                            P.op("act", lambda h, vs=vs, ps_=ps_: h.activation(
                                out=vF[vs][:, 0:128], in_=ps_[:, 128:256], func=AF.Copy),
                                reads=[pb_], writes=[vFb[vs]])
                            vo = ngv[seq, l, r0:r0 + 128, :]
                            P.dma("sp", lambda h, vs=vs, vo=vo: h.dma_start(out=vo, in_=vF[vs][:, 0:128]),
                                  reads=[vFb[vs]], sembuf=vFb[vs])
                            P.op("pool", lambda h, vs=vs, vbdst=vbdst: h.tensor_copy(
                                out=vbdst, in_=vF[vs][:, 0:128].rearrange("p (h e) -> p h e", h=2)),
                                reads=[vFb[vs]], writes=[VAb[t]] + ffn_bufs)
                        else:
                            P.op("act", lambda h, ps_=ps_, vbdst=vbdst: h.activation(
                                out=vbdst, in_=ps_[:, 128:256].rearrange("p (h e) -> p h e", h=2),
                                func=AF.Copy), reads=[pb_], writes=[VAb[t]] + ffn_bufs)
                        st2k = qk_chain(ps_, pb_, 2, 3, t, slot, g == "P", kout)

                        def st2(st2k=st2k, slot=slot):
                            st2k()
                            S_ = qst[slot]
                            P.op("pool", lambda h, S_=S_: h.tensor_copy(
                                out=S_[:, 128:256].rearrange("p (a b) -> p a b", a=2),
                                in_=S_[:, 64:128].unsqueeze(1).broadcast_to([128, 2, 64])),
                                reads=[qstb[slot]], writes=[qstb[slot]])
                            P.op("pool", lambda h, S_=S_: h.tensor_copy(out=S_[:, 64:128], in_=S_[:, 0:64]),
                                 reads=[qstb[slot]], writes=[qstb[slot]])
                        dst_fn = (lambda t=t: KT[:, 4:6, t * 128:(t + 1) * 128])
                        defer(st2, transposes(qst[slot], [qstb[slot]], 2, dst_fn, [KTb[1][t]], gi=(None if rope else 3)))
                    flush(NQ - 2)
                drain_q2()
                flush(NQ - 2)
                for _ in range(nrel):
                    w_release()
            if g == "S":
                for ct in range(4):
                    kt = 8 + ct
                    cs_, (cbk_, cb_) = cst[ct], cstb[ct]
                    P.op("pool", lambda h, cs_=cs_: h.tensor_copy(
                        out=cs_[:, 640:768].rearrange("p (a b) -> p a b", a=2),
                        in_=cs_[:, 576:640].unsqueeze(1).broadcast_to([128, 2, 64])),
                        reads=[cb_], writes=[cb_])
                    P.op("pool", lambda h, cs_=cs_: h.tensor_copy(out=cs_[:, 576:640], in_=cs_[:, 512:576]),
                         reads=[cb_], writes=[cb_])
                    dst_fn = (lambda kt=kt: KT[:, 0:6, kt * 128:(kt + 1) * 128])
                    pending.append(transposes(cs_, [cbk_, cb_], 6, dst_fn, [KTb[0][kt], KTb[1][kt]]))
                    flush(1)
            flush(0)

        def attention(g, l, tq_tiles):
            if g == "P":
                s_ring = bank_ring([0, 1])
                o_ring = bank_ring([2, 3, 4, 5])
                mt_ring = bank_ring([6, 7])
            else:
                s_ring = bank_ring([0, 1, 2])
                o_ring = bank_ring([3, 4, 5, 6])
                mt_ring = bank_ring([7])
            if g == "P":
                blocks = [(s_ * 2, [s_ * 2, s_ * 2 + 1]) for s_ in range(4)]
            else:
                blocks = [(qb * 2, list(range(12))) for qb in range(tq_tiles // 2)]
            pctr = [0]
            att_alias = qAb[0:3] + attb + atub
            P.op("pool", lambda h: h.memset(qA[0][:, 0:8], 0.0), writes=att_alias)

            def mix_transposes(qt0, mslots):
                def run():
                    for i, ms in enumerate(mslots):
                        qt = qt0 + i
                        ps_, pb_ = mt_ring.next()
                        pv = ps_[:, :].bitcast(BF16)
                        for c in range(KC):
                            P.op("pe", lambda h, c=c, ms=ms, pv=pv: h.transpose(
                                out=pv[:, c * 128:(c + 1) * 128], in_=mixtok[ms][:, c * 128:(c + 1) * 128],
                                identity=identb[:]), reads=[mixb[ms], cdone], writes=[pb_])
                        dst = hT[:, :, qt * 128:(qt + 1) * 128]
                        srcv = pv[:, :].rearrange("p (a b) -> p a b", a=KC)
                        if g == "P" and i == 0:
                            P.op("act", lambda h, dst=dst, srcv=srcv: h.activation(out=dst, in_=srcv, func=AF.Copy),
                                 reads=[pb_], writes=[hTb[k][qt // 4] for k in range(KC)])
                        else:
                            P.op("dve", lambda h, dst=dst, srcv=srcv: h.tensor_copy(out=dst, in_=srcv),
                                 reads=[pb_], writes=[hTb[k][qt // 4] for k in range(KC)])
                return run

            units = []
            for bi, (qt0, ktiles) in enumerate(blocks):
                hc = [("d", h_, 0) for h_ in range(4)] + [("g", g_, rp_) for g_ in range(2) for rp_ in range(2)]
                for ui, (kind, a_, b_) in enumerate(hc):
                    units.append(dict(bi=bi, qt0=qt0, ktiles=ktiles, kind=kind, a=a_, b=b_, first=(ui == 0),
                                      last=(ui == len(hc) - 1), mslots=[(bi % 2) * 2 + i for i in range(2)]))
            for ui, u in enumerate(units):
                u["pslot"] = ui % 2
            obanks = {}
            qbd_b = xsb
            post_fin = []
            mod_tail = list(range(4, 12)) if (g == group_order[0] and l == 0) else []

            def build_qbd(u):
                qt0 = u["qt0"]
                qcs = slice(qt0 * 128, qt0 * 128 + 256)
                rd = [QTb[0][qt0], QTb[0][qt0 + 1], QTb[1][qt0], QTb[1][qt0 + 1]]
                P.op("dve", lambda h, qcs=qcs: h.tensor_copy(out=Qbd[0:64, :, 0:256], in_=QT[0:64, :, qcs]),
                     reads=rd, writes=qbd_b)
                P.op("dve", lambda h, qcs=qcs: h.tensor_copy(out=Qbd[64:128, :, 256:512], in_=QT[64:128, :, qcs]),
                     reads=rd, writes=qbd_b)

            def S_step(u, ki):
                kind, a_, b_ = u["kind"], u["a"], u["b"]
                kt = u["ktiles"][ki]
                if kind == "d":
                    qch, kch, kgrp = a_, a_, 0
                else:
                    qch, kch, kgrp = 4 + 2 * a_ + b_, 4 + a_, 1
                pslot = u["pslot"]
                ps_, pb_ = s_ring.next()
                P.op("pe", lambda h, ps_=ps_, kch=kch, kt=kt, qch=qch: h.matmul(
                    ps_[:, :], lhsT=KT[:, kch, kt * 128:(kt + 1) * 128], rhs=Qbd[:, qch, :],
                    start=True, stop=True),
                    reads=[KTb[kgrp][kt]] + qbd_b, writes=[pb_])
                P.op("act", lambda h, ps_=ps_, pslot=pslot, ki=ki: h.activation(
                    out=PT[:, pslot, ki, :], in_=ps_[:, :], func=AF.Exp, scale=0.125),
                    reads=[pb_], writes=[PTb[pslot]] + ffn_bufs + cst_all)
                if g == "S" and PE_WARM:
                    wps, wpb = mt_ring.next()
                    for _ in range(PE_WARM):
                        P.op("pe", lambda h, wps=wps, kch=kch, kt=kt, qch=qch: h.matmul(
                            wps[:, :], lhsT=KT[:, kch, kt * 128:(kt + 1) * 128], rhs=Qbd[:, qch, :],
                            start=True, stop=True),
                            reads=[KTb[kgrp][kt]] + qbd_b, writes=[wpb])

            def O_group(u, i, cc, k0=0, k1=None):
                kind, a_, b_ = u["kind"], u["a"], u["b"]
                nk = len(u["ktiles"])
                k1 = nk if k1 is None else k1
                pslot = u["pslot"]
                key = (u["bi"], kind, a_, i)
                if b_ == 0 and cc == 0 and k0 == 0:
                    obanks[key] = o_ring.next()
                ob, obb = obanks[key]
                for ki, kt in list(enumerate(u["ktiles"]))[k0:k1]:
                    if kind == "d":
                        oap = ob[:, cc * 129:(cc + 1) * 129]
                        rhs = VA[:, kt, a_ * 129:(a_ + 1) * 129]
                        vb_ = VAa
                    else:
                        r = 2 * b_ + cc
                        oap = ob[:, r * 65:(r + 1) * 65]
                        rhs = VA[:, kt, 516 + a_ * 65:516 + (a_ + 1) * 65]
                        vb_ = VAb
                    P.op("pe", lambda h, oap=oap, pslot=pslot, ki=ki, i=i, cc=cc, rhs=rhs, nk=nk: h.matmul(
                        oap, lhsT=PT[:, pslot, ki, cc * 256 + i * 128:cc * 256 + (i + 1) * 128], rhs=rhs,
                        start=(ki == 0), stop=(ki == nk - 1)),
                        reads=[PTb[pslot], vb_[kt]], writes=[obb])

            def post(u):
                kind, a_, b_ = u["kind"], u["a"], u["b"]
                mslots = u["mslots"]
                if kind == "d":
                    ctx = []
                    for i in range(2):
                        ob, obb = obanks[(u["bi"], "d", a_, i)]
                        ps2 = pctr[0] % 6
                        pctr[0] += 1
                        ctx.append(dict(ob=ob, obb=obb, ms=mslots[i], rc=arec[ps2], rcb=arecb[ps2],
                                        tt=at_t[ps2], ttb=attb[ps2], uu=at_u[ps2], uub=atub[ps2],
                                        o3=ob[:, 0:258].rearrange("p (c e) -> p c e", c=2)))
                    for c in ctx:
                        P.op("dve", lambda h, c=c: h.reciprocal(out=c["rc"][:, 0:2], in_=c["o3"][:, :, 128]),
                             reads=[c["obb"]], writes=[c["rcb"]])
                    for c in ctx:
                        P.op("dve", lambda h, c=c: h.tensor_tensor(out=c["rc"][:, 2:3], in0=c["rc"][:, 1:2],
                                                                   in1=lamt[:, l, 5:6], op=ALU.mult),
                             reads=[c["rcb"], cdone], writes=[c["rcb"]])
                    for c in ctx:
                        if g == "P":
                            P.op("act", lambda h, c=c: h.activation(
                                out=c["tt"][:], in_=c["ob"][:, 0:128], func=AF.Copy, scale=c["rc"][:, 0:1]),
                                reads=[c["obb"], c["rcb"]], writes=[c["ttb"]])
                        else:
                            P.op("dve", lambda h, c=c: h.tensor_scalar(
                                out=c["tt"][:], in0=c["ob"][:, 0:128], scalar1=c["rc"][:, 0:1], scalar2=None,
                                op0=ALU.mult), reads=[c["obb"], c["rcb"]], writes=[c["ttb"]])
                    for c in ctx:
                        P.op("dve", lambda h, c=c: h.scalar_tensor_tensor(
                            out=c["uu"][:], in0=c["ob"][:, 129:257], scalar=c["rc"][:, 2:3], in1=c["tt"][:],
                            op0=ALU.mult, op1=ALU.add), reads=[c["obb"], c["rcb"], c["ttb"]], writes=[c["uub"]])
                    for c in ctx:
                        if g == "P":
                            P.op("act", lambda h, c=c: h.activation(
                                out=c["tt"][:], in_=c["uu"][:], func=AF.Square, accum_out=c["rc"][:, 4:5]),
                                reads=[c["uub"]], writes=[c["ttb"], c["rcb"]])
                        else:
                            P.op("dve", lambda h, c=c: h.scalar_tensor_tensor(
                                out=c["tt"][:], in0=c["uu"][:], scalar=1.0, in1=c["uu"][:], op0=ALU.mult,
                                op1=ALU.mult, accum_out=c["rc"][:, 4:5]),
                                reads=[c["uub"]], writes=[c["ttb"], c["rcb"]])
                    for c in ctx:
                        P.op("act", lambda h, c=c: h.activation(out=c["rc"][:, 5:6], in_=c["rc"][:, 4:5],
                                                                func=AF.Ln, bias=epst[:, 0:1], scale=1.0 / 128),
                             reads=[c["rcb"], cdone], writes=[c["rcb"]])
                        P.op("act", lambda h, c=c: h.activation(out=c["rc"][:, 6:7], in_=c["rc"][:, 5:6],
                                                                func=AF.Exp, scale=-0.5),
                             reads=[c["rcb"]], writes=[c["rcb"]])
                    def fin(ctx=ctx, a_=a_):
                        for c in ctx:
                            P.op("dve", lambda h, c=c: h.scalar_tensor_tensor(
                                out=mixtok[c["ms"]][:, a_ * 128:(a_ + 1) * 128], in0=c["uu"][:],
                                scalar=c["rc"][:, 6:7], in1=sublnb[:, l, :], op0=ALU.mult, op1=ALU.mult),
                                reads=[c["uub"], c["rcb"], cdone], writes=[mixb[c["ms"]], sqbb, sqbb2])
                    post_fin.append(fin)
                if kind == "g" and b_ == 1:
                    for i in range(2):
                        ob, obb = obanks[(u["bi"], "g", a_, i)]
                        ms = mslots[i]
                        ps2 = pctr[0] % 6
                        pctr[0] += 1
                        rc, rcb = arec[ps2], arecb[ps2]
                        o3 = ob[:, 0:260].rearrange("p (r e) -> p r e", r=4)
                        P.op("dve", lambda h, rc=rc, o3=o3: h.reciprocal(out=rc[:, 8:12], in_=o3[:, :, 64]),
                             reads=[obb], writes=[rcb])
                        mdst = mixtok[ms][:, 512 + a_ * 256:512 + (a_ + 1) * 256].rearrange(
                            "p (r e) -> p r e", r=4)
                        P.op("dve", lambda h, rc=rc, o3=o3, mdst=mdst: h.tensor_tensor(
                            out=mdst, in0=o3[:, :, 0:64],
                            in1=rc[:, 8:12].unsqueeze(2).broadcast_to([128, 4, 64]), op=ALU.mult),
                            reads=[obb, rcb], writes=[mixb[ms], sqbb, sqbb2])

            pending = []
            nu = len(units)
            P.op("pool", lambda h: h.memset(Qbd[64:128, :, 0:256], 0.0), writes=qbd_b)
            P.op("pool", lambda h: h.memset(Qbd[0:64, :, 256:512], 0.0), writes=qbd_b)
            ogroups = [(i, cc) for i in range(2) for cc in range(2)]
            for idx in range(nu + 1):
                cur = units[idx] if idx < nu else None
                prv = units[idx - 1] if idx >= 1 else None
                nk = len((cur or prv)["ktiles"])
                if cur is not None and cur["first"] and idx == 0:
                    build_qbd(cur)
                chunks = [list(range(c0, min(c0 + SCHUNK, nk))) for c0 in range(0, nk, SCHUNK)]
                opieces = []
                if prv is not None:
                    nkp = len(prv["ktiles"])
                    for (i_, cc_) in ogroups:
                        for k0 in range(0, nkp, OSUB):
                            opieces.append((i_, cc_, k0, min(k0 + OSUB, nkp)))
                gi = 0
                for ci, ch in enumerate(chunks):
                    if cur is not None:
                        for ki in ch:
                            S_step(cur, ki)
                    if prv is not None:
                        ng = -(-len(opieces) * (ci + 1) // len(chunks))
                        while gi < ng:
                            O_group(prv, *opieces[gi])
                            gi += 1
                if cur is not None and cur["last"] and idx + 1 < nu:
                    build_qbd(units[idx + 1])
                if prv is not None:
                    while gi < len(opieces):
                        O_group(prv, *opieces[gi])
                        gi += 1
                    nfin = len(post_fin)
                    post(prv)
                    for _ in range(nfin):
                        post_fin.pop(0)()
                    if prv["last"]:
                        while post_fin:
                            post_fin.pop(0)()
                        if mod_tail:
                            for _ in range(2):
                                mod_piece(0, mod_tail.pop(0), s_ring)
                    for pnd in pending:
                        pnd[0] -= 1
                    if prv["last"]:
                        pending.append([MIX_DEFER, mix_transposes(prv["qt0"], prv["mslots"])])
                    while pending and pending[0][0] <= 0:
                        pending.pop(0)[1]()
            while pending:
                pending.pop(0)[1]()
            P.op("pool", lambda h: h.memset(qA[0][:, 0:8], 0.0), writes=att_alias)

        def out_proj(l, n, ntb):
            ring = bank_ring([0, 1, 2, 3])
            for pc in range(2):
                wslot, wb = w_acquire("out")
                wv = wslot[:, 0:4096].rearrange("p (a b) -> p a b", a=8)
                for tb in range(ntb):
                    cs = slice(tb * 512, (tb + 1) * 512)
                    for mm in range(4):
                        m = pc * 4 + mm
                        ps_, pb_ = ring.next()
                        for k in range(KC):
                            P.op("pe", lambda h, ps_=ps_, k=k, mm=mm, cs=cs, wv=wv: h.matmul(
                                ps_[:, :], lhsT=wv[:, k, mm * 128:(mm + 1) * 128], rhs=hT[:, k, cs],
                                start=(k == 0), stop=(k == KC - 1)),
                                reads=[wb, hTb[k][tb]], writes=[pb_])
                        P.op("dve", lambda h, ps_=ps_, m=m, cs=cs: h.scalar_tensor_tensor(
                            out=xT[:, m, cs], in0=ps_[:, :], scalar=MS[:, l, 2, m, n:n + 1], in1=xT[:, m, cs],
                            op0=ALU.mult, op1=ALU.add), reads=[pb_, msb, xTb[m][tb]], writes=[xTb[m][tb]])
                w_release()

        def ffn(l, n, ntb, with_mod=None):
            if with_mod is not None:
                mring = bank_ring([0, 1])
                ring = bank_ring([2, 3, 4, 5, 6, 7])
            else:
                ring = bank_ring([0, 1, 2, 3, 4, 5, 6, 7])
            sctr = [0]
            for pc in range(11):
                wslot, wb = w_acquire("gu")
                wb2 = wstate["b2"]
                wg = wslot[:, 0:2048].rearrange("p (a b) -> p a b", a=8)
                wu = wslot[:, 2048:4096].rearrange("p (a b) -> p a b", a=8)
                for tb in range(ntb):
                    cs = slice(tb * 512, (tb + 1) * 512)
                    for jj in range(2):
                        j = pc * 2 + jj
                        pg, pgb = ring.next()
                        pu, pub = ring.next()
                        for (pp, ppb, wv, wbx) in ((pg, pgb, wg, wb), (pu, pub, wu, wb2)):
                            for k in range(KC):
                                P.op("pe", lambda h, pp=pp, k=k, jj=jj, cs=cs, wv=wv: h.matmul(
                                    pp[:, :], lhsT=wv[:, k, jj * 128:(jj + 1) * 128], rhs=hT[:, k, cs],
                                    start=(k == 0), stop=(k == KC - 1)),
                                    reads=[wbx, hTb[k][tb]], writes=[ppb])
                        ss = sctr[0] % 2
                        sctr[0] += 1
                        P.op("act", lambda h, ss=ss, pg=pg: h.activation(out=sgb[ss][:], in_=pg[:, :], func=AF.Silu),
                             reads=[pgb], writes=[sgbb[ss]])
                        P.op("dve", lambda h, ss=ss, pu=pu, j=j, cs=cs: h.tensor_tensor(
                            out=actT[:, j, cs], in0=sgb[ss][:], in1=pu[:, :], op=ALU.mult),
                            reads=[sgbb[ss], pub], writes=[actb[j][tb]] + attn_bufs)
                w_release()
                if with_mod is not None:
                    mod_piece(with_mod, pc, mring)
                    if pc == 10:
                        mod_piece(with_mod, 11, mring)
            ring2 = bank_ring([0, 1, 2, 3])
            for m in range(KC):
                wslot, wb = w_acquire("down")
                wv = wslot[:, 0:NJ * 128].rearrange("p (a b) -> p a b", a=NJ)
                for tb in range(ntb):
                    cs = slice(tb * 512, (tb + 1) * 512)
                    ps_, pb_ = ring2.next()
                    for j in range(NJ):
                        P.op("pe", lambda h, ps_=ps_, j=j, cs=cs, wv=wv: h.matmul(
                            ps_[:, :], lhsT=wv[:, j, :], rhs=actT[:, j, cs], start=(j == 0), stop=(j == NJ - 1)),
                            reads=[wb, actb[j][tb]], writes=[pb_])
                    P.op("dve", lambda h, ps_=ps_, m=m, cs=cs: h.scalar_tensor_tensor(
                        out=xT[:, m, cs], in0=ps_[:, :], scalar=MS[:, l, 5, m, n:n + 1], in1=xT[:, m, cs],
                        op0=ALU.mult, op1=ALU.add), reads=[pb_, msb, xTb[m][tb]], writes=[xTb[m][tb]])
                w_release()

        for g in group_order:
            n = 0 if g == "P" else 1
            mark(f"{g}_loadx")
            if g == group_order[0]:
                load_x(g)
            if g == group_order[0]:
                ring0 = bank_ring([0, 1])
                for pc in range(4):
                    mod_piece(0, pc, ring0)
            for l in range(DEPTH):
                tq_tiles = 4 if (g == "S" and l == DEPTH - 1) else NT
                ntb = tq_tiles // 4
                all_h = [b for row in hTb for b in row]
                all_x = [b for row in xTb for b in row]
                mark(f"{g}{l}_norm1")
                if g == "S" and l == 0:
                    prefetch_cache(0)
                norm_mod(l, n, 1, 0, 2)
                dbg_dump(f"{g}{l}_h", hT[:], all_h)
                mark(f"{g}{l}_proj")
                proj_step(g, l, tq_tiles)
                dbg_dump(f"{g}{l}_QT", QT, attn_bufs)
                dbg_dump(f"{g}{l}_KT", KT, attn_bufs)
                dbg_dump(f"{g}{l}_VA", VA, attn_bufs)
                mark(f"{g}{l}_attn")
                attention(g, l, tq_tiles)
                dbg_dump(f"{g}{l}_mix", hT[:], all_h)
                mark(f"{g}{l}_outproj")
                if g == "S" and l + 1 < DEPTH:
                    prefetch_cache(l + 1)
                out_proj(l, n, ntb)
                dbg_dump(f"{g}{l}_x1", xT[:], all_x)
                mark(f"{g}{l}_norm2")
                norm_mod(l, n, 4, 3, ntb)
                dbg_dump(f"{g}{l}_h2", hT[:], all_h)
                mark(f"{g}{l}_ffn")
                ffn(l, n, ntb, with_mod=(1 if (g == group_order[0] and l == 0) else None))
                dbg_dump(f"{g}{l}_x2", xT[:], all_x)
            mark(f"{g}_store")
            gi_ = group_order.index(g)
            if gi_ + 1 < len(group_order):
                gn = group_order[gi_ + 1]
                lal = [sqbb, sqbb2] + mixb + ldb
                P.op("pool", lambda h: h.memset(sqmix[:, 0:8], 0.0), writes=lal)
                store_y(g, range(0, 4))
                load_x(gn, range(0, 4), alt=True)
                store_y(g, range(4, 8))
                load_x(gn, range(4, 8), alt=True)
                P.op("pool", lambda h: h.memset(sqmix[:, 0:8], 0.0), writes=lal)
            else:
                store_y(g, range(NT if g == "P" else NT // 2))

        mark("end")
        P.final_wait("sp", out_bufs + xsb + qGb + vFb)
        P.emit()
    return nc


def _rope_tables(perm):
    t = np.asarray(perm, dtype=np.int64)
    row = (t // 64).astype(np.float32)
    col = (t % 64).astype(np.float32)
    freqs = (10000.0 ** (-np.arange(0, 32, 2, dtype=np.float32) / 32)).astype(np.float32)
    ang = np.concatenate([row[:, None] * freqs, col[:, None] * freqs], axis=-1).astype(np.float32)
    c, s = np.cos(ang).astype(np.float32), np.sin(ang).astype(np.float32)
    C = np.concatenate([c, c], axis=-1)
    S = np.concatenate([-s, s], axis=-1)
    C = np.ascontiguousarray(C.reshape(NT, 128, 64).transpose(1, 0, 2))
    S = np.ascontiguousarray(S.reshape(NT, 128, 64).transpose(1, 0, 2))
    return C, S


def make_in_maps(inp, cores):
    f = lambda a: np.ascontiguousarray(np.asarray(a, dtype=np.float32))
    x_prompt, x_sample = f(inp["x_prompt"]), f(inp["x_sample"])
    shared = {
        "w_mod": f(inp["w_mod"]), "w_in": f(inp["w_in"]), "w_out": f(inp["w_out"]),
        "w_gu": f(inp["w_gate_up"]), "w_down": f(inp["w_down"]),
        "bmodT": f(np.asarray(inp["b_mod"]).reshape(DEPTH, 48, 128).transpose(2, 0, 1)),
        "nrmT": f(np.stack([np.asarray(inp["norm_attn"]), np.asarray(inp["norm_ffn"])], 0)
                  .reshape(2, DEPTH, KC, 128).transpose(3, 0, 1, 2)),
        "gains": f(np.stack([np.asarray(inp["q_norm_a"]), np.asarray(inp["k_norm_a"]),
                             np.asarray(inp["q_norm_b"]), np.asarray(inp["k_norm_b"])], 1)),
        "gainsT": f(np.tile(np.stack([np.asarray(inp["q_norm_a"]), np.asarray(inp["k_norm_a"]),
                                      np.asarray(inp["q_norm_b"]), np.asarray(inp["k_norm_b"])], 1), (1, 1, 2))
                    .transpose(2, 0, 1)),
        "lamv": f(np.stack([np.asarray(inp["lambda_q1"]), np.asarray(inp["lambda_k1"]),
                            np.asarray(inp["lambda_q2"]), np.asarray(inp["lambda_k2"])], 1)),
        "subln": f(inp["subln"]),
        "identf": np.eye(128, dtype=np.float32),
    }
    maps = []
    for c in cores:
        b, hh = c // 2, c % 2
        perm = np.concatenate([np.arange(hh * 512, hh * 512 + 512), np.arange((1 - hh) * 512, (1 - hh) * 512 + 512)])
        C, S = _rope_tables(perm)
        cond = np.stack([np.asarray(inp["c_ctx"]), np.asarray(inp["c"])[b]], 0)
        m = dict(shared)
        m.update({
            "xp": f(x_prompt[4 * c:4 * c + 4].reshape(T, D)),
            "xs": f(x_sample[b][perm]),
            "cdk": f(np.asarray(inp["cache_diff_k"])[b].reshape(DEPTH, PAST, 512)),
            "cdv": f(np.asarray(inp["cache_diff_v"])[b].reshape(DEPTH, PAST, 512)),
            "cgk": f(np.asarray(inp["cache_gqa_k"])[b].reshape(DEPTH, PAST, 128)),
            "cgv": f(np.asarray(inp["cache_gqa_v"])[b].reshape(DEPTH, PAST, 128)),
            "condT": f(cond.reshape(2, KC, 128).transpose(2, 1, 0)),
            "ropeC": C, "ropeS": S,
        })
        maps.append(m)
    return maps


_NC_CACHE = {}


def kernel(**inputs):
    if "nc" not in _NC_CACHE:
        _NC_CACHE["nc"] = build_program()
    nc = _NC_CACHE["nc"]
    cores = list(range(NCORES))
    in_maps = make_in_maps(inputs, cores)
    res = run_bass_kernel_spmd(nc, in_maps, core_ids=cores)
    R = res.results
    yp = np.concatenate([np.asarray(R[c]["yp"]).reshape(4, 256, D) for c in cores], 0)
    ys = np.zeros((4, 1024, D), np.float32)
    for c in cores:
        b, hh = c // 2, c % 2
        ys[b, hh * 512:(hh + 1) * 512] = np.asarray(R[c]["ys"])
    ndk = np.concatenate([np.asarray(R[c]["ndk"]) for c in cores], 0).reshape(32, DEPTH, 256, 4, 2, 64)
    ndv = np.concatenate([np.asarray(R[c]["ndv"]) for c in cores], 0).reshape(32, DEPTH, 256, 4, 128)
    ngk = np.concatenate([np.asarray(R[c]["ngk"]) for c in cores], 0).reshape(32, DEPTH, 256, 2, 64)
    ngv = np.concatenate([np.asarray(R[c]["ngv"]) for c in cores], 0).reshape(32, DEPTH, 256, 2, 64)
    return (yp.astype(np.float32), ys, ndk.astype(np.float32), ndv.astype(np.float32),
            ngk.astype(np.float32), ngv.astype(np.float32))
```
